# Optimizing a Trainium2 kernel written in Bass

```python
import jax, jax.numpy as jnp
from jax import lax
import numpy as np

D_MODEL = 2048
BATCH = 8
SEQ = 2048
DEPTH = 1
DEC_BATCH = 128
DEC_SEQ = 4
PAST_LEN = 2048
PAGE_SIZE = 128

N_META = 16
D_ATTN = D_MODEL // 2
HEAD_DIM = 128
N_HEADS = D_ATTN // HEAD_DIM
D_CONV = D_MODEL // 2
CONV_WIDTH = 3
Q_BLOCK = 128
RMS_EPS = 1e-6
SB_BIAS_INIT = -7.0
SPLIT_SIZES = (D_ATTN, D_ATTN, D_ATTN, D_ATTN, D_CONV, D_CONV, D_CONV, D_CONV, D_MODEL, D_MODEL)
IN_COLS = sum(SPLIT_SIZES)

kernel_name = "stickbreak_shortconv_parallel_hybrid_step"


def rmsnorm(x, g):
    xf = x.astype(jnp.float32)
    y = xf * lax.rsqrt(jnp.mean(xf * xf, axis=-1, keepdims=True) + RMS_EPS)
    return (y * g.astype(jnp.float32)).astype(x.dtype)


def split_offsets():
    return [int(v) for v in np.cumsum(SPLIT_SIZES)[:-1]]


def stick_breaking(q, k, v, q_pos, k_pos, b_sb):
    z = (jnp.einsum('bqhd,bkhd->bhqk', q, k).astype(jnp.float32) * (HEAD_DIM ** -0.5)
         + b_sb.astype(jnp.float32)[None, :, None, None])
    mask = k_pos[None, :] < q_pos[:, None]
    log_keep = jnp.where(mask, jax.nn.log_sigmoid(-z), 0.0)
    after = lax.cumsum(log_keep, axis=3, reverse=True) - log_keep
    attn = jnp.where(mask, jnp.exp(jax.nn.log_sigmoid(z) + after), 0.0)
    return jnp.einsum('bhqk,bkhd->bqhd', attn.astype(v.dtype), v)


def attend_prompt(q, k, v, b_sb):
    b, t = q.shape[0], q.shape[1]
    nqb = -(-t // Q_BLOCK)
    pad = nqb * Q_BLOCK - t
    qp = jnp.pad(q, ((0, 0), (0, pad), (0, 0), (0, 0)))
    qb = qp.reshape(b, nqb, Q_BLOCK, N_HEADS, HEAD_DIM).transpose(1, 0, 2, 3, 4)
    q_pos = jnp.arange(nqb * Q_BLOCK).reshape(nqb, Q_BLOCK)
    k_pos = jnp.arange(t)
    out = lax.map(lambda a: stick_breaking(a[0], k, v, a[1], k_pos, b_sb), (qb, q_pos))
    return out.transpose(1, 0, 2, 3, 4).reshape(b, nqb * Q_BLOCK, N_HEADS, HEAD_DIM)[:, :t]


def causal_conv(u_ext, w, b):
    length = u_ext.shape[1] - (CONV_WIDTH - 1)
    y = b
    for i in range(CONV_WIDTH):
        y = y + w[i] * u_ext[:, i:i + length]
    return y


def mixer_layer(x, conv_hist, attend, g_pre, w_in, b_sb, b_gate, w_conv, b_conv, w_pa, w_pc, w_out, g_post):
    bsz, t = x.shape[0], x.shape[1]
    h = rmsnorm(x, g_pre)
    q, k, v, ga, cb, cc, ch, gc, ma, mc = jnp.split(h @ w_in, split_offsets(), axis=-1)
    q = q.reshape(bsz, t, N_HEADS, HEAD_DIM)
    k = k.reshape(bsz, t, N_HEADS, HEAD_DIM)
    v = v.reshape(bsz, t, N_HEADS, HEAD_DIM)
    a = attend(q, k, v, b_sb).reshape(bsz, t, D_ATTN) * jax.nn.silu(ga)
    u = cc * ch
    u_ext = jnp.concatenate([conv_hist.astype(u.dtype), u], axis=1)
    c = cb * causal_conv(u_ext, w_conv, b_conv) * jax.nn.silu(gc)
    merged = (jax.nn.sigmoid(ma + b_gate[:D_MODEL]) * (a @ w_pa)
              + jax.nn.sigmoid(mc + b_gate[D_MODEL:]) * (c @ w_pc))
    y = x + rmsnorm(merged @ w_out, g_post)
    return y, k, v, u_ext[:, -(CONV_WIDTH - 1):]


def setup_inputs(seed: int = 0) -> dict:
    key = jax.random.key(seed)
    ks = jax.random.split(key, 20)
    n_pages = PAST_LEN // PAGE_SIZE
    n_used = DEC_BATCH * n_pages
    n_pool = n_used + (n_used + 3) // 4
    perm = jax.random.permutation(ks[0], n_pool)
    page_table = perm[:n_used].reshape(DEC_BATCH, n_pages).astype(jnp.int32)
    nrm = lambda k, s, sc: jax.random.normal(k, s, jnp.float32) * sc
    return {
        "x_prompt": nrm(ks[1], (BATCH, SEQ, D_MODEL), 1.0),
        "x_sample": nrm(ks[2], (DEC_BATCH, DEC_SEQ, D_MODEL), 1.0),
        "cache_k": nrm(ks[3], (DEPTH, n_pool, PAGE_SIZE, N_HEADS, HEAD_DIM), 1.0),
        "cache_v": nrm(ks[4], (DEPTH, n_pool, PAGE_SIZE, N_HEADS, HEAD_DIM), 1.0),
        "state_conv": nrm(ks[5], (DEPTH, DEC_BATCH, CONV_WIDTH - 1, D_CONV), 1.0),
        "page_table": page_table,
        "meta_tokens": nrm(ks[6], (N_META, D_MODEL), 1.0),
        "g_pre": 1.0 + nrm(ks[7], (DEPTH, D_MODEL), 0.02),
        "w_in": nrm(ks[8], (DEPTH, D_MODEL, IN_COLS), D_MODEL ** -0.5),
        "b_sb": SB_BIAS_INIT + nrm(ks[16], (DEPTH, N_HEADS), 0.1),
        "b_gate": nrm(ks[9], (DEPTH, 2 * D_MODEL), 0.02),
        "w_conv": nrm(ks[10], (DEPTH, CONV_WIDTH, D_CONV), CONV_WIDTH ** -0.5),
        "b_conv": nrm(ks[11], (DEPTH, D_CONV), 0.02),
        "w_pa": nrm(ks[12], (DEPTH, D_ATTN, D_MODEL), D_ATTN ** -0.5),
        "w_pc": nrm(ks[13], (DEPTH, D_CONV, D_MODEL), D_CONV ** -0.5),
        "w_out": nrm(ks[14], (DEPTH, D_MODEL, D_MODEL), D_MODEL ** -0.5),
        "g_post": 1.0 + nrm(ks[15], (DEPTH, D_MODEL), 0.02),
    }


def reference(x_prompt, x_sample, cache_k, cache_v, state_conv, page_table, meta_tokens,
              g_pre, w_in, b_sb, b_gate, w_conv, b_conv, w_pa, w_pc, w_out, g_post):
    bsz = x_prompt.shape[0]
    dec_b = x_sample.shape[0]
    meta = jnp.broadcast_to(meta_tokens[None].astype(x_prompt.dtype), (bsz, N_META, D_MODEL))
    xp = jnp.concatenate([meta, x_prompt], axis=1)
    xs = x_sample
    kp_l, vp_l, cp_l, ks_l, vs_l, cs_l = [], [], [], [], [], []
    for l in range(DEPTH):
        lw = (g_pre[l], w_in[l], b_sb[l], b_gate[l], w_conv[l], b_conv[l], w_pa[l], w_pc[l], w_out[l], g_post[l])
        zero_hist = jnp.zeros((bsz, CONV_WIDTH - 1, D_CONV), xp.dtype)
        xp, kp, vp, cp = mixer_layer(xp, zero_hist, attend_prompt, *lw)
        k_past = cache_k[l][page_table].reshape(dec_b, -1, N_HEADS, HEAD_DIM)
        v_past = cache_v[l][page_table].reshape(dec_b, -1, N_HEADS, HEAD_DIM)

        def attend_sample(q, k, v, bias, k_past=k_past, v_past=v_past):
            kf = jnp.concatenate([k_past.astype(k.dtype), k], axis=1)
            vf = jnp.concatenate([v_past.astype(v.dtype), v], axis=1)
            q_pos = k_past.shape[1] + jnp.arange(q.shape[1])
            k_pos = jnp.arange(kf.shape[1])
            return stick_breaking(q, kf, vf, q_pos, k_pos, bias)

        xs, ks_, vs_, cs_ = mixer_layer(xs, state_conv[l], attend_sample, *lw)
        kp_l.append(kp); vp_l.append(vp); cp_l.append(cp)
        ks_l.append(ks_); vs_l.append(vs_); cs_l.append(cs_)
    y_prompt = xp[:, N_META:]
    y_sample = xs
    return (y_prompt, y_sample, jnp.stack(kp_l), jnp.stack(vp_l), jnp.stack(cp_l),
            jnp.stack(ks_l), jnp.stack(vs_l), jnp.stack(cs_l))
```

```python
import numpy as np
import concourse.bass as bass
import concourse.mybir as mybir
from concourse.bass_utils import run_bass_kernel_spmd

F32, BF16, I32 = mybir.dt.float32, mybir.dt.bfloat16, mybir.dt.int32
AF = mybir.ActivationFunctionType
import os as _os
SILU = AF.Copy if _os.environ.get('KSILU', '') == '0' else AF.Silu
ALU = mybir.AluOpType
ET = mybir.EngineType

P = 128
NMETA = 16
EPS = 1e-6


class Cfg:
    def __init__(self, D=2048, SEQ=2048, DEC_BATCH=128, DEC_SEQ=4, NCORES=8, PAST=2048, NPOOL=2560, mode=1):
        self.mode = mode
        self.NB = DEC_BATCH
        self.NPG = PAST // P
        self.NPOOL = NPOOL
        self.GS = 1
        self.NSA = DEC_BATCH * DEC_SEQ
        self.D = D
        self.KC = D // P
        self.DA = D // 2
        self.H = self.DA // P
        self.DC = D // 2
        self.CC = self.DC // P
        self.SEQ = SEQ
        self.TP = (NMETA + SEQ) if mode == 1 else 0
        self.NCORES = NCORES
        self.NSEQ = DEC_BATCH // NCORES
        self.DEC_SEQ = DEC_SEQ
        self.NSO = self.NSEQ * DEC_SEQ
        self.TT = self.TP + self.NSO
        self.NT = (self.TT + P - 1) // P
        self.TTP = self.NT * P
        self.INC = 4 * self.DA + 4 * self.DC + 2 * D
        self.NCS = 2 + 2 * self.NSEQ
        o = 0
        self.o_gpre = o; o += self.KC
        self.o_bg = o; o += 2 * self.KC
        self.o_wc = o; o += 3 * self.CC
        self.o_bc = o; o += self.CC
        self.o_st = o; o += self.CC * self.NSEQ * 2
        self.NPC = o


class Tile:
    def __init__(self, ap, name):
        self.ap = ap
        self.name = name
        self.w = None
        self.r = {}

    def __getitem__(self, k):
        return self.ap[k]


class Prog:
    ENG = ["sp", "act", "dve", "pool", "pe"]
    NDMASEM = 16

    def __init__(self):
        self.ops = {e: [] for e in self.ENG}
        self.needed = set()

    cut = False

    def barrier(self):
        tk = set()
        for e in self.ENG:
            ops = self.ops[e]
            last_c = [i for i in range(len(ops)) if not ops[i][2]][-1:]
            last_d = [i for i in range(len(ops)) if ops[i][2]][-self.NDMASEM:]
            for i in last_c + last_d:
                tk.add((e, i))
        self.pending = {e: set(tk) for e in self.ENG}

    pending = {}

    def op(self, eng, fn, reads=(), writes=(), dma=False):
        if self.cut:
            return None
        deps = set(self.pending.pop(eng, ()))
        for t in reads:
            if t.w is not None:
                deps.add(t.w)
        for t in writes:
            if t.w is not None:
                deps.add(t.w)
            for tk in t.r.values():
                deps.add(tk)
        tk = (eng, len(self.ops[eng]))
        if eng == "pe":
            deps = {d for d in deps if d[0] != "pe"}
        deps.discard(tk)
        self.ops[eng].append((fn, deps, dma))
        self.needed |= deps
        for t in reads:
            t.r[eng] = tk
        for t in writes:
            t.w = tk
            t.r = {}
        return tk

    def emit(self, nc, block, sems, dsems, final_waits):
        val = {}
        cnt = {e: 0 for e in self.ENG}
        dcnt = {}
        dma_prev = {}
        for e in self.ENG:
            ndma = 0
            for i, (fn, deps, dma) in enumerate(self.ops[e]):
                if dma:
                    s = (e, ndma % self.NDMASEM)
                    dcnt[s] = dcnt.get(s, 0) + 16
                    val[(e, i)] = ("d", s, dcnt[s])
                    dma_prev[(e, i)] = ("d", s, dcnt[s] - 16) if dcnt[s] > 16 else None
                    ndma += 1
                elif (e, i) in self.needed:
                    cnt[e] += 1
                    val[(e, i)] = ("e", e, cnt[e])
        all_dma = [(e, i) for e in self.ENG for i, o in enumerate(self.ops[e]) if o[2]]

        def run(e, engobj):
            waited = {}

            def wait(v):
                if v is None:
                    return
                kind, s, n = v
                key = (kind, s)
                if waited.get(key, 0) >= n:
                    return
                waited[key] = n
                engobj.wait_ge(dsems[s] if kind == "d" else sems[s], n)

            for i, (fn, deps, dma) in enumerate(self.ops[e]):
                for d in sorted(deps):
                    wait(val[d])
                if dma:
                    wait(dma_prev[(e, i)])
                ins = fn(engobj)
                v = val.get((e, i))
                if v is not None:
                    kind, s, n = v
                    ins.then_inc(dsems[s] if kind == "d" else sems[s], 16 if kind == "d" else 1)
            if e == "sp":
                for d in all_dma:
                    wait(val[d])
                for d in final_waits:
                    wait(val[d])

        block.sync(lambda eng: run("sp", eng))
        block.scalar(lambda eng: run("act", eng))
        block.vector(lambda eng: run("dve", eng))
        block.gpsimd(lambda eng: run("pool", eng))
        block.tensor(lambda eng: run("pe", eng))


def build(cfg):
    c = cfg
    D, KC, DA, H, DC, CC, TP, TT, NT, TTP = c.D, c.KC, c.DA, c.H, c.DC, c.CC, c.TP, c.TT, c.NT, c.TTP
    NSEQ, NCS = c.NSEQ, c.NCS
    nc = bass.Bass("TRN2", target_bir_lowering=False)
    dt = lambda n, s, k="ExternalInput": nc.dram_tensor(n, s, F32, kind=k).ap()
    x_own = dt("x_own", [TTP, D])
    w_in = dt("w_in", [c.INC, D])
    w_pa = dt("w_pa", [D, DA])
    w_pc = dt("w_pc", [D, DC])
    w_out = dt("w_out", [D, D])
    pcol_d = dt("pcol", [P, c.NPC])
    prow_d = dt("prow", [P, D + H + 1])
    NCST = 4 * P + c.GS * 4 + 1
    consts_d = dt("consts", [P, NCST])
    if c.mode == 1:
        ckv = dt("ckv", [c.NPOOL * H * P, 2 * P])
        pt_d = nc.dram_tensor("pt", [P, c.NSEQ * c.NPG], I32, kind="ExternalInput").ap()
    else:
        a_in = dt("a_in", [P, H * P])
    y_own = dt("y_own", [TTP, D], "ExternalOutput")
    k_own = dt("k_own", [TTP, DA], "ExternalOutput")
    v_own = dt("v_own", [TTP, DA], "ExternalOutput")
    conv_own = dt("conv_own", [NCS, DC], "ExternalOutput")

    pg = Prog()
    import contextlib
    es = contextlib.ExitStack()

    def sb(name, shape, dtype=F32):
        return Tile(es.enter_context(nc.sbuf_tensor("s_" + name, shape, dtype)), name)

    def ps(name):
        return Tile(es.enter_context(nc.psum_tensor(name, [P, 512], F32)), name)

    with es:
        sems = {e: es.enter_context(nc.semaphore("sem_" + e)) for e in Prog.ENG}
        dsems = {(e_, i): es.enter_context(nc.semaphore("dsem_%s%d" % (e_, i))) for e_ in ("sp", "pool") for i in range(Prog.NDMASEM)}
        pcol = sb("pcol", [P, c.NPC])
        prow = sb("prow", [P, D + H + 1])
        cst = sb("cst", [P, NCST])
        ident = cst
        cbf = sb("cbf", [P, NCST - P], BF16)
        ones1 = sb("ones1", [P, 1])
        NWST = 3
        wst = [sb("wst%d" % i, [P, KC, P]) for i in range(NWST)]
        wbf = [sb("wbf%d" % i, [P, KC, P], BF16) for i in range(2)]
        KH = max(1, KC // 2)
        wbf_lo = [Tile(wbf[i].ap[:, 0:KH, :], "wlo") for i in range(2)]
        wbf_hi = [Tile(wbf[i].ap[:, KH:KC, :], "whi") for i in range(2)]
        PS = [ps("ps%d" % i) for i in range(8)]
        ucs = sb("ucs", [P, CC, NCS])
        uh = sb("uh", [P, CC, 2])

        dma = lambda fn, reads=(), writes=(): pg.op("sp", fn, reads, writes, dma=True)

        dma(lambda e: e.dma_start(out=pcol[:], in_=pcol_d), writes=[pcol])
        dma(lambda e: e.dma_start(out=prow[:], in_=prow_d), writes=[prow])
        dma(lambda e: e.dma_start(out=cst[:], in_=consts_d), writes=[cst])
        pg.op("pool", lambda e: e.tensor_copy(out=cbf[:], in_=cst[:, P:NCST]), [cst], [cbf])
        pg.op("pool", lambda e: e.memset(ones1[:], 1.0), [], [ones1])
        pg.op("pool", lambda e: e.memset(uh[:], 0.0), [], [uh])
        negtri = lambda k, m: cbf[:k, 0:m]
        negones = lambda k, m: cbf[:k, P:P + m]
        masku = lambda k, m: cbf[:k, 2 * P:2 * P + m]
        maskp = lambda: cbf[:, 3 * P:3 * P + c.GS * 4]
        import os

        def stage_a(x_src, ntiles, hT, tag):
          with contextlib.ExitStack() as sa:
            sbl = lambda name, shape, dtype=F32: Tile(sa.enter_context(nc.sbuf_tensor("s_" + tag + name, shape, dtype)), name)
            xt = [sbl("xt%d" % i, [P, D]) for i in range(2)]
            xn = [sbl("xn%d" % i, [P, D]) for i in range(2)]
            junk = sbl("junk", [P, D], BF16)
            ss = [sbl("ss%d" % i, [P, 1]) for i in range(2)]
            rs = [sbl("rs%d" % i, [P, 1]) for i in range(2)]
            for i in range(ntiles):
                X, XN, SS, RS = xt[i % 2], xn[i % 2], ss[i % 2], rs[i % 2]
                dma(lambda e, X=X, i=i: e.dma_start(out=X[:], in_=x_src[i * P:(i + 1) * P, :]), writes=[X])
                pg.op("act", lambda e, X=X, SS=SS: e.activation(out=junk[:], in_=X[:], func=AF.Square, accum_out=SS[:]),
                      [X], [junk, SS])
                pg.op("dve", lambda e, SS=SS, RS=RS: e.tensor_scalar(out=RS[:], in0=SS[:], scalar1=1.0 / D, scalar2=EPS,
                                                                      op0=ALU.mult, op1=ALU.add), [SS], [RS])
                pg.op("act", lambda e, RS=RS: e.activation(out=RS[:], in_=RS[:], func=AF.Ln), [RS], [RS])
                pg.op("act", lambda e, RS=RS: e.activation(out=RS[:], in_=RS[:], func=AF.Exp, scale=-0.5), [RS], [RS])
                pg.op("act", lambda e, X=X, XN=XN, RS=RS: e.activation(out=XN[:], in_=X[:], func=AF.Copy, scale=RS[:, 0:1]),
                      [X, RS], [XN])
                for g in range((KC + 3) // 4):
                    B = PS[g % 2]
                    nq = min(4, KC - 4 * g)
                    for q in range(nq):
                        kc = 4 * g + q
                        pg.op("pe", lambda e, B=B, XN=XN, q=q, kc=kc: e.transpose(
                            out=B[:, q * P:(q + 1) * P], in_=XN[:, kc * P:(kc + 1) * P], identity=ident[:, 0:P]),
                            [XN, cst], [B])
                    for q in range(nq):
                        kc = 4 * g + q
                        pg.op("dve", lambda e, B=B, q=q, kc=kc, i=i: e.tensor_scalar(
                            out=hT[:, kc, i * P:(i + 1) * P], in0=B[:, q * P:(q + 1) * P],
                            scalar1=pcol[:, c.o_gpre + kc:c.o_gpre + kc + 1], scalar2=None, op0=ALU.mult),
                            [B, pcol], [hT])
          pg.barrier()

        slab_i = [0]
        pp_i = [0]

        def proj(wd, kcx, col0, act, act_c0, ranges, evac, pbanks=(0, 1)):
            k = slab_i[0]
            slab_i[0] += 1
            ST, WB, WLO, WHI = wst[k % NWST], wbf[k % 2], wbf_lo[k % 2], wbf_hi[k % 2]
            dma(lambda e: e.dma_start(out=ST[:, 0:kcx, :], in_=wd[col0:col0 + P, :].rearrange("p (kc c) -> p kc c", c=P)),
                writes=[ST])
            kh = min(KH, kcx)
            pg.op("pool", lambda e: e.tensor_copy(out=WB[:, 0:kh, :], in_=ST[:, 0:kh, :]), [ST], [WLO])
            if kcx > kh:
                pg.op("act", lambda e: e.activation(out=WB[:, kh:kcx, :], in_=ST[:, kh:kcx, :], func=AF.Copy), [ST], [WHI])
            for (t0, n) in ranges:
                B = PS[pbanks[pp_i[0] % len(pbanks)]]
                pp_i[0] += 1
                for kc in range(kcx):
                    pg.op("pe", lambda e, B=B, kc=kc, t0=t0, n=n: e.matmul(
                        B[:, 0:n], lhsT=WB[:, kc, :], rhs=act[:, kc, act_c0 + t0:act_c0 + t0 + n],
                        start=(kc == 0), stop=(kc == kcx - 1)), [WLO if kc < KH else WHI, act], [B])
                evac(B, t0, n)

        def phase_s():
          NPG, GS = c.NPG, 1
          NSO, NSQ = c.NSO, c.NSEQ
          NPP = NPG + 1
          G4 = GS * 4
          W = NPP * G4
          NSL = GS * NPG
          NPT = NSQ * NPG
          with contextlib.ExitStack() as sa:
            sbl = lambda name, shape, dtype=F32: Tile(sa.enter_context(nc.sbuf_tensor("s_ps_" + name, shape, dtype)), name)
            pts = Tile(sa.enter_context(nc.sbuf_tensor("s_ps_pts", [P, NPT], I32)), "pts")
            ptf = sbl("ptf", [P, NPT])
            idxf = sbl("idxf", [P, NPT])
            idxh = [Tile(sa.enter_context(nc.sbuf_tensor("s_ps_idx%d" % i, [P, NPT], I32)), "idx") for i in range(2)]
            kvst = [sbl("kvst%d" % i, [P, NSL, 2 * P]) for i in range(2)]
            kvslot = [[Tile(kvst[i].ap[:, sl, :], "kvs") for sl in range(NSL)] for i in range(2)]
            kbf = sbl("kbf", [P, NSL, P], BF16)
            vbf = sbl("vbf", [P, NSL, P], BF16)
            kpad = [sbl("kpad%d" % i, [P, GS, P], BF16) for i in range(2)]
            vpad = [sbl("vpad%d" % i, [P, GS, P], BF16) for i in range(2)]
            zsb = sbl("zsb", [P, W])
            E = sbl("E", [P, W])
            lg = sbl("lg", [P, W])
            L = sbl("L", [P, W], BF16)
            AT = sbl("AT", [P, W], BF16)
            LKb = sbl("LKb", [P, W], BF16)
            LK32 = sbl("LK32", [P, W])
            dma(lambda e: e.dma_start(out=pts[:], in_=pt_d), writes=[pts])
            pg.op("dve", lambda e: e.tensor_copy(out=ptf[:], in_=pts[:]), [pts], [ptf])
            pg.op("dve", lambda e: e.tensor_scalar(out=ptf[:], in0=ptf[:], scalar1=float(H * P), scalar2=cst[:, NCST - 1:NCST],
                                                   op0=ALU.mult, op1=ALU.add), [ptf, cst], [ptf])
            for i in range(2):
                pg.op("pool", lambda e, i=i: e.memset(kpad[i][:], 0.0), [], [kpad[i]])
                pg.op("pool", lambda e, i=i: e.memset(vpad[i][:], 0.0), [], [vpad[i]])
            pg.op("pool", lambda e: e.memset(LK32[:], 0.0), [], [LK32])
            units = [(h, b) for h in range(H) for b in range(NSQ)]

            def head_idx(h):
                IX = idxh[h % 2]
                pg.op("dve", lambda e: e.tensor_scalar(out=idxf[:], in0=ptf[:], scalar1=float(h * P), scalar2=None, op0=ALU.add),
                      [ptf], [idxf])
                pg.op("dve", lambda e: e.tensor_copy(out=IX[:], in_=idxf[:]), [idxf], [IX])

            def loads(u):
                h, b = units[u]
                if b == 0:
                    head_idx(h)
                KV, KVL, IX = kvst[u % 2], kvslot[u % 2], idxh[h % 2]
                for p in range(NPG):
                    page_load(KV, KVL, IX, p, b * NPG + p)

            def page_load(KV, KVL, IX, slot, j):
                pg.op("pool", lambda e: e.indirect_dma_start(
                    out=KV[:, slot, :], out_offset=None, in_=ckv,
                    in_offset=bass.IndirectOffsetOnAxis(ap=IX[:, j:j + 1], axis=0)), [IX], [KVL[slot]], dma=True)

            def group(u):
                h, b = units[u]
                i2 = u % 2
                KP, VP = kpad[i2], vpad[i2]
                KV, KVL = kvst[i2], kvslot[i2]
                AB, CB, OB, VT = PS[2 + i2], PS[4 + i2], PS[6 + i2], PS[0]
                bias_c = lambda: prow[:, D + h:D + h + 1]
                if u == 0:
                    loads(0)
                if u + 1 < len(units):
                    loads(u + 1)
                pg.op("act", lambda e: e.activation(out=kbf[:], in_=KV[:, :, 0:P], func=AF.Copy), KVL, [kbf])
                pg.op("dve", lambda e: e.tensor_copy(out=vbf[:], in_=KV[:, :, P:2 * P]), KVL, [vbf])
                pg.op("dve", lambda e: e.tensor_copy(out=KP[:, 0, 0:4], in_=kS[:, h, 4 * b:4 * b + 4]), [kS], [KP])
                pg.op("pe", lambda e: e.transpose(out=VT[0:4, 0:P], in_=vS32[:, h, 4 * b:4 * b + 4], identity=ident[:, 0:P]), [vS32, cst], [VT])
                pg.op("dve", lambda e: e.tensor_copy(out=VP[0:4, 0, :], in_=VT[0:4, 0:P]), [VT], [VP])
                for p in range(NPP):
                    zmm(AB, KP, p, p * 4, h, b)
                pg.op("dve", lambda e: e.tensor_copy(out=zsb[:], in_=AB[:, 0:W]), [AB], [zsb])
                pg.op("act", lambda e: e.activation(out=E[:], in_=zsb[:], func=AF.Exp, bias=bias_c()), [zsb, prow], [E])
                pg.op("act", lambda e: e.activation(out=L[:], in_=E[:], func=AF.Ln, bias=ones1[:, 0:1]), [E, ones1], [L])
                pg.op("dve", lambda e: e.tensor_tensor(out=L[:, NPG * G4:W], in0=L[:, NPG * G4:W], in1=maskp(), op=ALU.mult), [L, cbf], [L])
                for p in range(NPG - 1, -1, -1):
                    pg.op("dve", lambda e, p=p: e.tensor_tensor(out=LK32[:, p * G4:(p + 1) * G4], in0=LK32[:, (p + 1) * G4:(p + 2) * G4],
                                                                in1=L[:, (p + 1) * G4:(p + 2) * G4], op=ALU.add), [LK32, L], [LK32])
                pg.op("dve", lambda e: e.tensor_copy(out=LKb[:], in_=LK32[:]), [LK32], [LKb])
                pg.op("pe", lambda e: e.matmul(CB[:, 0:W], lhsT=negtri(P, P), rhs=L[:], start=True, stop=False), [L, cbf], [CB])
                pg.op("pe", lambda e: e.matmul(CB[:, 0:W], lhsT=negones(P, P), rhs=LKb[:], start=False, stop=True), [LKb, cbf], [CB])
                pg.op("dve", lambda e: e.tensor_tensor(out=lg[:], in0=CB[:, 0:W], in1=zsb[:], op=ALU.add), [CB, zsb], [lg])
                pg.op("act", lambda e: e.activation(out=AT[:], in_=lg[:], func=AF.Exp, bias=bias_c()), [lg, prow], [AT])
                pg.op("dve", lambda e: e.tensor_tensor(out=AT[:, NPG * G4:W], in0=AT[:, NPG * G4:W], in1=maskp(), op=ALU.mult), [AT, cbf], [AT])
                for p in range(NPP):
                    avmm(OB, VP, p, p * 4)
                pg.op("dve", lambda e: e.tensor_tensor(out=a_t[:, h, TP + 4 * b:TP + 4 * b + 4], in0=OB[:, 0:4], in1=sgS[:, h, 4 * b:4 * b + 4],
                                                       op=ALU.mult), [OB, sgS], [a_t])

            def zmm(AB, KP, p, col, h, b):
                if p < NPG:
                    pg.op("pe", lambda e: e.matmul(AB[:, col:col + 4], lhsT=kbf[:, p, :], rhs=qS[:, h, 4 * b:4 * b + 4],
                                                   start=True, stop=True), [kbf, qS], [AB])
                else:
                    pg.op("pe", lambda e: e.matmul(AB[:, col:col + 4], lhsT=KP[:, 0, :], rhs=qS[:, h, 4 * b:4 * b + 4],
                                                   start=True, stop=True), [KP, qS], [AB])

            def avmm(OB, VP, p, col):
                if p < NPG:
                    pg.op("pe", lambda e: e.matmul(OB[:, 0:4], lhsT=vbf[:, p, :], rhs=AT[:, col:col + 4],
                                                   start=(p == 0), stop=False, skip_group_check=True), [vbf, AT], [OB])
                else:
                    pg.op("pe", lambda e: e.matmul(OB[:, 0:4], lhsT=VP[:, 0, :], rhs=AT[:, col:col + 4],
                                                   start=False, stop=True, skip_group_check=True), [VP, AT], [OB])

            for u in range(len(units)):
                group(u)
          pg.barrier()

        full_ranges = [(t, min(512, TTP - t)) for t in range(0, TTP, 512)]

        hT = sb("hT", [P, KC, TTP], BF16)
        a_t = sb("a_t", [P, H, TTP], BF16)
        qS = sb("qS", [P, H, c.NSO], BF16)
        kS = sb("kS", [P, H, c.NSO], BF16)
        vS32 = sb("vS32", [P, H, c.NSO])
        sgS = sb("sgS", [P, H, c.NSO])
        if c.mode == 1:
            pg.op("pool", lambda e: e.memset(a_t[:], 0.0), [], [a_t])
        else:
            for hh in range(H):
                ST = wst[hh % 2]
                dma(lambda e, ST=ST, hh=hh: e.dma_start(out=ST[:, 0, :], in_=a_in[:, hh * P:(hh + 1) * P]), writes=[ST])
                pg.op("pool", lambda e, ST=ST, hh=hh: e.tensor_copy(out=a_t[:, hh, 0:P], in_=ST[:, 0, :]), [ST], [a_t])
        stage_a(x_own, NT, hT, "a")
        if os.environ.get("KSTOP", "") == "A":
            pg.cut = True

        with contextlib.ExitStack() as sa:
            sbl = lambda name, shape, dtype=F32: Tile(sa.enter_context(nc.sbuf_tensor("s_" + name, shape, dtype)), name)
            qT = sbl("qT", [P, TTP], BF16)
            kT = sbl("kT", [P, TTP], BF16)
            vtok = sbl("vtok", [P, NT, P], BF16)
            sg = sbl("sg", [P, TTP], BF16)
            kv32 = [sbl("kv32_%d" % i, [P, 512]) for i in range(2)]
            tokst = [sbl("tokst%d" % i, [P, 4, P]) for i in range(2)]
            e32 = [sbl("e32_%d" % i, [P, 512]) for i in range(2)]
            lkp = [sbl("lkp%d" % i, [P, 512], BF16) for i in range(2)]
            lks = [sbl("lks%d" % i, [P, 512], BF16) for i in range(2)]
            att = [sbl("att%d" % i, [P, 512], BF16) for i in range(2)]
            kvi = [0]
            scale = float(P) ** -0.5

            def kv_out(B, t0, n, out_d, h, isv):
                i = kvi[0] % 2
                kvi[0] += 1
                K32, TS, PT = kv32[i], tokst[i], PS[2 + i]
                pg.op("act", lambda e: e.activation(out=K32[:, 0:n], in_=B[:, 0:n], func=AF.Copy), [B], [K32])
                nb = n // P
                if os.environ.get("KKV", "") == "min":
                    return K32
                for j in range(nb):
                    pg.op("pe", lambda e, j=j: e.transpose(out=PT[:, j * P:(j + 1) * P], in_=K32[:, j * P:(j + 1) * P],
                                                           identity=ident[:, 0:P]), [K32, cst], [PT])
                pg.op("dve", lambda e: e.tensor_copy(out=TS[:, 0:nb, :], in_=PT[:, 0:nb * P].rearrange("p (j d) -> p j d", d=P)),
                      [PT], [TS])
                if isv:
                    pg.op("pool", lambda e: e.tensor_copy(out=vtok[:, t0 // P:t0 // P + nb, :], in_=TS[:, 0:nb, :]), [TS], [vtok])
                if os.environ.get("KKV", "") != "noout":
                    dma(lambda e: e.dma_start(out=out_d[t0:t0 + n, h * P:(h + 1) * P].rearrange("(j p) d -> p j d", p=P),
                                              in_=TS[:, 0:nb, :]), reads=[TS])
                return K32

            def head(h):
                def ev_q(B, t0, n):
                    pg.op("act", lambda e: e.activation(out=qT[:, t0:t0 + n], in_=B[:, 0:n], func=AF.Copy, scale=scale), [B], [qT])
                proj(w_in, KC, h * P, hT, 0, full_ranges, ev_q)
                if os.environ.get("KSTOP", "") == "P0":
                    pg.cut = True

                def ev_k(B, t0, n, h=h):
                    K32 = kv_out(B, t0, n, k_own, h, False)
                    pg.op("dve", lambda e: e.tensor_copy(out=kT[:, t0:t0 + n], in_=K32[:, 0:n]), [K32], [kT])
                proj(w_in, KC, DA + h * P, hT, 0, full_ranges, ev_k)

                def ev_v(B, t0, n, h=h):
                    K32 = kv_out(B, t0, n, v_own, h, True)
                    if t0 <= TP and TT <= t0 + n:
                        pg.op("dve", lambda e: e.tensor_copy(out=vS32[:, h, :], in_=K32[:, TP - t0:TT - t0]), [K32], [vS32])
                proj(w_in, KC, 2 * DA + h * P, hT, 0, full_ranges, ev_v)

                def ev_g(B, t0, n):
                    pg.op("act", lambda e: e.activation(out=sg[:, t0:t0 + n], in_=B[:, 0:n], func=SILU), [B], [sg])
                proj(w_in, KC, 3 * DA + h * P, hT, 0, full_ranges, ev_g)

                pg.op("dve", lambda e: e.tensor_copy(out=qS[:, h, :], in_=qT[:, TP:TT]), [qT], [qS])
                pg.op("dve", lambda e: e.tensor_copy(out=kS[:, h, :], in_=kT[:, TP:TT]), [kT], [kS])
                pg.op("dve", lambda e: e.tensor_copy(out=sgS[:, h, :], in_=sg[:, TP:TT]), [sg], [sgS])
                bias_h = lambda k, h=h: prow[:k, D + h:D + h + 1]
                if os.environ.get("KSTOP", "") == "P1":
                    pg.cut = True
                for qi, q0 in enumerate(range(0, TP, 512)):
                    qtile(h, bias_h, qi, q0)

            def qtile(h, bias_h, qi, q0):
                if True:
                    qn = min(512, TP - q0)
                    OB = PS[6 + qi % 2]
                    pg.op("pool", lambda e: e.memset(lks[0][:], 0.0), [], [lks[0]])
                    pg.op("pool", lambda e: e.memset(lks[1][:], 0.0), [], [lks[1]])
                    S_hi = (q0 + qn - 1) // P
                    for st, S in enumerate(range(S_hi, -1, -1)):
                        step(h, bias_h, q0, qn, OB, st, S)
                    pg.op("dve", lambda e: e.tensor_tensor(
                        out=a_t[:, h, q0:q0 + qn], in0=OB[:, 0:qn], in1=sg[:, q0:q0 + qn], op=ALU.mult), [OB, sg], [a_t])

            def step(h, bias_h, q0, qn, OB, st, S):
                if True:
                    if True:
                        li = st
                        nk = min(P, TP - S * P)
                        c0 = max(q0, S * P) - q0
                        diag = S * P >= q0
                        nd = min(P, qn - c0)
                        AB = PS[4 + st % 2]
                        E, L, AT = e32[st % 2], lkp[st % 2], att[st % 2]
                        LO, LN = lks[li % 2], lks[(li + 1) % 2]
                        first, last = (st == 0), (S == 0)
                        pg.op("pe", lambda e, AB=AB, S=S, nk=nk, c0=c0, qn=qn, q0=q0: e.matmul(
                            AB[:nk, c0:qn], lhsT=kT[:, S * P:S * P + nk], rhs=qT[:, q0 + c0:q0 + qn], start=True, stop=False),
                            [kT, qT], [AB])
                        pg.op("act", lambda e, AB=AB, E=E, nk=nk, c0=c0, qn=qn: e.activation(
                            out=E[:nk, c0:qn], in_=AB[:nk, c0:qn], func=AF.Exp, bias=bias_h(nk)), [AB, prow], [E])
                        pg.op("act", lambda e, L=L, E=E, nk=nk, c0=c0, qn=qn: e.activation(
                            out=L[:nk, c0:qn], in_=E[:nk, c0:qn], func=AF.Ln, bias=ones1[:nk, 0:1]), [E, ones1], [L])
                        if diag:
                            pg.op("pool", lambda e, L=L, nk=nk, c0=c0, nd=nd: e.tensor_tensor(
                                out=L[:nk, c0:c0 + nd], in0=L[:nk, c0:c0 + nd], in1=masku(nk, nd), op=ALU.mult), [L, cbf], [L])
                        pg.op("pe", lambda e, AB=AB, L=L, nk=nk, c0=c0, qn=qn, first=first: e.matmul(
                            AB[:nk, c0:qn], lhsT=negtri(nk, nk), rhs=L[:nk, c0:qn], start=False, stop=first,
                            skip_group_check=True), [L, cbf], [AB])
                        if not first:
                            pg.op("pe", lambda e, AB=AB, LO=LO, nk=nk, c0=c0, qn=qn: e.matmul(
                                AB[:nk, c0:qn], lhsT=negones(P, nk), rhs=LO[:, c0:qn], start=False, stop=True,
                                skip_group_check=True), [LO, cbf], [AB])
                        if not last:
                            pg.op("dve", lambda e, LO=LO, LN=LN, L=L, nk=nk, c0=c0, qn=qn: e.tensor_tensor(
                                out=LN[:nk, c0:qn], in0=LO[:nk, c0:qn], in1=L[:nk, c0:qn], op=ALU.add), [LO, L], [LN])
                        pg.op("act", lambda e, AB=AB, AT=AT, nk=nk, c0=c0, qn=qn: e.activation(
                            out=AT[:nk, c0:qn], in_=AB[:nk, c0:qn], func=AF.Exp, bias=bias_h(nk)), [AB, prow], [AT])
                        if diag:
                            pg.op("pool", lambda e, AT=AT, nk=nk, c0=c0, nd=nd: e.tensor_tensor(
                                out=AT[:nk, c0:c0 + nd], in0=AT[:nk, c0:c0 + nd], in1=masku(nk, nd), op=ALU.mult), [AT, cbf], [AT])
                        pg.op("pe", lambda e, OB=OB, AT=AT, S=S, nk=nk, c0=c0, qn=qn, first=first, last=last: e.matmul(
                            OB[:, c0:qn], lhsT=vtok[:nk, S, :], rhs=AT[:nk, c0:qn], start=first, stop=last,
                            skip_group_check=True), [vtok, AT], [OB])
            if c.mode == 1:
                for h in range(H):
                    head(h)
        pg.barrier()
        if c.mode == 1 and os.environ.get("KSTOP", "") != "P":
            phase_s()
        if os.environ.get("KSTOP", "") == "S":
            pg.cut = True

        if os.environ.get("KSTOP", "") == "P":
            pg.cut = True
        with contextlib.ExitStack() as sa:
            sbl = lambda name, shape, dtype=F32: Tile(sa.enter_context(nc.sbuf_tensor("s_" + name, shape, dtype)), name)
            cpart = sbl("cpart", [P, CC, 256], BF16)
            merged = sbl("merged", [P, KC, 256], BF16)
            oT = sbl("oT", [P, KC, 256])
            cb32 = sbl("cb32", [P, 256])
            cc32 = sbl("cc32", [P, 256])
            uext = sbl("uext", [P, 258])
            sgc = sbl("sgc", [P, 256])
            yc = sbl("yc", [P, 256])
            ues = sbl("ues", [P, NSEQ, 6])
            ga32 = cb32
            gc32 = cc32
            t1 = sbl("t1", [P, 256])
            t2 = sbl("t2", [P, 512])
            ssq = sbl("ssq", [P, 8])
            rstd = sbl("rstd", [P, 1])
            xq = [sbl("xq%d" % i, [P, 512]) for i in range(2)]
            yq = [sbl("yq%d" % i, [P, 512]) for i in range(2)]
            cstage = oT
            wc = lambda j, i: pcol[:, c.o_wc + 3 * j + i:c.o_wc + 3 * j + i + 1]
            bc = lambda j: pcol[:, c.o_bc + j:c.o_bc + j + 1]
            xi = [0]
            PART = 256
            part_ranges = [(t, min(PART, TTP - t)) for t in range(0, TTP, PART)]

            def part(t0, n):
                rng = [(0, n)]
                np_ = max(0, min(t0 + n, TP) - t0)
                s0, s1 = max(t0, TP) - t0, min(t0 + n, TT) - t0
                has_s = s1 > s0
                if has_s:
                    assert s1 - s0 == c.NSO, "sample tokens must sit inside one part"
                def convchunk(j):
                    base = 4 * DA
                    proj(w_in, KC, base + j * P, hT, t0, rng,
                         lambda B, _t, _n: pg.op("act", lambda e: e.activation(out=cb32[:, 0:n], in_=B[:, 0:n], func=AF.Copy), [B], [cb32]))
                    proj(w_in, KC, base + DC + j * P, hT, t0, rng,
                         lambda B, _t, _n: pg.op("act", lambda e: e.activation(out=cc32[:, 0:n], in_=B[:, 0:n], func=AF.Copy), [B], [cc32]))
                    proj(w_in, KC, base + 2 * DC + j * P, hT, t0, rng,
                         lambda B, _t, _n: pg.op("dve", lambda e: e.tensor_tensor(out=uext[:, 2:2 + n], in0=B[:, 0:n], in1=cc32[:, 0:n], op=ALU.mult),
                                                 [B, cc32], [uext]))
                    proj(w_in, KC, base + 3 * DC + j * P, hT, t0, rng,
                         lambda B, _t, _n: pg.op("act", lambda e: e.activation(out=sgc[:, 0:n], in_=B[:, 0:n], func=SILU), [B], [sgc]))
                    pg.op("dve", lambda e, j=j: e.tensor_copy(out=uext[:, 0:2], in_=uh[:, j, :]), [uh], [uext])
                    pg.op("dve", lambda e, j=j: e.tensor_scalar(out=yc[:, 0:n], in0=uext[:, 2:2 + n], scalar1=wc(j, 2), scalar2=bc(j),
                                                                 op0=ALU.mult, op1=ALU.add), [uext, pcol], [yc])
                    pg.op("dve", lambda e, j=j: e.scalar_tensor_tensor(out=yc[:, 0:n], in0=uext[:, 1:1 + n], scalar=wc(j, 1), in1=yc[:, 0:n],
                                                                        op0=ALU.mult, op1=ALU.add), [uext, pcol, yc], [yc])
                    pg.op("dve", lambda e, j=j: e.scalar_tensor_tensor(out=yc[:, 0:n], in0=uext[:, 0:n], scalar=wc(j, 0), in1=yc[:, 0:n],
                                                                        op0=ALU.mult, op1=ALU.add), [uext, pcol, yc], [yc])
                    if has_s:
                        so = c.o_st + j * NSEQ * 2
                        pg.op("dve", lambda e, so=so: e.tensor_copy(out=ues[:, :, 0:2], in_=pcol[:, so:so + 2 * NSEQ].rearrange("p (b r) -> p b r", r=2)),
                              [pcol], [ues])
                        pg.op("dve", lambda e: e.tensor_copy(out=ues[:, :, 2:6], in_=uext[:, 2 + s0:2 + s1].rearrange("p (b t) -> p b t", t=4)),
                              [uext], [ues])
                        ysv = lambda: yc[:, s0:s1].rearrange("p (b t) -> p b t", t=4)
                        pg.op("dve", lambda e, j=j: e.tensor_scalar(out=ysv(), in0=ues[:, :, 2:6], scalar1=wc(j, 2), scalar2=bc(j),
                                                                     op0=ALU.mult, op1=ALU.add), [ues, pcol], [yc])
                        pg.op("dve", lambda e, j=j: e.scalar_tensor_tensor(out=ysv(), in0=ues[:, :, 1:5], scalar=wc(j, 1), in1=ysv(),
                                                                            op0=ALU.mult, op1=ALU.add), [ues, pcol, yc], [yc])
                        pg.op("dve", lambda e, j=j: e.scalar_tensor_tensor(out=ysv(), in0=ues[:, :, 0:4], scalar=wc(j, 0), in1=ysv(),
                                                                            op0=ALU.mult, op1=ALU.add), [ues, pcol, yc], [yc])
                        pg.op("dve", lambda e, j=j: e.tensor_copy(out=ucs[:, j, 2:NCS].rearrange("p (b r) -> p b r", r=2), in_=ues[:, :, 4:6]),
                              [ues], [ucs])
                    if t0 <= TP - 2 and TP <= t0 + n:
                        pg.op("dve", lambda e, j=j: e.tensor_copy(out=ucs[:, j, 0:2], in_=uext[:, 2 + TP - 2 - t0:2 + TP - t0]), [uext], [ucs])
                    if np_ >= 2:
                        pg.op("dve", lambda e, j=j: e.tensor_copy(out=uh[:, j, :], in_=uext[:, np_:np_ + 2]), [uext], [uh])
                    pg.op("dve", lambda e: e.tensor_tensor(out=t1[:, 0:n], in0=cb32[:, 0:n], in1=yc[:, 0:n], op=ALU.mult), [cb32, yc], [t1])
                    pg.op("dve", lambda e, j=j: e.tensor_tensor(out=cpart[:, j, 0:n], in0=t1[:, 0:n], in1=sgc[:, 0:n], op=ALU.mult),
                          [t1, sgc], [cpart])
                for j in range(CC):
                    convchunk(j)

                def mergechunk(f):
                    base = 4 * DA + 4 * DC
                    bga = pcol[:, c.o_bg + f:c.o_bg + f + 1]
                    bgc = pcol[:, c.o_bg + KC + f:c.o_bg + KC + f + 1]
                    proj(w_in, KC, base + f * P, hT, t0, rng,
                         lambda B, _t, _n, bga=bga: pg.op("act", lambda e: e.activation(out=ga32[:, 0:n], in_=B[:, 0:n], func=AF.Sigmoid, bias=bga),
                                                          [B, pcol], [ga32]))
                    proj(w_in, KC, base + D + f * P, hT, t0, rng,
                         lambda B, _t, _n, bgc=bgc: pg.op("act", lambda e: e.activation(out=gc32[:, 0:n], in_=B[:, 0:n], func=AF.Sigmoid, bias=bgc),
                                                          [B, pcol], [gc32]))
                    proj(w_pa, H, f * P, a_t, t0, rng,
                         lambda B, _t, _n: pg.op("dve", lambda e: e.tensor_tensor(out=t1[:, 0:n], in0=B[:, 0:n], in1=ga32[:, 0:n], op=ALU.mult),
                                                 [B, ga32], [t1]))

                    def ev_pc(B, _t, _n, f=f):
                        pg.op("dve", lambda e: e.tensor_tensor(out=t2[:, 0:n], in0=B[:, 0:n], in1=gc32[:, 0:n], op=ALU.mult), [B, gc32], [t2])
                        pg.op("dve", lambda e: e.tensor_tensor(out=merged[:, f, 0:n], in0=t1[:, 0:n], in1=t2[:, 0:n], op=ALU.add), [t1, t2], [merged])
                    proj(w_pc, CC, f * P, cpart, 0, rng, ev_pc)
                for f in range(KC):
                    mergechunk(f)

                def outchunk(f):
                    proj(w_out, KC, f * P, merged, 0, rng,
                         lambda B, _t, _n: pg.op("act", lambda e: e.activation(out=oT[:, f, 0:n], in_=B[:, 0:n], func=AF.Copy), [B], [oT]))
                for f in range(KC):
                    outchunk(f)
                NG = (D + 511) // 512
                for tt in range(n // P):
                    finalize(t0, tt, NG)

            def finalize(t0, tt, NG):
                if True:
                    row0 = t0 + tt * P
                    for g in range(NG):
                        B = PS[2 + g]
                        nq = min(4, KC - 4 * g)
                        for q in range(nq):
                            pg.op("pe", lambda e, B=B, q=q, g=g, tt=tt: e.transpose(
                                out=B[:, q * P:(q + 1) * P], in_=oT[:, 4 * g + q, tt * P:(tt + 1) * P], identity=ident[:, 0:P]), [oT, cst], [B])
                        pg.op("act", lambda e, B=B, g=g, nq=nq: e.activation(out=t2[:, 0:nq * P], in_=B[:, 0:nq * P], func=AF.Square,
                                                                             accum_out=ssq[:, g:g + 1]), [B], [t2, ssq])
                    pg.op("dve", lambda e: e.tensor_reduce(out=rstd[:], in_=ssq[:, 0:NG], axis=mybir.AxisListType.X, op=ALU.add), [ssq], [rstd])
                    pg.op("dve", lambda e: e.tensor_scalar(out=rstd[:], in0=rstd[:], scalar1=1.0 / D, scalar2=EPS, op0=ALU.mult, op1=ALU.add),
                          [rstd], [rstd])
                    pg.op("act", lambda e: e.activation(out=rstd[:], in_=rstd[:], func=AF.Ln), [rstd], [rstd])
                    pg.op("act", lambda e: e.activation(out=rstd[:], in_=rstd[:], func=AF.Exp, scale=-0.5), [rstd], [rstd])
                    for g in range(NG):
                        B = PS[2 + g]
                        w = min(512, D - g * 512)
                        XQ, YQ = xq[xi[0] % 2], yq[xi[0] % 2]
                        xi[0] += 1
                        dma(lambda e, XQ=XQ, g=g, w=w, row0=row0: e.dma_start(out=XQ[:, 0:w], in_=x_own[row0:row0 + P, g * 512:g * 512 + w]), writes=[XQ])
                        pg.op("dve", lambda e, B=B, YQ=YQ, g=g, w=w: e.scalar_tensor_tensor(
                            out=YQ[:, 0:w], in0=B[:, 0:w], scalar=rstd[:, 0:1], in1=prow[:, g * 512:g * 512 + w], op0=ALU.mult, op1=ALU.mult),
                            [B, rstd, prow], [YQ])
                        pg.op("pool", lambda e, XQ=XQ, YQ=YQ, w=w: e.tensor_tensor(out=YQ[:, 0:w], in0=YQ[:, 0:w], in1=XQ[:, 0:w], op=ALU.add),
                              [XQ, YQ], [YQ])
                        dma(lambda e, YQ=YQ, g=g, w=w, row0=row0: e.dma_start(out=y_own[row0:row0 + P, g * 512:g * 512 + w], in_=YQ[:, 0:w]), reads=[YQ])
            for (t0_, n_) in part_ranges:
                part(t0_, n_)
            for j in range(CC):
                B = PS[2 + j % 2]
                pg.op("pe", lambda e, B=B, j=j: e.transpose(out=B[:NCS, 0:P], in_=ucs[:, j, :], identity=ident[:, 0:P]), [ucs, cst], [B])
                pg.op("dve", lambda e, B=B, j=j: e.tensor_copy(out=cstage[:NCS, j, 0:P], in_=B[:NCS, 0:P]), [B], [cstage])
            dma(lambda e: e.dma_start(out=conv_own.rearrange("r (j d) -> r j d", d=P), in_=cstage[:NCS, 0:CC, 0:P]), reads=[cstage])

            with nc.Block() as block:
                pg.emit(nc, block, sems, dsems, [])
    return nc


_CACHE = {}


def make_consts():
    cs = np.zeros((P, 4 * P), np.float32)
    cs[:, 0:P] = np.eye(P, dtype=np.float32)
    j = np.arange(P)[:, None]
    s = np.arange(P)[None, :]
    cs[:, P:2 * P] = -(j >= s).astype(np.float32)
    cs[:, 2 * P:3 * P] = -1.0
    cs[:, 3 * P:4 * P] = (s > j).astype(np.float32)
    return cs


def make_consts_full(GS):
    cs = np.zeros((P, 4 * P + GS * 4 + 1), np.float32)
    cs[:, -1] = np.arange(P, dtype=np.float32)
    cs[:, :4 * P] = make_consts()
    for bi in range(GS):
        for t in range(4):
            cs[:t, 4 * P + bi * 4 + t] = 1.0
    return cs


def run(cfg, x_prompt, x_sample, cache_k, cache_v, state_conv, page_table, meta_tokens,
        g_pre, w_in, b_sb, b_gate, w_conv, b_conv, w_pa, w_pc, w_out, g_post):
    c = cfg
    f32 = lambda a: np.ascontiguousarray(np.asarray(a, dtype=np.float32))
    x_prompt, x_sample, state_conv, meta_tokens = map(f32, (x_prompt, x_sample, state_conv, meta_tokens))
    g_pre, w_in, b_sb, b_gate, w_conv, b_conv, w_pa, w_pc, w_out, g_post = map(
        f32, (g_pre, w_in, b_sb, b_gate, w_conv, b_conv, w_pa, w_pc, w_out, g_post))
    cache_k = np.asarray(cache_k, dtype=np.float32)
    cache_v = np.asarray(cache_v, dtype=np.float32)
    page_table = np.ascontiguousarray(np.asarray(page_table, dtype=np.int32))
    key = (1, c.D, c.SEQ, c.NB, c.NCORES, c.NPG, c.NPOOL)
    if key not in _CACHE:
        _CACHE[key] = build(c)
    nc = _CACHE[key]
    consts = make_consts_full(c.GS)
    B = c.NCORES
    ckv = np.concatenate([cache_k[0].transpose(0, 2, 3, 1), cache_v[0].transpose(0, 2, 1, 3)], axis=3).reshape(c.NPOOL * c.H * P, 2 * P)
    del cache_k, cache_v

    def tile_w(w):
        k, n = w.shape
        return np.ascontiguousarray(w.reshape(k // P, P, n // P, P).transpose(2, 1, 0, 3)).reshape(n, k)
    w_in_t, w_pa_t, w_pc_t, w_out_t = tile_w(w_in[0]), tile_w(w_pa[0]), tile_w(w_pc[0]), tile_w(w_out[0])

    in_maps = []
    for core in range(B):
        xo = np.zeros((c.TTP, c.D), np.float32)
        xo[0:NMETA] = meta_tokens
        xo[NMETA:c.TP] = x_prompt[core]
        xo[c.TP:c.TT] = x_sample[core * c.NSEQ:(core + 1) * c.NSEQ].reshape(c.NSO, c.D)
        pcol = np.zeros((P, c.NPC), np.float32)
        pcol[:, c.o_gpre:c.o_gpre + c.KC] = g_pre[0].reshape(c.KC, P).T
        pcol[:, c.o_bg:c.o_bg + 2 * c.KC] = b_gate[0].reshape(2 * c.KC, P).T
        pcol[:, c.o_wc:c.o_wc + 3 * c.CC] = w_conv[0].reshape(3, c.CC, P).transpose(2, 1, 0).reshape(P, 3 * c.CC)
        pcol[:, c.o_bc:c.o_bc + c.CC] = b_conv[0].reshape(c.CC, P).T
        st = state_conv[0, core * c.NSEQ:(core + 1) * c.NSEQ]
        pcol[:, c.o_st:c.o_st + c.CC * c.NSEQ * 2] = st.reshape(c.NSEQ, 2, c.CC, P).transpose(3, 2, 0, 1).reshape(P, -1)
        prow = np.zeros((P, c.D + c.H + 1), np.float32)
        prow[:, :c.D] = g_post[0][None, :]
        prow[:, c.D:c.D + c.H] = b_sb[0][None, :]
        pt = np.ascontiguousarray(np.broadcast_to(
            page_table[core * c.NSEQ:(core + 1) * c.NSEQ].reshape(1, c.NSEQ * c.NPG), (P, c.NSEQ * c.NPG)))
        in_maps.append({"x_own": xo, "w_in": w_in_t, "w_pa": w_pa_t, "w_pc": w_pc_t, "w_out": w_out_t,
                        "pcol": pcol, "prow": prow, "consts": consts, "ckv": ckv, "pt": pt})
    res = run_bass_kernel_spmd(nc, in_maps, core_ids=list(range(B)))
    R = res.results
    del in_maps, ckv
    y_prompt = np.stack([R[i]["y_own"][NMETA:c.TP] for i in range(B)])
    y_sample = np.concatenate([R[i]["y_own"][c.TP:c.TT].reshape(c.NSEQ, c.DEC_SEQ, c.D) for i in range(B)])
    kp = np.stack([R[i]["k_own"][:c.TP].reshape(c.TP, c.H, P) for i in range(B)])[None]
    vp = np.stack([R[i]["v_own"][:c.TP].reshape(c.TP, c.H, P) for i in range(B)])[None]
    cp = np.stack([R[i]["conv_own"][0:2] for i in range(B)])[None]
    ks = np.concatenate([R[i]["k_own"][c.TP:c.TT].reshape(c.NSEQ, c.DEC_SEQ, c.H, P) for i in range(B)])[None]
    vs = np.concatenate([R[i]["v_own"][c.TP:c.TT].reshape(c.NSEQ, c.DEC_SEQ, c.H, P) for i in range(B)])[None]
    cs = np.concatenate([R[i]["conv_own"][2:].reshape(c.NSEQ, 2, c.DC) for i in range(B)])[None]
    return (y_prompt, y_sample, kp, vp, cp, ks, vs, cs)


def kernel(**inputs):
    return run(Cfg(), **inputs)
```

```python
import numpy as np
import concourse.bass as bass
import concourse.mybir as mybir
from concourse.bass_utils import run_bass_kernel_spmd

F32, BF16, I32 = mybir.dt.float32, mybir.dt.bfloat16, mybir.dt.int32
AF = mybir.ActivationFunctionType
import os as _os
SILU = AF.Copy if _os.environ.get('KSILU', '') == '0' else AF.Silu
ALU = mybir.AluOpType
ET = mybir.EngineType

P = 128
NMETA = 16
EPS = 1e-6


class Cfg:
    def __init__(self, D=2048, SEQ=2048, DEC_BATCH=128, DEC_SEQ=4, NCORES=8, PAST=2048, NPOOL=2560, mode=1):
        self.mode = mode
        self.NB = DEC_BATCH
        self.NPG = PAST // P
        self.NPOOL = NPOOL
        self.GS = 1
        self.NSA = DEC_BATCH * DEC_SEQ
        self.D = D
        self.KC = D // P
        self.DA = D // 2
        self.H = self.DA // P
        self.DC = D // 2
        self.CC = self.DC // P
        self.SEQ = SEQ
        self.TP = (NMETA + SEQ) if mode == 1 else 0
        self.NCORES = NCORES
        self.NSEQ = DEC_BATCH // NCORES
        self.DEC_SEQ = DEC_SEQ
        self.NSO = self.NSEQ * DEC_SEQ
        self.TT = self.TP + self.NSO
        self.NT = (self.TT + P - 1) // P
        self.TTP = self.NT * P
        self.INC = 4 * self.DA + 4 * self.DC + 2 * D
        self.NCS = 2 + 2 * self.NSEQ
        o = 0
        self.o_gpre = o; o += self.KC
        self.o_bg = o; o += 2 * self.KC
        self.o_wc = o; o += 3 * self.CC
        self.o_bc = o; o += self.CC
        self.o_st = o; o += self.CC * self.NSEQ * 2
        self.NPC = o


class Tile:
    def __init__(self, ap, name):
        self.ap = ap
        self.name = name
        self.w = None
        self.r = {}

    def __getitem__(self, k):
        return self.ap[k]


class Prog:
    ENG = ["sp", "act", "dve", "pool", "pe"]
    NDMASEM = 16

    def __init__(self):
        self.ops = {e: [] for e in self.ENG}
        self.needed = set()

    cut = False

    def barrier(self):
        tk = set()
        for e in self.ENG:
            ops = self.ops[e]
            last_c = [i for i in range(len(ops)) if not ops[i][2]][-1:]
            last_d = [i for i in range(len(ops)) if ops[i][2]][-self.NDMASEM:]
            for i in last_c + last_d:
                tk.add((e, i))
        self.pending = {e: set(tk) for e in self.ENG}

    pending = {}

    def op(self, eng, fn, reads=(), writes=(), dma=False):
        if self.cut:
            return None
        deps = set(self.pending.pop(eng, ()))
        for t in reads:
            if t.w is not None:
                deps.add(t.w)
        for t in writes:
            if t.w is not None:
                deps.add(t.w)
            for tk in t.r.values():
                deps.add(tk)
        tk = (eng, len(self.ops[eng]))
        if eng == "pe":
            deps = {d for d in deps if d[0] != "pe"}
        deps.discard(tk)
        self.ops[eng].append((fn, deps, dma))
        self.needed |= deps
        for t in reads:
            t.r[eng] = tk
        for t in writes:
            t.w = tk
            t.r = {}
        return tk

    def emit(self, nc, block, sems, dsems, final_waits):
        val = {}
        cnt = {e: 0 for e in self.ENG}
        dcnt = {}
        dma_prev = {}
        for e in self.ENG:
            ndma = 0
            for i, (fn, deps, dma) in enumerate(self.ops[e]):
                if dma:
                    s = (e, ndma % self.NDMASEM)
                    dcnt[s] = dcnt.get(s, 0) + 16
                    val[(e, i)] = ("d", s, dcnt[s])
                    dma_prev[(e, i)] = ("d", s, dcnt[s] - 16) if dcnt[s] > 16 else None
                    ndma += 1
                elif (e, i) in self.needed:
                    cnt[e] += 1
                    val[(e, i)] = ("e", e, cnt[e])
        all_dma = [(e, i) for e in self.ENG for i, o in enumerate(self.ops[e]) if o[2]]

        def run(e, engobj):
            waited = {}

            def wait(v):
                if v is None:
                    return
                kind, s, n = v
                key = (kind, s)
                if waited.get(key, 0) >= n:
                    return
                waited[key] = n
                engobj.wait_ge(dsems[s] if kind == "d" else sems[s], n)

            for i, (fn, deps, dma) in enumerate(self.ops[e]):
                for d in sorted(deps):
                    wait(val[d])
                if dma:
                    wait(dma_prev[(e, i)])
                ins = fn(engobj)
                v = val.get((e, i))
                if v is not None:
                    kind, s, n = v
                    ins.then_inc(dsems[s] if kind == "d" else sems[s], 16 if kind == "d" else 1)
            if e == "sp":
                for d in all_dma:
                    wait(val[d])
                for d in final_waits:
                    wait(val[d])

        block.sync(lambda eng: run("sp", eng))
        block.scalar(lambda eng: run("act", eng))
        block.vector(lambda eng: run("dve", eng))
        block.gpsimd(lambda eng: run("pool", eng))
        block.tensor(lambda eng: run("pe", eng))


def build(cfg):
    c = cfg
    D, KC, DA, H, DC, CC, TP, TT, NT, TTP = c.D, c.KC, c.DA, c.H, c.DC, c.CC, c.TP, c.TT, c.NT, c.TTP
    NSEQ, NCS = c.NSEQ, c.NCS
    nc = bass.Bass("TRN2", target_bir_lowering=False)
    dt = lambda n, s, k="ExternalInput": nc.dram_tensor(n, s, F32, kind=k).ap()
    x_own = dt("x_own", [TTP, D])
    w_in = dt("w_in", [c.INC, D])
    w_pa = dt("w_pa", [D, DA])
    w_pc = dt("w_pc", [D, DC])
    w_out = dt("w_out", [D, D])
    pcol_d = dt("pcol", [P, c.NPC])
    prow_d = dt("prow", [P, D + H + 1])
    NCST = 4 * P + c.GS * 4 + 1
    consts_d = dt("consts", [P, NCST])
    if c.mode == 1:
        ckv = dt("ckv", [c.NPOOL * (H // 2) * P, 4 * P])
        pt_d = nc.dram_tensor("pt", [P, c.NSEQ * c.NPG], I32, kind="ExternalInput").ap()
    else:
        a_in = dt("a_in", [P, H * P])
    y_own = dt("y_own", [TTP, D], "ExternalOutput")
    k_own = dt("k_own", [TTP, DA], "ExternalOutput")
    v_own = dt("v_own", [TTP, DA], "ExternalOutput")
    conv_own = dt("conv_own", [NCS, DC], "ExternalOutput")

    pg = Prog()
    import contextlib
    es = contextlib.ExitStack()

    def sb(name, shape, dtype=F32):
        return Tile(es.enter_context(nc.sbuf_tensor("s_" + name, shape, dtype)), name)

    def ps(name):
        return Tile(es.enter_context(nc.psum_tensor(name, [P, 512], F32)), name)

    with es:
        sems = {e: es.enter_context(nc.semaphore("sem_" + e)) for e in Prog.ENG}
        dsems = {(e_, i): es.enter_context(nc.semaphore("dsem_%s%d" % (e_, i))) for e_ in ("sp", "pool") for i in range(Prog.NDMASEM)}
        pcol = sb("pcol", [P, c.NPC])
        prow = sb("prow", [P, D + H + 1])
        cst = sb("cst", [P, NCST])
        ident = cst
        cbf = sb("cbf", [P, NCST - P], BF16)
        ones1 = sb("ones1", [P, 1])
        NWST = 3
        wst = [sb("wst%d" % i, [P, KC, P]) for i in range(NWST)]
        wbf = [sb("wbf%d" % i, [P, KC, P], BF16) for i in range(2)]
        KH = max(1, KC // 2)
        wbf_lo = [Tile(wbf[i].ap[:, 0:KH, :], "wlo") for i in range(2)]
        wbf_hi = [Tile(wbf[i].ap[:, KH:KC, :], "whi") for i in range(2)]
        PS = [ps("ps%d" % i) for i in range(8)]
        ucs = sb("ucs", [P, CC, NCS])
        uh = sb("uh", [P, CC, 2])

        dma = lambda fn, reads=(), writes=(): pg.op("sp", fn, reads, writes, dma=True)

        dma(lambda e: e.dma_start(out=pcol[:], in_=pcol_d), writes=[pcol])
        dma(lambda e: e.dma_start(out=prow[:], in_=prow_d), writes=[prow])
        dma(lambda e: e.dma_start(out=cst[:], in_=consts_d), writes=[cst])
        pg.op("pool", lambda e: e.tensor_copy(out=cbf[:], in_=cst[:, P:NCST]), [cst], [cbf])
        pg.op("pool", lambda e: e.memset(ones1[:], 1.0), [], [ones1])
        pg.op("pool", lambda e: e.memset(uh[:], 0.0), [], [uh])
        negtri = lambda k, m: cbf[:k, 0:m]
        negones = lambda k, m: cbf[:k, P:P + m]
        masku = lambda k, m: cbf[:k, 2 * P:2 * P + m]
        maskp = lambda: cbf[:, 3 * P:3 * P + c.GS * 4]
        import os

        def stage_a(x_src, ntiles, hT, tag):
          with contextlib.ExitStack() as sa:
            sbl = lambda name, shape, dtype=F32: Tile(sa.enter_context(nc.sbuf_tensor("s_" + tag + name, shape, dtype)), name)
            xt = [sbl("xt%d" % i, [P, D]) for i in range(2)]
            xn = [sbl("xn%d" % i, [P, D]) for i in range(2)]
            junk = sbl("junk", [P, D], BF16)
            ss = [sbl("ss%d" % i, [P, 1]) for i in range(2)]
            rs = [sbl("rs%d" % i, [P, 1]) for i in range(2)]
            for i in range(ntiles):
                X, XN, SS, RS = xt[i % 2], xn[i % 2], ss[i % 2], rs[i % 2]
                dma(lambda e, X=X, i=i: e.dma_start(out=X[:], in_=x_src[i * P:(i + 1) * P, :]), writes=[X])
                pg.op("act", lambda e, X=X, SS=SS: e.activation(out=junk[:], in_=X[:], func=AF.Square, accum_out=SS[:]),
                      [X], [junk, SS])
                pg.op("dve", lambda e, SS=SS, RS=RS: e.tensor_scalar(out=RS[:], in0=SS[:], scalar1=1.0 / D, scalar2=EPS,
                                                                      op0=ALU.mult, op1=ALU.add), [SS], [RS])
                pg.op("act", lambda e, RS=RS: e.activation(out=RS[:], in_=RS[:], func=AF.Ln), [RS], [RS])
                pg.op("act", lambda e, RS=RS: e.activation(out=RS[:], in_=RS[:], func=AF.Exp, scale=-0.5), [RS], [RS])
                pg.op("act", lambda e, X=X, XN=XN, RS=RS: e.activation(out=XN[:], in_=X[:], func=AF.Copy, scale=RS[:, 0:1]),
                      [X, RS], [XN])
                for g in range((KC + 3) // 4):
                    B = PS[g % 2]
                    nq = min(4, KC - 4 * g)
                    for q in range(nq):
                        kc = 4 * g + q
                        pg.op("pe", lambda e, B=B, XN=XN, q=q, kc=kc: e.transpose(
                            out=B[:, q * P:(q + 1) * P], in_=XN[:, kc * P:(kc + 1) * P], identity=ident[:, 0:P]),
                            [XN, cst], [B])
                    for q in range(nq):
                        kc = 4 * g + q
                        pg.op("dve", lambda e, B=B, q=q, kc=kc, i=i: e.tensor_scalar(
                            out=hT[:, kc, i * P:(i + 1) * P], in0=B[:, q * P:(q + 1) * P],
                            scalar1=pcol[:, c.o_gpre + kc:c.o_gpre + kc + 1], scalar2=None, op0=ALU.mult),
                            [B, pcol], [hT])
          pg.barrier()

        slab_i = [0]
        pp_i = [0]

        def proj(wd, kcx, col0, act, act_c0, ranges, evac, pbanks=(0, 1)):
            k = slab_i[0]
            slab_i[0] += 1
            ST, WB, WLO, WHI = wst[k % NWST], wbf[k % 2], wbf_lo[k % 2], wbf_hi[k % 2]
            dma(lambda e: e.dma_start(out=ST[:, 0:kcx, :], in_=wd[col0:col0 + P, :].rearrange("p (kc c) -> p kc c", c=P)),
                writes=[ST])
            kh = min(KH, kcx)
            pg.op("pool", lambda e: e.tensor_copy(out=WB[:, 0:kh, :], in_=ST[:, 0:kh, :]), [ST], [WLO])
            if kcx > kh:
                pg.op("act", lambda e: e.activation(out=WB[:, kh:kcx, :], in_=ST[:, kh:kcx, :], func=AF.Copy), [ST], [WHI])
            for (t0, n) in ranges:
                B = PS[pbanks[pp_i[0] % len(pbanks)]]
                pp_i[0] += 1
                for kc in range(kcx):
                    pg.op("pe", lambda e, B=B, kc=kc, t0=t0, n=n: e.matmul(
                        B[:, 0:n], lhsT=WB[:, kc, :], rhs=act[:, kc, act_c0 + t0:act_c0 + t0 + n],
                        start=(kc == 0), stop=(kc == kcx - 1)), [WLO if kc < KH else WHI, act], [B])
                evac(B, t0, n)

        def phase_s():
          NPG, GS = c.NPG, 1
          NSO, NSQ = c.NSO, c.NSEQ
          NPP = NPG + 1
          G4 = GS * 4
          W = NPP * G4
          NPT = NSQ * NPG
          HP = H // 2
          with contextlib.ExitStack() as sa:
            sbl = lambda name, shape, dtype=F32: Tile(sa.enter_context(nc.sbuf_tensor("s_ps_" + name, shape, dtype)), name)
            pts = Tile(sa.enter_context(nc.sbuf_tensor("s_ps_pts", [P, NPT], I32)), "pts")
            ptf = sbl("ptf", [P, NPT])
            idxf = sbl("idxf", [P, NPT])
            idxh = [Tile(sa.enter_context(nc.sbuf_tensor("s_ps_idx%d" % i, [P, NPT], I32)), "idx") for i in range(2)]
            kvb = [sbl("kvb%d" % i, [P, NPG, 4 * P], BF16) for i in range(2)]
            kvslot = [[Tile(kvb[i].ap[:, sl, :], "kvs") for sl in range(NPG)] for i in range(2)]
            kpad = [sbl("kpad%d" % i, [P, GS, P], BF16) for i in range(2)]
            vpad = [sbl("vpad%d" % i, [P, GS, P], BF16) for i in range(2)]
            zsb = sbl("zsb", [P, W])
            E = sbl("E", [P, W])
            lg = sbl("lg", [P, W])
            L = sbl("L", [P, W], BF16)
            AT = sbl("AT", [P, W], BF16)
            LKb = sbl("LKb", [P, W], BF16)
            LK32 = sbl("LK32", [P, W])
            dma(lambda e: e.dma_start(out=pts[:], in_=pt_d), writes=[pts])
            pg.op("dve", lambda e: e.tensor_copy(out=ptf[:], in_=pts[:]), [pts], [ptf])
            pg.op("dve", lambda e: e.tensor_scalar(out=ptf[:], in0=ptf[:], scalar1=float(HP * P), scalar2=cst[:, NCST - 1:NCST],
                                                   op0=ALU.mult, op1=ALU.add), [ptf, cst], [ptf])
            for i in range(2):
                pg.op("pool", lambda e, i=i: e.memset(kpad[i][:], 0.0), [], [kpad[i]])
                pg.op("pool", lambda e, i=i: e.memset(vpad[i][:], 0.0), [], [vpad[i]])
            pg.op("pool", lambda e: e.memset(LK32[:], 0.0), [], [LK32])
            units = [(hp, b) for hp in range(HP) for b in range(NSQ)]

            def head_idx(hp):
                IX = idxh[hp % 2]
                pg.op("dve", lambda e: e.tensor_scalar(out=idxf[:], in0=ptf[:], scalar1=float(hp * P), scalar2=None, op0=ALU.add),
                      [ptf], [idxf])
                pg.op("dve", lambda e: e.tensor_copy(out=IX[:], in_=idxf[:]), [idxf], [IX])

            def loads(u):
                hp, b = units[u]
                if b == 0:
                    head_idx(hp)
                KV, KVL, IX = kvb[u % 2], kvslot[u % 2], idxh[hp % 2]
                for p in range(NPG):
                    page_load(KV, KVL, IX, p, b * NPG + p)

            def page_load(KV, KVL, IX, slot, j):
                pg.op("pool", lambda e: e.indirect_dma_start(
                    out=KV[:, slot, :], out_offset=None, in_=ckv,
                    in_offset=bass.IndirectOffsetOnAxis(ap=IX[:, j:j + 1], axis=0)), [IX], [KVL[slot]], dma=True)

            cnt = [0]

            def unit(u):
                hp, b = units[u]
                if u == 0:
                    loads(0)
                if u + 1 < len(units):
                    loads(u + 1)
                for hh in range(2):
                    group(u, 2 * hp + hh, hh, b)

            def group(u, h, hh, b):
                i2 = cnt[0] % 2
                cnt[0] += 1
                KP, VP = kpad[i2], vpad[i2]
                KV, KVL = kvb[u % 2], kvslot[u % 2]
                ko, vo = hh * 2 * P, hh * 2 * P + P
                AB, CB, OB, VT = PS[2 + i2], PS[4 + i2], PS[6 + i2], PS[0]
                bias_c = lambda: prow[:, D + h:D + h + 1]
                pg.op("dve", lambda e: e.tensor_copy(out=KP[:, 0, 0:4], in_=kS[:, h, 4 * b:4 * b + 4]), [kS], [KP])
                pg.op("pe", lambda e: e.transpose(out=VT[0:4, 0:P], in_=vS32[:, h, 4 * b:4 * b + 4], identity=ident[:, 0:P]), [vS32, cst], [VT])
                pg.op("dve", lambda e: e.tensor_copy(out=VP[0:4, 0, :], in_=VT[0:4, 0:P]), [VT], [VP])
                for p in range(NPP):
                    zmm(AB, KP, KV, KVL, ko, p, p * 4, h, b)
                pg.op("dve", lambda e: e.tensor_copy(out=zsb[:], in_=AB[:, 0:W]), [AB], [zsb])
                pg.op("act", lambda e: e.activation(out=E[:], in_=zsb[:], func=AF.Exp, bias=bias_c()), [zsb, prow], [E])
                pg.op("act", lambda e: e.activation(out=L[:], in_=E[:], func=AF.Ln, bias=ones1[:, 0:1]), [E, ones1], [L])
                pg.op("dve", lambda e: e.tensor_tensor(out=L[:, NPG * G4:W], in0=L[:, NPG * G4:W], in1=maskp(), op=ALU.mult), [L, cbf], [L])
                for p in range(NPG - 1, -1, -1):
                    pg.op("dve", lambda e, p=p: e.tensor_tensor(out=LK32[:, p * G4:(p + 1) * G4], in0=LK32[:, (p + 1) * G4:(p + 2) * G4],
                                                                in1=L[:, (p + 1) * G4:(p + 2) * G4], op=ALU.add), [LK32, L], [LK32])
                pg.op("dve", lambda e: e.tensor_copy(out=LKb[:], in_=LK32[:]), [LK32], [LKb])
                pg.op("pe", lambda e: e.matmul(CB[:, 0:W], lhsT=negtri(P, P), rhs=L[:], start=True, stop=False), [L, cbf], [CB])
                pg.op("pe", lambda e: e.matmul(CB[:, 0:W], lhsT=negones(P, P), rhs=LKb[:], start=False, stop=True), [LKb, cbf], [CB])
                pg.op("dve", lambda e: e.tensor_tensor(out=lg[:], in0=CB[:, 0:W], in1=zsb[:], op=ALU.add), [CB, zsb], [lg])
                pg.op("act", lambda e: e.activation(out=AT[:], in_=lg[:], func=AF.Exp, bias=bias_c()), [lg, prow], [AT])
                pg.op("dve", lambda e: e.tensor_tensor(out=AT[:, NPG * G4:W], in0=AT[:, NPG * G4:W], in1=maskp(), op=ALU.mult), [AT, cbf], [AT])
                for p in range(NPP):
                    avmm(OB, VP, KV, KVL, vo, p, p * 4)
                pg.op("dve", lambda e: e.tensor_tensor(out=a_t[:, h, TP + 4 * b:TP + 4 * b + 4], in0=OB[:, 0:4], in1=sgS[:, h, 4 * b:4 * b + 4],
                                                       op=ALU.mult), [OB, sgS], [a_t])

            def zmm(AB, KP, KV, KVL, ko, p, col, h, b):
                if p < NPG:
                    pg.op("pe", lambda e: e.matmul(AB[:, col:col + 4], lhsT=KV[:, p, ko:ko + P], rhs=qS[:, h, 4 * b:4 * b + 4],
                                                   start=True, stop=True), [KVL[p], qS], [AB])
                else:
                    pg.op("pe", lambda e: e.matmul(AB[:, col:col + 4], lhsT=KP[:, 0, :], rhs=qS[:, h, 4 * b:4 * b + 4],
                                                   start=True, stop=True), [KP, qS], [AB])

            def avmm(OB, VP, KV, KVL, vo, p, col):
                if p < NPG:
                    pg.op("pe", lambda e: e.matmul(OB[:, 0:4], lhsT=KV[:, p, vo:vo + P], rhs=AT[:, col:col + 4],
                                                   start=(p == 0), stop=False, skip_group_check=True), [KVL[p], AT], [OB])
                else:
                    pg.op("pe", lambda e: e.matmul(OB[:, 0:4], lhsT=VP[:, 0, :], rhs=AT[:, col:col + 4],
                                                   start=False, stop=True, skip_group_check=True), [VP, AT], [OB])

            for u in range(len(units)):
                unit(u)
          pg.barrier()

        full_ranges = [(t, min(512, TTP - t)) for t in range(0, TTP, 512)]

        hT = sb("hT", [P, KC, TTP], BF16)
        a_t = sb("a_t", [P, H, TTP], BF16)
        qS = sb("qS", [P, H, c.NSO], BF16)
        kS = sb("kS", [P, H, c.NSO], BF16)
        vS32 = sb("vS32", [P, H, c.NSO])
        sgS = sb("sgS", [P, H, c.NSO])
        if c.mode == 1:
            pg.op("pool", lambda e: e.memset(a_t[:], 0.0), [], [a_t])
        else:
            for hh in range(H):
                ST = wst[hh % 2]
                dma(lambda e, ST=ST, hh=hh: e.dma_start(out=ST[:, 0, :], in_=a_in[:, hh * P:(hh + 1) * P]), writes=[ST])
                pg.op("pool", lambda e, ST=ST, hh=hh: e.tensor_copy(out=a_t[:, hh, 0:P], in_=ST[:, 0, :]), [ST], [a_t])
        stage_a(x_own, NT, hT, "a")
        if os.environ.get("KSTOP", "") == "A":
            pg.cut = True

        with contextlib.ExitStack() as sa:
            sbl = lambda name, shape, dtype=F32: Tile(sa.enter_context(nc.sbuf_tensor("s_" + name, shape, dtype)), name)
            qT = sbl("qT", [P, TTP], BF16)
            kT = sbl("kT", [P, TTP], BF16)
            vtok = sbl("vtok", [P, NT, P], BF16)
            sg = sbl("sg", [P, TTP], BF16)
            kv32 = [sbl("kv32_%d" % i, [P, 512]) for i in range(2)]
            tokst = [sbl("tokst%d" % i, [P, 4, P]) for i in range(2)]
            e32 = [sbl("e32_%d" % i, [P, 512]) for i in range(2)]
            lkp = [sbl("lkp%d" % i, [P, 512], BF16) for i in range(2)]
            lks = [sbl("lks%d" % i, [P, 512], BF16) for i in range(2)]
            att = [sbl("att%d" % i, [P, 512], BF16) for i in range(2)]
            kvi = [0]
            scale = float(P) ** -0.5

            def kv_out(B, t0, n, out_d, h, isv):
                i = kvi[0] % 2
                kvi[0] += 1
                K32, TS, PT = kv32[i], tokst[i], PS[2 + i]
                pg.op("act", lambda e: e.activation(out=K32[:, 0:n], in_=B[:, 0:n], func=AF.Copy), [B], [K32])
                nb = n // P
                if os.environ.get("KKV", "") == "min":
                    return K32
                for j in range(nb):
                    pg.op("pe", lambda e, j=j: e.transpose(out=PT[:, j * P:(j + 1) * P], in_=K32[:, j * P:(j + 1) * P],
                                                           identity=ident[:, 0:P]), [K32, cst], [PT])
                pg.op("dve", lambda e: e.tensor_copy(out=TS[:, 0:nb, :], in_=PT[:, 0:nb * P].rearrange("p (j d) -> p j d", d=P)),
                      [PT], [TS])
                if isv:
                    pg.op("pool", lambda e: e.tensor_copy(out=vtok[:, t0 // P:t0 // P + nb, :], in_=TS[:, 0:nb, :]), [TS], [vtok])
                if os.environ.get("KKV", "") != "noout":
                    dma(lambda e: e.dma_start(out=out_d[t0:t0 + n, h * P:(h + 1) * P].rearrange("(j p) d -> p j d", p=P),
                                              in_=TS[:, 0:nb, :]), reads=[TS])
                return K32

            def head(h):
                def ev_q(B, t0, n):
                    pg.op("act", lambda e: e.activation(out=qT[:, t0:t0 + n], in_=B[:, 0:n], func=AF.Copy, scale=scale), [B], [qT])
                proj(w_in, KC, h * P, hT, 0, full_ranges, ev_q)
                if os.environ.get("KSTOP", "") == "P0":
                    pg.cut = True

                def ev_k(B, t0, n, h=h):
                    K32 = kv_out(B, t0, n, k_own, h, False)
                    pg.op("dve", lambda e: e.tensor_copy(out=kT[:, t0:t0 + n], in_=K32[:, 0:n]), [K32], [kT])
                proj(w_in, KC, DA + h * P, hT, 0, full_ranges, ev_k)

                def ev_v(B, t0, n, h=h):
                    K32 = kv_out(B, t0, n, v_own, h, True)
                    if t0 <= TP and TT <= t0 + n:
                        pg.op("dve", lambda e: e.tensor_copy(out=vS32[:, h, :], in_=K32[:, TP - t0:TT - t0]), [K32], [vS32])
                proj(w_in, KC, 2 * DA + h * P, hT, 0, full_ranges, ev_v)

                def ev_g(B, t0, n):
                    pg.op("act", lambda e: e.activation(out=sg[:, t0:t0 + n], in_=B[:, 0:n], func=SILU), [B], [sg])
                proj(w_in, KC, 3 * DA + h * P, hT, 0, full_ranges, ev_g)

                pg.op("dve", lambda e: e.tensor_copy(out=qS[:, h, :], in_=qT[:, TP:TT]), [qT], [qS])
                pg.op("dve", lambda e: e.tensor_copy(out=kS[:, h, :], in_=kT[:, TP:TT]), [kT], [kS])
                pg.op("dve", lambda e: e.tensor_copy(out=sgS[:, h, :], in_=sg[:, TP:TT]), [sg], [sgS])
                bias_h = lambda k, h=h: prow[:k, D + h:D + h + 1]
                if os.environ.get("KSTOP", "") == "P1":
                    pg.cut = True
                for qi, q0 in enumerate(range(0, TP, 512)):
                    qtile(h, bias_h, qi, q0)

            def qtile(h, bias_h, qi, q0):
                if True:
                    qn = min(512, TP - q0)
                    OB = PS[6 + qi % 2]
                    pg.op("pool", lambda e: e.memset(lks[0][:], 0.0), [], [lks[0]])
                    pg.op("pool", lambda e: e.memset(lks[1][:], 0.0), [], [lks[1]])
                    S_hi = (q0 + qn - 1) // P
                    for st, S in enumerate(range(S_hi, -1, -1)):
                        step(h, bias_h, q0, qn, OB, st, S)
                    pg.op("dve", lambda e: e.tensor_tensor(
                        out=a_t[:, h, q0:q0 + qn], in0=OB[:, 0:qn], in1=sg[:, q0:q0 + qn], op=ALU.mult), [OB, sg], [a_t])

            def step(h, bias_h, q0, qn, OB, st, S):
                if True:
                    if True:
                        li = st
                        nk = min(P, TP - S * P)
                        c0 = max(q0, S * P) - q0
                        diag = S * P >= q0
                        nd = min(P, qn - c0)
                        AB = PS[4 + st % 2]
                        E, L, AT = e32[st % 2], lkp[st % 2], att[st % 2]
                        LO, LN = lks[li % 2], lks[(li + 1) % 2]
                        first, last = (st == 0), (S == 0)
                        pg.op("pe", lambda e, AB=AB, S=S, nk=nk, c0=c0, qn=qn, q0=q0: e.matmul(
                            AB[:nk, c0:qn], lhsT=kT[:, S * P:S * P + nk], rhs=qT[:, q0 + c0:q0 + qn], start=True, stop=False),
                            [kT, qT], [AB])
                        pg.op("act", lambda e, AB=AB, E=E, nk=nk, c0=c0, qn=qn: e.activation(
                            out=E[:nk, c0:qn], in_=AB[:nk, c0:qn], func=AF.Exp, bias=bias_h(nk)), [AB, prow], [E])
                        pg.op("act", lambda e, L=L, E=E, nk=nk, c0=c0, qn=qn: e.activation(
                            out=L[:nk, c0:qn], in_=E[:nk, c0:qn], func=AF.Ln, bias=ones1[:nk, 0:1]), [E, ones1], [L])
                        if diag:
                            pg.op("pool", lambda e, L=L, nk=nk, c0=c0, nd=nd: e.tensor_tensor(
                                out=L[:nk, c0:c0 + nd], in0=L[:nk, c0:c0 + nd], in1=masku(nk, nd), op=ALU.mult), [L, cbf], [L])
                        pg.op("pe", lambda e, AB=AB, L=L, nk=nk, c0=c0, qn=qn, first=first: e.matmul(
                            AB[:nk, c0:qn], lhsT=negtri(nk, nk), rhs=L[:nk, c0:qn], start=False, stop=first,
                            skip_group_check=True), [L, cbf], [AB])
                        if not first:
                            pg.op("pe", lambda e, AB=AB, LO=LO, nk=nk, c0=c0, qn=qn: e.matmul(
                                AB[:nk, c0:qn], lhsT=negones(P, nk), rhs=LO[:, c0:qn], start=False, stop=True,
                                skip_group_check=True), [LO, cbf], [AB])
                        if not last:
                            pg.op("dve", lambda e, LO=LO, LN=LN, L=L, nk=nk, c0=c0, qn=qn: e.tensor_tensor(
                                out=LN[:nk, c0:qn], in0=LO[:nk, c0:qn], in1=L[:nk, c0:qn], op=ALU.add), [LO, L], [LN])
                        pg.op("act", lambda e, AB=AB, AT=AT, nk=nk, c0=c0, qn=qn: e.activation(
                            out=AT[:nk, c0:qn], in_=AB[:nk, c0:qn], func=AF.Exp, bias=bias_h(nk)), [AB, prow], [AT])
                        if diag:
                            pg.op("pool", lambda e, AT=AT, nk=nk, c0=c0, nd=nd: e.tensor_tensor(
                                out=AT[:nk, c0:c0 + nd], in0=AT[:nk, c0:c0 + nd], in1=masku(nk, nd), op=ALU.mult), [AT, cbf], [AT])
                        pg.op("pe", lambda e, OB=OB, AT=AT, S=S, nk=nk, c0=c0, qn=qn, first=first, last=last: e.matmul(
                            OB[:, c0:qn], lhsT=vtok[:nk, S, :], rhs=AT[:nk, c0:qn], start=first, stop=last,
                            skip_group_check=True), [vtok, AT], [OB])
            if c.mode == 1:
                for h in range(H):
                    head(h)
        pg.barrier()
        if c.mode == 1 and os.environ.get("KSTOP", "") != "P":
            phase_s()
        if os.environ.get("KSTOP", "") == "S":
            pg.cut = True

        if os.environ.get("KSTOP", "") == "P":
            pg.cut = True
        with contextlib.ExitStack() as sa:
            sbl = lambda name, shape, dtype=F32: Tile(sa.enter_context(nc.sbuf_tensor("s_" + name, shape, dtype)), name)
            cpart = sbl("cpart", [P, CC, 256], BF16)
            merged = sbl("merged", [P, KC, 256], BF16)
            oT = sbl("oT", [P, KC, 256])
            cb32 = sbl("cb32", [P, 256])
            cc32 = sbl("cc32", [P, 256])
            uext = sbl("uext", [P, 258])
            sgc = sbl("sgc", [P, 256])
            yc = sbl("yc", [P, 256])
            ues = sbl("ues", [P, NSEQ, 6])
            ga32 = cb32
            gc32 = cc32
            t1 = sbl("t1", [P, 256])
            t2 = sbl("t2", [P, 512])
            ssq = sbl("ssq", [P, 8])
            rstd = sbl("rstd", [P, 1])
            xq = [sbl("xq%d" % i, [P, 512]) for i in range(2)]
            yq = [sbl("yq%d" % i, [P, 512]) for i in range(2)]
            cstage = oT
            wc = lambda j, i: pcol[:, c.o_wc + 3 * j + i:c.o_wc + 3 * j + i + 1]
            bc = lambda j: pcol[:, c.o_bc + j:c.o_bc + j + 1]
            xi = [0]
            PART = 256
            part_ranges = [(t, min(PART, TTP - t)) for t in range(0, TTP, PART)]

            def part(t0, n):
                rng = [(0, n)]
                np_ = max(0, min(t0 + n, TP) - t0)
                s0, s1 = max(t0, TP) - t0, min(t0 + n, TT) - t0
                has_s = s1 > s0
                if has_s:
                    assert s1 - s0 == c.NSO, "sample tokens must sit inside one part"
                def convchunk(j):
                    base = 4 * DA
                    proj(w_in, KC, base + j * P, hT, t0, rng,
                         lambda B, _t, _n: pg.op("act", lambda e: e.activation(out=cb32[:, 0:n], in_=B[:, 0:n], func=AF.Copy), [B], [cb32]))
                    proj(w_in, KC, base + DC + j * P, hT, t0, rng,
                         lambda B, _t, _n: pg.op("act", lambda e: e.activation(out=cc32[:, 0:n], in_=B[:, 0:n], func=AF.Copy), [B], [cc32]))
                    proj(w_in, KC, base + 2 * DC + j * P, hT, t0, rng,
                         lambda B, _t, _n: pg.op("dve", lambda e: e.tensor_tensor(out=uext[:, 2:2 + n], in0=B[:, 0:n], in1=cc32[:, 0:n], op=ALU.mult),
                                                 [B, cc32], [uext]))
                    proj(w_in, KC, base + 3 * DC + j * P, hT, t0, rng,
                         lambda B, _t, _n: pg.op("act", lambda e: e.activation(out=sgc[:, 0:n], in_=B[:, 0:n], func=SILU), [B], [sgc]))
                    pg.op("dve", lambda e, j=j: e.tensor_copy(out=uext[:, 0:2], in_=uh[:, j, :]), [uh], [uext])
                    pg.op("dve", lambda e, j=j: e.tensor_scalar(out=yc[:, 0:n], in0=uext[:, 2:2 + n], scalar1=wc(j, 2), scalar2=bc(j),
                                                                 op0=ALU.mult, op1=ALU.add), [uext, pcol], [yc])
                    pg.op("dve", lambda e, j=j: e.scalar_tensor_tensor(out=yc[:, 0:n], in0=uext[:, 1:1 + n], scalar=wc(j, 1), in1=yc[:, 0:n],
                                                                        op0=ALU.mult, op1=ALU.add), [uext, pcol, yc], [yc])
                    pg.op("dve", lambda e, j=j: e.scalar_tensor_tensor(out=yc[:, 0:n], in0=uext[:, 0:n], scalar=wc(j, 0), in1=yc[:, 0:n],
                                                                        op0=ALU.mult, op1=ALU.add), [uext, pcol, yc], [yc])
                    if has_s:
                        so = c.o_st + j * NSEQ * 2
                        pg.op("dve", lambda e, so=so: e.tensor_copy(out=ues[:, :, 0:2], in_=pcol[:, so:so + 2 * NSEQ].rearrange("p (b r) -> p b r", r=2)),
                              [pcol], [ues])
                        pg.op("dve", lambda e: e.tensor_copy(out=ues[:, :, 2:6], in_=uext[:, 2 + s0:2 + s1].rearrange("p (b t) -> p b t", t=4)),
                              [uext], [ues])
                        ysv = lambda: yc[:, s0:s1].rearrange("p (b t) -> p b t", t=4)
                        pg.op("dve", lambda e, j=j: e.tensor_scalar(out=ysv(), in0=ues[:, :, 2:6], scalar1=wc(j, 2), scalar2=bc(j),
                                                                     op0=ALU.mult, op1=ALU.add), [ues, pcol], [yc])
                        pg.op("dve", lambda e, j=j: e.scalar_tensor_tensor(out=ysv(), in0=ues[:, :, 1:5], scalar=wc(j, 1), in1=ysv(),
                                                                            op0=ALU.mult, op1=ALU.add), [ues, pcol, yc], [yc])
                        pg.op("dve", lambda e, j=j: e.scalar_tensor_tensor(out=ysv(), in0=ues[:, :, 0:4], scalar=wc(j, 0), in1=ysv(),
                                                                            op0=ALU.mult, op1=ALU.add), [ues, pcol, yc], [yc])
                        pg.op("dve", lambda e, j=j: e.tensor_copy(out=ucs[:, j, 2:NCS].rearrange("p (b r) -> p b r", r=2), in_=ues[:, :, 4:6]),
                              [ues], [ucs])
                    if t0 <= TP - 2 and TP <= t0 + n:
                        pg.op("dve", lambda e, j=j: e.tensor_copy(out=ucs[:, j, 0:2], in_=uext[:, 2 + TP - 2 - t0:2 + TP - t0]), [uext], [ucs])
                    if np_ >= 2:
                        pg.op("dve", lambda e, j=j: e.tensor_copy(out=uh[:, j, :], in_=uext[:, np_:np_ + 2]), [uext], [uh])
                    pg.op("dve", lambda e: e.tensor_tensor(out=t1[:, 0:n], in0=cb32[:, 0:n], in1=yc[:, 0:n], op=ALU.mult), [cb32, yc], [t1])
                    pg.op("dve", lambda e, j=j: e.tensor_tensor(out=cpart[:, j, 0:n], in0=t1[:, 0:n], in1=sgc[:, 0:n], op=ALU.mult),
                          [t1, sgc], [cpart])
                for j in range(CC):
                    convchunk(j)

                def mergechunk(f):
                    base = 4 * DA + 4 * DC
                    bga = pcol[:, c.o_bg + f:c.o_bg + f + 1]
                    bgc = pcol[:, c.o_bg + KC + f:c.o_bg + KC + f + 1]
                    proj(w_in, KC, base + f * P, hT, t0, rng,
                         lambda B, _t, _n, bga=bga: pg.op("act", lambda e: e.activation(out=ga32[:, 0:n], in_=B[:, 0:n], func=AF.Sigmoid, bias=bga),
                                                          [B, pcol], [ga32]))
                    proj(w_in, KC, base + D + f * P, hT, t0, rng,
                         lambda B, _t, _n, bgc=bgc: pg.op("act", lambda e: e.activation(out=gc32[:, 0:n], in_=B[:, 0:n], func=AF.Sigmoid, bias=bgc),
                                                          [B, pcol], [gc32]))
                    proj(w_pa, H, f * P, a_t, t0, rng,
                         lambda B, _t, _n: pg.op("dve", lambda e: e.tensor_tensor(out=t1[:, 0:n], in0=B[:, 0:n], in1=ga32[:, 0:n], op=ALU.mult),
                                                 [B, ga32], [t1]))

                    def ev_pc(B, _t, _n, f=f):
                        pg.op("dve", lambda e: e.tensor_tensor(out=t2[:, 0:n], in0=B[:, 0:n], in1=gc32[:, 0:n], op=ALU.mult), [B, gc32], [t2])
                        pg.op("dve", lambda e: e.tensor_tensor(out=merged[:, f, 0:n], in0=t1[:, 0:n], in1=t2[:, 0:n], op=ALU.add), [t1, t2], [merged])
                    proj(w_pc, CC, f * P, cpart, 0, rng, ev_pc)
                for f in range(KC):
                    mergechunk(f)

                def outchunk(f):
                    proj(w_out, KC, f * P, merged, 0, rng,
                         lambda B, _t, _n: pg.op("act", lambda e: e.activation(out=oT[:, f, 0:n], in_=B[:, 0:n], func=AF.Copy), [B], [oT]))
                for f in range(KC):
                    outchunk(f)
                NG = (D + 511) // 512
                for tt in range(n // P):
                    finalize(t0, tt, NG)

            def finalize(t0, tt, NG):
                if True:
                    row0 = t0 + tt * P
                    for g in range(NG):
                        B = PS[2 + g]
                        nq = min(4, KC - 4 * g)
                        for q in range(nq):
                            pg.op("pe", lambda e, B=B, q=q, g=g, tt=tt: e.transpose(
                                out=B[:, q * P:(q + 1) * P], in_=oT[:, 4 * g + q, tt * P:(tt + 1) * P], identity=ident[:, 0:P]), [oT, cst], [B])
                        pg.op("act", lambda e, B=B, g=g, nq=nq: e.activation(out=t2[:, 0:nq * P], in_=B[:, 0:nq * P], func=AF.Square,
                                                                             accum_out=ssq[:, g:g + 1]), [B], [t2, ssq])
                    pg.op("dve", lambda e: e.tensor_reduce(out=rstd[:], in_=ssq[:, 0:NG], axis=mybir.AxisListType.X, op=ALU.add), [ssq], [rstd])
                    pg.op("dve", lambda e: e.tensor_scalar(out=rstd[:], in0=rstd[:], scalar1=1.0 / D, scalar2=EPS, op0=ALU.mult, op1=ALU.add),
                          [rstd], [rstd])
                    pg.op("act", lambda e: e.activation(out=rstd[:], in_=rstd[:], func=AF.Ln), [rstd], [rstd])
                    pg.op("act", lambda e: e.activation(out=rstd[:], in_=rstd[:], func=AF.Exp, scale=-0.5), [rstd], [rstd])
                    for g in range(NG):
                        B = PS[2 + g]
                        w = min(512, D - g * 512)
                        XQ, YQ = xq[xi[0] % 2], yq[xi[0] % 2]
                        xi[0] += 1
                        dma(lambda e, XQ=XQ, g=g, w=w, row0=row0: e.dma_start(out=XQ[:, 0:w], in_=x_own[row0:row0 + P, g * 512:g * 512 + w]), writes=[XQ])
                        pg.op("dve", lambda e, B=B, YQ=YQ, g=g, w=w: e.scalar_tensor_tensor(
                            out=YQ[:, 0:w], in0=B[:, 0:w], scalar=rstd[:, 0:1], in1=prow[:, g * 512:g * 512 + w], op0=ALU.mult, op1=ALU.mult),
                            [B, rstd, prow], [YQ])
                        pg.op("pool", lambda e, XQ=XQ, YQ=YQ, w=w: e.tensor_tensor(out=YQ[:, 0:w], in0=YQ[:, 0:w], in1=XQ[:, 0:w], op=ALU.add),
                              [XQ, YQ], [YQ])
                        dma(lambda e, YQ=YQ, g=g, w=w, row0=row0: e.dma_start(out=y_own[row0:row0 + P, g * 512:g * 512 + w], in_=YQ[:, 0:w]), reads=[YQ])
            for (t0_, n_) in part_ranges:
                part(t0_, n_)
            for j in range(CC):
                B = PS[2 + j % 2]
                pg.op("pe", lambda e, B=B, j=j: e.transpose(out=B[:NCS, 0:P], in_=ucs[:, j, :], identity=ident[:, 0:P]), [ucs, cst], [B])
                pg.op("dve", lambda e, B=B, j=j: e.tensor_copy(out=cstage[:NCS, j, 0:P], in_=B[:NCS, 0:P]), [B], [cstage])
            dma(lambda e: e.dma_start(out=conv_own.rearrange("r (j d) -> r j d", d=P), in_=cstage[:NCS, 0:CC, 0:P]), reads=[cstage])

            with nc.Block() as block:
                pg.emit(nc, block, sems, dsems, [])
    return nc


_CACHE = {}


def make_consts():
    cs = np.zeros((P, 4 * P), np.float32)
    cs[:, 0:P] = np.eye(P, dtype=np.float32)
    j = np.arange(P)[:, None]
    s = np.arange(P)[None, :]
    cs[:, P:2 * P] = -(j >= s).astype(np.float32)
    cs[:, 2 * P:3 * P] = -1.0
    cs[:, 3 * P:4 * P] = (s > j).astype(np.float32)
    return cs


def make_consts_full(GS):
    cs = np.zeros((P, 4 * P + GS * 4 + 1), np.float32)
    cs[:, -1] = np.arange(P, dtype=np.float32)
    cs[:, :4 * P] = make_consts()
    for bi in range(GS):
        for t in range(4):
            cs[:t, 4 * P + bi * 4 + t] = 1.0
    return cs


def run(cfg, x_prompt, x_sample, cache_k, cache_v, state_conv, page_table, meta_tokens,
        g_pre, w_in, b_sb, b_gate, w_conv, b_conv, w_pa, w_pc, w_out, g_post):
    c = cfg
    f32 = lambda a: np.ascontiguousarray(np.asarray(a, dtype=np.float32))
    x_prompt, x_sample, state_conv, meta_tokens = map(f32, (x_prompt, x_sample, state_conv, meta_tokens))
    g_pre, w_in, b_sb, b_gate, w_conv, b_conv, w_pa, w_pc, w_out, g_post = map(
        f32, (g_pre, w_in, b_sb, b_gate, w_conv, b_conv, w_pa, w_pc, w_out, g_post))
    cache_k = np.asarray(cache_k, dtype=np.float32)
    cache_v = np.asarray(cache_v, dtype=np.float32)
    page_table = np.ascontiguousarray(np.asarray(page_table, dtype=np.int32))
    key = (1, c.D, c.SEQ, c.NB, c.NCORES, c.NPG, c.NPOOL)
    if key not in _CACHE:
        _CACHE[key] = build(c)
    nc = _CACHE[key]
    consts = make_consts_full(c.GS)
    B = c.NCORES
    ckv = np.concatenate([cache_k[0].transpose(0, 2, 3, 1), cache_v[0].transpose(0, 2, 1, 3)], axis=3)
    ckv = np.ascontiguousarray(ckv.reshape(c.NPOOL, c.H // 2, 2, P, 2 * P).transpose(0, 1, 3, 2, 4)).reshape(c.NPOOL * (c.H // 2) * P, 4 * P)
    del cache_k, cache_v

    def tile_w(w):
        k, n = w.shape
        return np.ascontiguousarray(w.reshape(k // P, P, n // P, P).transpose(2, 1, 0, 3)).reshape(n, k)
    w_in_t, w_pa_t, w_pc_t, w_out_t = tile_w(w_in[0]), tile_w(w_pa[0]), tile_w(w_pc[0]), tile_w(w_out[0])

    in_maps = []
    for core in range(B):
        xo = np.zeros((c.TTP, c.D), np.float32)
        xo[0:NMETA] = meta_tokens
        xo[NMETA:c.TP] = x_prompt[core]
        xo[c.TP:c.TT] = x_sample[core * c.NSEQ:(core + 1) * c.NSEQ].reshape(c.NSO, c.D)
        pcol = np.zeros((P, c.NPC), np.float32)
        pcol[:, c.o_gpre:c.o_gpre + c.KC] = g_pre[0].reshape(c.KC, P).T
        pcol[:, c.o_bg:c.o_bg + 2 * c.KC] = b_gate[0].reshape(2 * c.KC, P).T
        pcol[:, c.o_wc:c.o_wc + 3 * c.CC] = w_conv[0].reshape(3, c.CC, P).transpose(2, 1, 0).reshape(P, 3 * c.CC)
        pcol[:, c.o_bc:c.o_bc + c.CC] = b_conv[0].reshape(c.CC, P).T
        st = state_conv[0, core * c.NSEQ:(core + 1) * c.NSEQ]
        pcol[:, c.o_st:c.o_st + c.CC * c.NSEQ * 2] = st.reshape(c.NSEQ, 2, c.CC, P).transpose(3, 2, 0, 1).reshape(P, -1)
        prow = np.zeros((P, c.D + c.H + 1), np.float32)
        prow[:, :c.D] = g_post[0][None, :]
        prow[:, c.D:c.D + c.H] = b_sb[0][None, :]
        pt = np.ascontiguousarray(np.broadcast_to(
            page_table[core * c.NSEQ:(core + 1) * c.NSEQ].reshape(1, c.NSEQ * c.NPG), (P, c.NSEQ * c.NPG)))
        in_maps.append({"x_own": xo, "w_in": w_in_t, "w_pa": w_pa_t, "w_pc": w_pc_t, "w_out": w_out_t,
                        "pcol": pcol, "prow": prow, "consts": consts, "ckv": ckv, "pt": pt})
    res = run_bass_kernel_spmd(nc, in_maps, core_ids=list(range(B)))
    R = res.results
    del in_maps, ckv
    y_prompt = np.stack([R[i]["y_own"][NMETA:c.TP] for i in range(B)])
    y_sample = np.concatenate([R[i]["y_own"][c.TP:c.TT].reshape(c.NSEQ, c.DEC_SEQ, c.D) for i in range(B)])
    kp = np.stack([R[i]["k_own"][:c.TP].reshape(c.TP, c.H, P) for i in range(B)])[None]
    vp = np.stack([R[i]["v_own"][:c.TP].reshape(c.TP, c.H, P) for i in range(B)])[None]
    cp = np.stack([R[i]["conv_own"][0:2] for i in range(B)])[None]
    ks = np.concatenate([R[i]["k_own"][c.TP:c.TT].reshape(c.NSEQ, c.DEC_SEQ, c.H, P) for i in range(B)])[None]
    vs = np.concatenate([R[i]["v_own"][c.TP:c.TT].reshape(c.NSEQ, c.DEC_SEQ, c.H, P) for i in range(B)])[None]
    cs = np.concatenate([R[i]["conv_own"][2:].reshape(c.NSEQ, 2, c.DC) for i in range(B)])[None]
    return (y_prompt, y_sample, kp, vp, cp, ks, vs, cs)


def kernel(**inputs):
    return run(Cfg(), **inputs)
```

```python
import numpy as np
import concourse.bass as bass
import concourse.mybir as mybir
from concourse.bass_utils import run_bass_kernel_spmd

F32, BF16, I32 = mybir.dt.float32, mybir.dt.bfloat16, mybir.dt.int32
AF = mybir.ActivationFunctionType
import os as _os
SILU = AF.Copy if _os.environ.get('KSILU', '') == '0' else AF.Silu
ALU = mybir.AluOpType
ET = mybir.EngineType

P = 128
NMETA = 16
EPS = 1e-6


class Cfg:
    def __init__(self, D=2048, SEQ=2048, DEC_BATCH=128, DEC_SEQ=4, NCORES=8, PAST=2048, NPOOL=2560, mode=1):
        self.mode = mode
        self.NB = DEC_BATCH
        self.NPG = PAST // P
        self.NPOOL = NPOOL
        self.GS = 1
        self.NSA = DEC_BATCH * DEC_SEQ
        self.D = D
        self.KC = D // P
        self.DA = D // 2
        self.H = self.DA // P
        self.DC = D // 2
        self.CC = self.DC // P
        self.SEQ = SEQ
        self.TP = (NMETA + SEQ) if mode == 1 else 0
        self.NCORES = NCORES
        self.NSEQ = DEC_BATCH // NCORES
        self.DEC_SEQ = DEC_SEQ
        self.NSO = self.NSEQ * DEC_SEQ
        self.TT = self.TP + self.NSO
        self.NT = (self.TT + P - 1) // P
        self.TTP = self.NT * P
        self.INC = 4 * self.DA + 4 * self.DC + 2 * D
        self.NCS = 2 + 2 * self.NSEQ
        o = 0
        self.o_gpre = o; o += self.KC
        self.o_bg = o; o += 2 * self.KC
        self.o_wc = o; o += 3 * self.CC
        self.o_bc = o; o += self.CC
        self.o_st = o; o += self.CC * self.NSEQ * 2
        self.NPC = o


class Tile:
    def __init__(self, ap, name):
        self.ap = ap
        self.name = name
        self.w = None
        self.r = {}

    def __getitem__(self, k):
        return self.ap[k]


class Prog:
    ENG = ["sp", "act", "dve", "pool", "pe"]
    NDMASEM = 16

    def __init__(self):
        self.ops = {e: [] for e in self.ENG}
        self.needed = set()

    cut = False

    def barrier(self):
        tk = set()
        for e in self.ENG:
            ops = self.ops[e]
            last_c = [i for i in range(len(ops)) if not ops[i][2]][-1:]
            last_d = [i for i in range(len(ops)) if ops[i][2]][-self.NDMASEM:]
            for i in last_c + last_d:
                tk.add((e, i))
        self.pending = {e: set(tk) for e in self.ENG}

    pending = {}

    def op(self, eng, fn, reads=(), writes=(), dma=False):
        if self.cut:
            return None
        deps = set(self.pending.pop(eng, ()))
        for t in reads:
            if t.w is not None:
                deps.add(t.w)
        for t in writes:
            if t.w is not None:
                deps.add(t.w)
            for tk in t.r.values():
                deps.add(tk)
        tk = (eng, len(self.ops[eng]))
        if eng == "pe":
            deps = {d for d in deps if d[0] != "pe"}
        deps.discard(tk)
        self.ops[eng].append((fn, deps, dma))
        self.needed |= deps
        for t in reads:
            t.r[eng] = tk
        for t in writes:
            t.w = tk
            t.r = {}
        return tk

    def emit(self, nc, block, sems, dsems, final_waits):
        val = {}
        cnt = {e: 0 for e in self.ENG}
        dcnt = {}
        dma_prev = {}
        for e in self.ENG:
            ndma = 0
            for i, (fn, deps, dma) in enumerate(self.ops[e]):
                if dma:
                    s = (e, ndma % self.NDMASEM)
                    dcnt[s] = dcnt.get(s, 0) + 16
                    val[(e, i)] = ("d", s, dcnt[s])
                    dma_prev[(e, i)] = ("d", s, dcnt[s] - 16) if dcnt[s] > 16 else None
                    ndma += 1
                elif (e, i) in self.needed:
                    cnt[e] += 1
                    val[(e, i)] = ("e", e, cnt[e])
        all_dma = [(e, i) for e in self.ENG for i, o in enumerate(self.ops[e]) if o[2]]

        def run(e, engobj):
            waited = {}

            def wait(v):
                if v is None:
                    return
                kind, s, n = v
                key = (kind, s)
                if waited.get(key, 0) >= n:
                    return
                waited[key] = n
                engobj.wait_ge(dsems[s] if kind == "d" else sems[s], n)

            for i, (fn, deps, dma) in enumerate(self.ops[e]):
                for d in sorted(deps):
                    wait(val[d])
                if dma:
                    wait(dma_prev[(e, i)])
                ins = fn(engobj)
                v = val.get((e, i))
                if v is not None:
                    kind, s, n = v
                    ins.then_inc(dsems[s] if kind == "d" else sems[s], 16 if kind == "d" else 1)
            if e == "sp":
                for d in all_dma:
                    wait(val[d])
                for d in final_waits:
                    wait(val[d])

        block.sync(lambda eng: run("sp", eng))
        block.scalar(lambda eng: run("act", eng))
        block.vector(lambda eng: run("dve", eng))
        block.gpsimd(lambda eng: run("pool", eng))
        block.tensor(lambda eng: run("pe", eng))


def build(cfg):
    c = cfg
    D, KC, DA, H, DC, CC, TP, TT, NT, TTP = c.D, c.KC, c.DA, c.H, c.DC, c.CC, c.TP, c.TT, c.NT, c.TTP
    NSEQ, NCS = c.NSEQ, c.NCS
    nc = bass.Bass("TRN2", target_bir_lowering=False)
    dt = lambda n, s, k="ExternalInput": nc.dram_tensor(n, s, F32, kind=k).ap()
    x_own = dt("x_own", [TTP, D])
    w_in = dt("w_in", [c.INC, D])
    w_pa = dt("w_pa", [D, DA])
    w_pc = dt("w_pc", [D, DC])
    w_out = dt("w_out", [D, D])
    pcol_d = dt("pcol", [P, c.NPC])
    prow_d = dt("prow", [P, D + H + 1])
    NCST = 4 * P + c.GS * 4 + 1
    consts_d = dt("consts", [P, NCST])
    if c.mode == 1:
        ckv = dt("ckv", [c.NPOOL * (H // 2) * P, 4 * P])
        pt_d = nc.dram_tensor("pt", [P, c.NSEQ * c.NPG], I32, kind="ExternalInput").ap()
    else:
        a_in = dt("a_in", [P, H * P])
    y_own = dt("y_own", [TTP, D], "ExternalOutput")
    k_own = dt("k_own", [TTP, DA], "ExternalOutput")
    v_own = dt("v_own", [TTP, DA], "ExternalOutput")
    conv_own = dt("conv_own", [NCS, DC], "ExternalOutput")

    pg = Prog()
    import contextlib
    es = contextlib.ExitStack()

    def sb(name, shape, dtype=F32):
        return Tile(es.enter_context(nc.sbuf_tensor("s_" + name, shape, dtype)), name)

    def ps(name):
        return Tile(es.enter_context(nc.psum_tensor(name, [P, 512], F32)), name)

    with es:
        sems = {e: es.enter_context(nc.semaphore("sem_" + e)) for e in Prog.ENG}
        dsems = {(e_, i): es.enter_context(nc.semaphore("dsem_%s%d" % (e_, i))) for e_ in ("sp", "pool") for i in range(Prog.NDMASEM)}
        pcol = sb("pcol", [P, c.NPC])
        prow = sb("prow", [P, D + H + 1])
        cst = sb("cst", [P, NCST])
        ident = cst
        cbf = sb("cbf", [P, NCST - P], BF16)
        ones1 = sb("ones1", [P, 1])
        NWB = 4
        wbf = [sb("wbf%d" % i, [P, KC, P], BF16) for i in range(NWB)]
        wst = [sb("wst0", [P, 1, P])]
        PS = [ps("ps%d" % i) for i in range(8)]
        ucs = sb("ucs", [P, CC, NCS])
        uh = sb("uh", [P, CC, 2])

        dma = lambda fn, reads=(), writes=(): pg.op("sp", fn, reads, writes, dma=True)

        dma(lambda e: e.dma_start(out=pcol[:], in_=pcol_d), writes=[pcol])
        dma(lambda e: e.dma_start(out=prow[:], in_=prow_d), writes=[prow])
        dma(lambda e: e.dma_start(out=cst[:], in_=consts_d), writes=[cst])
        pg.op("pool", lambda e: e.tensor_copy(out=cbf[:], in_=cst[:, P:NCST]), [cst], [cbf])
        pg.op("pool", lambda e: e.memset(ones1[:], 1.0), [], [ones1])
        pg.op("pool", lambda e: e.memset(uh[:], 0.0), [], [uh])
        negtri = lambda k, m: cbf[:k, 0:m]
        negones = lambda k, m: cbf[:k, P:P + m]
        masku = lambda k, m: cbf[:k, 2 * P:2 * P + m]
        maskp = lambda: cbf[:, 3 * P:3 * P + c.GS * 4]
        import os

        def stage_a(x_src, ntiles, hT, tag):
          with contextlib.ExitStack() as sa:
            sbl = lambda name, shape, dtype=F32: Tile(sa.enter_context(nc.sbuf_tensor("s_" + tag + name, shape, dtype)), name)
            xt = [sbl("xt%d" % i, [P, D]) for i in range(2)]
            xn = [sbl("xn%d" % i, [P, D]) for i in range(2)]
            junk = sbl("junk", [P, D], BF16)
            ss = [sbl("ss%d" % i, [P, 1]) for i in range(2)]
            rs = [sbl("rs%d" % i, [P, 1]) for i in range(2)]
            for i in range(ntiles):
                X, XN, SS, RS = xt[i % 2], xn[i % 2], ss[i % 2], rs[i % 2]
                dma(lambda e, X=X, i=i: e.dma_start(out=X[:], in_=x_src[i * P:(i + 1) * P, :]), writes=[X])
                pg.op("act", lambda e, X=X, SS=SS: e.activation(out=junk[:], in_=X[:], func=AF.Square, accum_out=SS[:]),
                      [X], [junk, SS])
                pg.op("dve", lambda e, SS=SS, RS=RS: e.tensor_scalar(out=RS[:], in0=SS[:], scalar1=1.0 / D, scalar2=EPS,
                                                                      op0=ALU.mult, op1=ALU.add), [SS], [RS])
                pg.op("act", lambda e, RS=RS: e.activation(out=RS[:], in_=RS[:], func=AF.Ln), [RS], [RS])
                pg.op("act", lambda e, RS=RS: e.activation(out=RS[:], in_=RS[:], func=AF.Exp, scale=-0.5), [RS], [RS])
                pg.op("act", lambda e, X=X, XN=XN, RS=RS: e.activation(out=XN[:], in_=X[:], func=AF.Copy, scale=RS[:, 0:1]),
                      [X, RS], [XN])
                for g in range((KC + 3) // 4):
                    B = PS[g % 2]
                    nq = min(4, KC - 4 * g)
                    for q in range(nq):
                        kc = 4 * g + q
                        pg.op("pe", lambda e, B=B, XN=XN, q=q, kc=kc: e.transpose(
                            out=B[:, q * P:(q + 1) * P], in_=XN[:, kc * P:(kc + 1) * P], identity=ident[:, 0:P]),
                            [XN, cst], [B])
                    for q in range(nq):
                        kc = 4 * g + q
                        pg.op("dve", lambda e, B=B, q=q, kc=kc, i=i: e.tensor_scalar(
                            out=hT[:, kc, i * P:(i + 1) * P], in0=B[:, q * P:(q + 1) * P],
                            scalar1=pcol[:, c.o_gpre + kc:c.o_gpre + kc + 1], scalar2=None, op0=ALU.mult),
                            [B, pcol], [hT])
          pg.barrier()

        slab_i = [0]
        pp_i = [0]

        def proj(wd, kcx, col0, act, act_c0, ranges, evac, pbanks=(0, 1)):
            k = slab_i[0]
            slab_i[0] += 1
            WB = wbf[k % NWB]
            pg.op("pool", lambda e: e.dma_start(out=WB[:, 0:kcx, :], in_=wd[col0:col0 + P, :].rearrange("p (kc c) -> p kc c", c=P)),
                  [], [WB], dma=True)
            for (t0, n) in ranges:
                B = PS[pbanks[pp_i[0] % len(pbanks)]]
                pp_i[0] += 1
                for kc in range(kcx):
                    pg.op("pe", lambda e, B=B, kc=kc, t0=t0, n=n: e.matmul(
                        B[:, 0:n], lhsT=WB[:, kc, :], rhs=act[:, kc, act_c0 + t0:act_c0 + t0 + n],
                        start=(kc == 0), stop=(kc == kcx - 1)), [WB, act], [B])
                evac(B, t0, n)

        def phase_s():
          NPG, GS = c.NPG, 1
          NSO, NSQ = c.NSO, c.NSEQ
          NPP = NPG + 1
          G4 = GS * 4
          W = NPP * G4
          NPT = NSQ * NPG
          HP = H // 2
          with contextlib.ExitStack() as sa:
            sbl = lambda name, shape, dtype=F32: Tile(sa.enter_context(nc.sbuf_tensor("s_ps_" + name, shape, dtype)), name)
            pts = Tile(sa.enter_context(nc.sbuf_tensor("s_ps_pts", [P, NPT], I32)), "pts")
            ptf = sbl("ptf", [P, NPT])
            idxf = sbl("idxf", [P, NPT])
            idxh = [Tile(sa.enter_context(nc.sbuf_tensor("s_ps_idx%d" % i, [P, NPT], I32)), "idx") for i in range(2)]
            kvb = [sbl("kvb%d" % i, [P, NPG, 4 * P], BF16) for i in range(2)]
            kvslot = [[Tile(kvb[i].ap[:, sl, :], "kvs") for sl in range(NPG)] for i in range(2)]
            kpad = [sbl("kpad%d" % i, [P, GS, P], BF16) for i in range(2)]
            vpad = [sbl("vpad%d" % i, [P, GS, P], BF16) for i in range(2)]
            zsb = sbl("zsb", [P, W])
            E = sbl("E", [P, W])
            lg = sbl("lg", [P, W])
            L = sbl("L", [P, W], BF16)
            AT = sbl("AT", [P, W], BF16)
            LKb = sbl("LKb", [P, W], BF16)
            LK32 = sbl("LK32", [P, W])
            dma(lambda e: e.dma_start(out=pts[:], in_=pt_d), writes=[pts])
            pg.op("dve", lambda e: e.tensor_copy(out=ptf[:], in_=pts[:]), [pts], [ptf])
            pg.op("dve", lambda e: e.tensor_scalar(out=ptf[:], in0=ptf[:], scalar1=float(HP * P), scalar2=cst[:, NCST - 1:NCST],
                                                   op0=ALU.mult, op1=ALU.add), [ptf, cst], [ptf])
            for i in range(2):
                pg.op("pool", lambda e, i=i: e.memset(kpad[i][:], 0.0), [], [kpad[i]])
                pg.op("pool", lambda e, i=i: e.memset(vpad[i][:], 0.0), [], [vpad[i]])
            pg.op("pool", lambda e: e.memset(LK32[:], 0.0), [], [LK32])
            units = [(hp, b) for hp in range(HP) for b in range(NSQ)]

            def head_idx(hp):
                IX = idxh[hp % 2]
                pg.op("dve", lambda e: e.tensor_scalar(out=idxf[:], in0=ptf[:], scalar1=float(hp * P), scalar2=None, op0=ALU.add),
                      [ptf], [idxf])
                pg.op("dve", lambda e: e.tensor_copy(out=IX[:], in_=idxf[:]), [idxf], [IX])

            def loads(u):
                hp, b = units[u]
                if b == 0:
                    head_idx(hp)
                KV, KVL, IX = kvb[u % 2], kvslot[u % 2], idxh[hp % 2]
                for p in range(NPG):
                    page_load(KV, KVL, IX, p, b * NPG + p)

            def page_load(KV, KVL, IX, slot, j):
                pg.op("pool", lambda e: e.indirect_dma_start(
                    out=KV[:, slot, :], out_offset=None, in_=ckv,
                    in_offset=bass.IndirectOffsetOnAxis(ap=IX[:, j:j + 1], axis=0)), [IX], [KVL[slot]], dma=True)

            cnt = [0]

            def unit(u):
                hp, b = units[u]
                if u == 0:
                    loads(0)
                if u + 1 < len(units):
                    loads(u + 1)
                for hh in range(2):
                    group(u, 2 * hp + hh, hh, b)

            def group(u, h, hh, b):
                i2 = cnt[0] % 2
                cnt[0] += 1
                KP, VP = kpad[i2], vpad[i2]
                KV, KVL = kvb[u % 2], kvslot[u % 2]
                ko, vo = hh * 2 * P, hh * 2 * P + P
                AB, CB, OB, VT = PS[2 + i2], PS[4 + i2], PS[6 + i2], PS[0]
                bias_c = lambda: prow[:, D + h:D + h + 1]
                pg.op("dve", lambda e: e.tensor_copy(out=KP[:, 0, 0:4], in_=kS[:, h, 4 * b:4 * b + 4]), [kS], [KP])
                pg.op("pe", lambda e: e.transpose(out=VT[0:4, 0:P], in_=vS32[:, h, 4 * b:4 * b + 4], identity=ident[:, 0:P]), [vS32, cst], [VT])
                pg.op("dve", lambda e: e.tensor_copy(out=VP[0:4, 0, :], in_=VT[0:4, 0:P]), [VT], [VP])
                for p in range(NPP):
                    zmm(AB, KP, KV, KVL, ko, p, p * 4, h, b)
                pg.op("dve", lambda e: e.tensor_copy(out=zsb[:], in_=AB[:, 0:W]), [AB], [zsb])
                pg.op("act", lambda e: e.activation(out=E[:], in_=zsb[:], func=AF.Exp, bias=bias_c()), [zsb, prow], [E])
                pg.op("act", lambda e: e.activation(out=L[:], in_=E[:], func=AF.Ln, bias=ones1[:, 0:1]), [E, ones1], [L])
                pg.op("dve", lambda e: e.tensor_tensor(out=L[:, NPG * G4:W], in0=L[:, NPG * G4:W], in1=maskp(), op=ALU.mult), [L, cbf], [L])
                for p in range(NPG - 1, -1, -1):
                    pg.op("dve", lambda e, p=p: e.tensor_tensor(out=LK32[:, p * G4:(p + 1) * G4], in0=LK32[:, (p + 1) * G4:(p + 2) * G4],
                                                                in1=L[:, (p + 1) * G4:(p + 2) * G4], op=ALU.add), [LK32, L], [LK32])
                pg.op("dve", lambda e: e.tensor_copy(out=LKb[:], in_=LK32[:]), [LK32], [LKb])
                pg.op("pe", lambda e: e.matmul(CB[:, 0:W], lhsT=negtri(P, P), rhs=L[:], start=True, stop=False), [L, cbf], [CB])
                pg.op("pe", lambda e: e.matmul(CB[:, 0:W], lhsT=negones(P, P), rhs=LKb[:], start=False, stop=True), [LKb, cbf], [CB])
                pg.op("dve", lambda e: e.tensor_tensor(out=lg[:], in0=CB[:, 0:W], in1=zsb[:], op=ALU.add), [CB, zsb], [lg])
                pg.op("act", lambda e: e.activation(out=AT[:], in_=lg[:], func=AF.Exp, bias=bias_c()), [lg, prow], [AT])
                pg.op("dve", lambda e: e.tensor_tensor(out=AT[:, NPG * G4:W], in0=AT[:, NPG * G4:W], in1=maskp(), op=ALU.mult), [AT, cbf], [AT])
                for p in range(NPP):
                    avmm(OB, VP, KV, KVL, vo, p, p * 4)
                pg.op("dve", lambda e: e.tensor_tensor(out=a_t[:, h, TP + 4 * b:TP + 4 * b + 4], in0=OB[:, 0:4], in1=sgS[:, h, 4 * b:4 * b + 4],
                                                       op=ALU.mult), [OB, sgS], [a_t])

            def zmm(AB, KP, KV, KVL, ko, p, col, h, b):
                if p < NPG:
                    pg.op("pe", lambda e: e.matmul(AB[:, col:col + 4], lhsT=KV[:, p, ko:ko + P], rhs=qS[:, h, 4 * b:4 * b + 4],
                                                   start=True, stop=True), [KVL[p], qS], [AB])
                else:
                    pg.op("pe", lambda e: e.matmul(AB[:, col:col + 4], lhsT=KP[:, 0, :], rhs=qS[:, h, 4 * b:4 * b + 4],
                                                   start=True, stop=True), [KP, qS], [AB])

            def avmm(OB, VP, KV, KVL, vo, p, col):
                if p < NPG:
                    pg.op("pe", lambda e: e.matmul(OB[:, 0:4], lhsT=KV[:, p, vo:vo + P], rhs=AT[:, col:col + 4],
                                                   start=(p == 0), stop=False, skip_group_check=True), [KVL[p], AT], [OB])
                else:
                    pg.op("pe", lambda e: e.matmul(OB[:, 0:4], lhsT=VP[:, 0, :], rhs=AT[:, col:col + 4],
                                                   start=False, stop=True, skip_group_check=True), [VP, AT], [OB])

            for u in range(len(units)):
                unit(u)
          pg.barrier()

        full_ranges = [(t, min(512, TTP - t)) for t in range(0, TTP, 512)]

        hT = sb("hT", [P, KC, TTP], BF16)
        a_t = sb("a_t", [P, H, TTP], BF16)
        qS = sb("qS", [P, H, c.NSO], BF16)
        kS = sb("kS", [P, H, c.NSO], BF16)
        vS32 = sb("vS32", [P, H, c.NSO])
        sgS = sb("sgS", [P, H, c.NSO])
        if c.mode == 1:
            pg.op("pool", lambda e: e.memset(a_t[:], 0.0), [], [a_t])
        else:
            for hh in range(H):
                ST = wst[0]
                dma(lambda e, ST=ST, hh=hh: e.dma_start(out=ST[:, 0, :], in_=a_in[:, hh * P:(hh + 1) * P]), writes=[ST])
                pg.op("pool", lambda e, ST=ST, hh=hh: e.tensor_copy(out=a_t[:, hh, 0:P], in_=ST[:, 0, :]), [ST], [a_t])
        stage_a(x_own, NT, hT, "a")
        if os.environ.get("KSTOP", "") == "A":
            pg.cut = True

        with contextlib.ExitStack() as sa:
            sbl = lambda name, shape, dtype=F32: Tile(sa.enter_context(nc.sbuf_tensor("s_" + name, shape, dtype)), name)
            qT = sbl("qT", [P, TTP], BF16)
            kT = sbl("kT", [P, TTP], BF16)
            vtok = sbl("vtok", [P, NT, P], BF16)
            sg = sbl("sg", [P, TTP], BF16)
            kv32 = [sbl("kv32_%d" % i, [P, 512]) for i in range(2)]
            tokst = [sbl("tokst%d" % i, [P, 4, P]) for i in range(2)]
            e32 = [sbl("e32_%d" % i, [P, 512]) for i in range(2)]
            lkp = [sbl("lkp%d" % i, [P, 512], BF16) for i in range(2)]
            lks = [sbl("lks%d" % i, [P, 512], BF16) for i in range(2)]
            att = [sbl("att%d" % i, [P, 512], BF16) for i in range(2)]
            kvi = [0]
            scale = float(P) ** -0.5

            def kv_out(B, t0, n, out_d, h, isv):
                i = kvi[0] % 2
                kvi[0] += 1
                K32, TS, PT = kv32[i], tokst[i], PS[2 + i]
                pg.op("act", lambda e: e.activation(out=K32[:, 0:n], in_=B[:, 0:n], func=AF.Copy), [B], [K32])
                nb = n // P
                if os.environ.get("KKV", "") == "min":
                    return K32
                for j in range(nb):
                    pg.op("pe", lambda e, j=j: e.transpose(out=PT[:, j * P:(j + 1) * P], in_=K32[:, j * P:(j + 1) * P],
                                                           identity=ident[:, 0:P]), [K32, cst], [PT])
                pg.op("dve", lambda e: e.tensor_copy(out=TS[:, 0:nb, :], in_=PT[:, 0:nb * P].rearrange("p (j d) -> p j d", d=P)),
                      [PT], [TS])
                if isv:
                    pg.op("pool", lambda e: e.tensor_copy(out=vtok[:, t0 // P:t0 // P + nb, :], in_=TS[:, 0:nb, :]), [TS], [vtok])
                if os.environ.get("KKV", "") != "noout":
                    dma(lambda e: e.dma_start(out=out_d[t0:t0 + n, h * P:(h + 1) * P].rearrange("(j p) d -> p j d", p=P),
                                              in_=TS[:, 0:nb, :]), reads=[TS])
                return K32

            def head(h):
                def ev_q(B, t0, n):
                    pg.op("act", lambda e: e.activation(out=qT[:, t0:t0 + n], in_=B[:, 0:n], func=AF.Copy, scale=scale), [B], [qT])
                proj(w_in, KC, h * P, hT, 0, full_ranges, ev_q)
                if os.environ.get("KSTOP", "") == "P0":
                    pg.cut = True

                def ev_k(B, t0, n, h=h):
                    K32 = kv_out(B, t0, n, k_own, h, False)
                    pg.op("dve", lambda e: e.tensor_copy(out=kT[:, t0:t0 + n], in_=K32[:, 0:n]), [K32], [kT])
                proj(w_in, KC, DA + h * P, hT, 0, full_ranges, ev_k)

                def ev_v(B, t0, n, h=h):
                    K32 = kv_out(B, t0, n, v_own, h, True)
                    if t0 <= TP and TT <= t0 + n:
                        pg.op("dve", lambda e: e.tensor_copy(out=vS32[:, h, :], in_=K32[:, TP - t0:TT - t0]), [K32], [vS32])
                proj(w_in, KC, 2 * DA + h * P, hT, 0, full_ranges, ev_v)

                def ev_g(B, t0, n):
                    pg.op("act", lambda e: e.activation(out=sg[:, t0:t0 + n], in_=B[:, 0:n], func=SILU), [B], [sg])
                proj(w_in, KC, 3 * DA + h * P, hT, 0, full_ranges, ev_g)

                pg.op("dve", lambda e: e.tensor_copy(out=qS[:, h, :], in_=qT[:, TP:TT]), [qT], [qS])
                pg.op("dve", lambda e: e.tensor_copy(out=kS[:, h, :], in_=kT[:, TP:TT]), [kT], [kS])
                pg.op("dve", lambda e: e.tensor_copy(out=sgS[:, h, :], in_=sg[:, TP:TT]), [sg], [sgS])
                bias_h = lambda k, h=h: prow[:k, D + h:D + h + 1]
                if os.environ.get("KSTOP", "") == "P1":
                    pg.cut = True
                for qi, q0 in enumerate(range(0, TP, 512)):
                    qtile(h, bias_h, qi, q0)

            def qtile(h, bias_h, qi, q0):
                if True:
                    qn = min(512, TP - q0)
                    OB = PS[6 + qi % 2]
                    pg.op("pool", lambda e: e.memset(lks[0][:], 0.0), [], [lks[0]])
                    pg.op("pool", lambda e: e.memset(lks[1][:], 0.0), [], [lks[1]])
                    S_hi = (q0 + qn - 1) // P
                    for st, S in enumerate(range(S_hi, -1, -1)):
                        step(h, bias_h, q0, qn, OB, st, S)
                    pg.op("dve", lambda e: e.tensor_tensor(
                        out=a_t[:, h, q0:q0 + qn], in0=OB[:, 0:qn], in1=sg[:, q0:q0 + qn], op=ALU.mult), [OB, sg], [a_t])

            def step(h, bias_h, q0, qn, OB, st, S):
                if True:
                    if True:
                        li = st
                        nk = min(P, TP - S * P)
                        c0 = max(q0, S * P) - q0
                        diag = S * P >= q0
                        nd = min(P, qn - c0)
                        AB = PS[4 + st % 2]
                        E, L, AT = e32[st % 2], lkp[st % 2], att[st % 2]
                        LO, LN = lks[li % 2], lks[(li + 1) % 2]
                        first, last = (st == 0), (S == 0)
                        pg.op("pe", lambda e, AB=AB, S=S, nk=nk, c0=c0, qn=qn, q0=q0: e.matmul(
                            AB[:nk, c0:qn], lhsT=kT[:, S * P:S * P + nk], rhs=qT[:, q0 + c0:q0 + qn], start=True, stop=False),
                            [kT, qT], [AB])
                        pg.op("act", lambda e, AB=AB, E=E, nk=nk, c0=c0, qn=qn: e.activation(
                            out=E[:nk, c0:qn], in_=AB[:nk, c0:qn], func=AF.Exp, bias=bias_h(nk)), [AB, prow], [E])
                        pg.op("act", lambda e, L=L, E=E, nk=nk, c0=c0, qn=qn: e.activation(
                            out=L[:nk, c0:qn], in_=E[:nk, c0:qn], func=AF.Ln, bias=ones1[:nk, 0:1]), [E, ones1], [L])
                        if diag:
                            pg.op("pool", lambda e, L=L, nk=nk, c0=c0, nd=nd: e.tensor_tensor(
                                out=L[:nk, c0:c0 + nd], in0=L[:nk, c0:c0 + nd], in1=masku(nk, nd), op=ALU.mult), [L, cbf], [L])
                        pg.op("pe", lambda e, AB=AB, L=L, nk=nk, c0=c0, qn=qn, first=first: e.matmul(
                            AB[:nk, c0:qn], lhsT=negtri(nk, nk), rhs=L[:nk, c0:qn], start=False, stop=first,
                            skip_group_check=True), [L, cbf], [AB])
                        if not first:
                            pg.op("pe", lambda e, AB=AB, LO=LO, nk=nk, c0=c0, qn=qn: e.matmul(
                                AB[:nk, c0:qn], lhsT=negones(P, nk), rhs=LO[:, c0:qn], start=False, stop=True,
                                skip_group_check=True), [LO, cbf], [AB])
                        if not last:
                            pg.op("dve", lambda e, LO=LO, LN=LN, L=L, nk=nk, c0=c0, qn=qn: e.tensor_tensor(
                                out=LN[:nk, c0:qn], in0=LO[:nk, c0:qn], in1=L[:nk, c0:qn], op=ALU.add), [LO, L], [LN])
                        pg.op("act", lambda e, AB=AB, AT=AT, nk=nk, c0=c0, qn=qn: e.activation(
                            out=AT[:nk, c0:qn], in_=AB[:nk, c0:qn], func=AF.Exp, bias=bias_h(nk)), [AB, prow], [AT])
                        if diag:
                            pg.op("pool", lambda e, AT=AT, nk=nk, c0=c0, nd=nd: e.tensor_tensor(
                                out=AT[:nk, c0:c0 + nd], in0=AT[:nk, c0:c0 + nd], in1=masku(nk, nd), op=ALU.mult), [AT, cbf], [AT])
                        pg.op("pe", lambda e, OB=OB, AT=AT, S=S, nk=nk, c0=c0, qn=qn, first=first, last=last: e.matmul(
                            OB[:, c0:qn], lhsT=vtok[:nk, S, :], rhs=AT[:nk, c0:qn], start=first, stop=last,
                            skip_group_check=True), [vtok, AT], [OB])
            if c.mode == 1:
                for h in range(H):
                    head(h)
        pg.barrier()
        if c.mode == 1 and os.environ.get("KSTOP", "") != "P":
            phase_s()
        if os.environ.get("KSTOP", "") == "S":
            pg.cut = True

        if os.environ.get("KSTOP", "") == "P":
            pg.cut = True
        with contextlib.ExitStack() as sa:
            sbl = lambda name, shape, dtype=F32: Tile(sa.enter_context(nc.sbuf_tensor("s_" + name, shape, dtype)), name)
            cpart = sbl("cpart", [P, CC, 384], BF16)
            merged = sbl("merged", [P, KC, 384], BF16)
            oT = sbl("oT", [P, KC, 384])
            cb32 = sbl("cb32", [P, 384])
            cc32 = sbl("cc32", [P, 384])
            uext = sbl("uext", [P, 386])
            sgc = sbl("sgc", [P, 384])
            yc = sbl("yc", [P, 384])
            ues = sbl("ues", [P, NSEQ, 6])
            ga32 = cb32
            gc32 = cc32
            t1 = sbl("t1", [P, 384])
            t2 = sbl("t2", [P, 512])
            ssq = sbl("ssq", [P, 8])
            rstd = sbl("rstd", [P, 1])
            xq = [sbl("xq%d" % i, [P, 512]) for i in range(2)]
            yq = [sbl("yq%d" % i, [P, 512]) for i in range(2)]
            cstage = oT
            wc = lambda j, i: pcol[:, c.o_wc + 3 * j + i:c.o_wc + 3 * j + i + 1]
            bc = lambda j: pcol[:, c.o_bc + j:c.o_bc + j + 1]
            xi = [0]
            PART = 384
            part_ranges = [(t, min(PART, TTP - t)) for t in range(0, TTP, PART)]

            def part(t0, n):
                rng = [(0, n)]
                np_ = max(0, min(t0 + n, TP) - t0)
                s0, s1 = max(t0, TP) - t0, min(t0 + n, TT) - t0
                has_s = s1 > s0
                if has_s:
                    assert s1 - s0 == c.NSO, "sample tokens must sit inside one part"
                def convchunk(j):
                    base = 4 * DA
                    proj(w_in, KC, base + j * P, hT, t0, rng,
                         lambda B, _t, _n: pg.op("act", lambda e: e.activation(out=cb32[:, 0:n], in_=B[:, 0:n], func=AF.Copy), [B], [cb32]))
                    proj(w_in, KC, base + DC + j * P, hT, t0, rng,
                         lambda B, _t, _n: pg.op("act", lambda e: e.activation(out=cc32[:, 0:n], in_=B[:, 0:n], func=AF.Copy), [B], [cc32]))
                    proj(w_in, KC, base + 2 * DC + j * P, hT, t0, rng,
                         lambda B, _t, _n: pg.op("dve", lambda e: e.tensor_tensor(out=uext[:, 2:2 + n], in0=B[:, 0:n], in1=cc32[:, 0:n], op=ALU.mult),
                                                 [B, cc32], [uext]))
                    proj(w_in, KC, base + 3 * DC + j * P, hT, t0, rng,
                         lambda B, _t, _n: pg.op("act", lambda e: e.activation(out=sgc[:, 0:n], in_=B[:, 0:n], func=SILU), [B], [sgc]))
                    pg.op("dve", lambda e, j=j: e.tensor_copy(out=uext[:, 0:2], in_=uh[:, j, :]), [uh], [uext])
                    pg.op("dve", lambda e, j=j: e.tensor_scalar(out=yc[:, 0:n], in0=uext[:, 2:2 + n], scalar1=wc(j, 2), scalar2=bc(j),
                                                                 op0=ALU.mult, op1=ALU.add), [uext, pcol], [yc])
                    pg.op("dve", lambda e, j=j: e.scalar_tensor_tensor(out=yc[:, 0:n], in0=uext[:, 1:1 + n], scalar=wc(j, 1), in1=yc[:, 0:n],
                                                                        op0=ALU.mult, op1=ALU.add), [uext, pcol, yc], [yc])
                    pg.op("dve", lambda e, j=j: e.scalar_tensor_tensor(out=yc[:, 0:n], in0=uext[:, 0:n], scalar=wc(j, 0), in1=yc[:, 0:n],
                                                                        op0=ALU.mult, op1=ALU.add), [uext, pcol, yc], [yc])
                    if has_s:
                        so = c.o_st + j * NSEQ * 2
                        pg.op("dve", lambda e, so=so: e.tensor_copy(out=ues[:, :, 0:2], in_=pcol[:, so:so + 2 * NSEQ].rearrange("p (b r) -> p b r", r=2)),
                              [pcol], [ues])
                        pg.op("dve", lambda e: e.tensor_copy(out=ues[:, :, 2:6], in_=uext[:, 2 + s0:2 + s1].rearrange("p (b t) -> p b t", t=4)),
                              [uext], [ues])
                        ysv = lambda: yc[:, s0:s1].rearrange("p (b t) -> p b t", t=4)
                        pg.op("dve", lambda e, j=j: e.tensor_scalar(out=ysv(), in0=ues[:, :, 2:6], scalar1=wc(j, 2), scalar2=bc(j),
                                                                     op0=ALU.mult, op1=ALU.add), [ues, pcol], [yc])
                        pg.op("dve", lambda e, j=j: e.scalar_tensor_tensor(out=ysv(), in0=ues[:, :, 1:5], scalar=wc(j, 1), in1=ysv(),
                                                                            op0=ALU.mult, op1=ALU.add), [ues, pcol, yc], [yc])
                        pg.op("dve", lambda e, j=j: e.scalar_tensor_tensor(out=ysv(), in0=ues[:, :, 0:4], scalar=wc(j, 0), in1=ysv(),
                                                                            op0=ALU.mult, op1=ALU.add), [ues, pcol, yc], [yc])
                        pg.op("dve", lambda e, j=j: e.tensor_copy(out=ucs[:, j, 2:NCS].rearrange("p (b r) -> p b r", r=2), in_=ues[:, :, 4:6]),
                              [ues], [ucs])
                    if t0 <= TP - 2 and TP <= t0 + n:
                        pg.op("dve", lambda e, j=j: e.tensor_copy(out=ucs[:, j, 0:2], in_=uext[:, 2 + TP - 2 - t0:2 + TP - t0]), [uext], [ucs])
                    if np_ >= 2:
                        pg.op("dve", lambda e, j=j: e.tensor_copy(out=uh[:, j, :], in_=uext[:, np_:np_ + 2]), [uext], [uh])
                    pg.op("dve", lambda e: e.tensor_tensor(out=t1[:, 0:n], in0=cb32[:, 0:n], in1=yc[:, 0:n], op=ALU.mult), [cb32, yc], [t1])
                    pg.op("dve", lambda e, j=j: e.tensor_tensor(out=cpart[:, j, 0:n], in0=t1[:, 0:n], in1=sgc[:, 0:n], op=ALU.mult),
                          [t1, sgc], [cpart])
                for j in range(CC):
                    convchunk(j)

                def mergechunk(f):
                    base = 4 * DA + 4 * DC
                    bga = pcol[:, c.o_bg + f:c.o_bg + f + 1]
                    bgc = pcol[:, c.o_bg + KC + f:c.o_bg + KC + f + 1]
                    proj(w_in, KC, base + f * P, hT, t0, rng,
                         lambda B, _t, _n, bga=bga: pg.op("act", lambda e: e.activation(out=ga32[:, 0:n], in_=B[:, 0:n], func=AF.Sigmoid, bias=bga),
                                                          [B, pcol], [ga32]))
                    proj(w_in, KC, base + D + f * P, hT, t0, rng,
                         lambda B, _t, _n, bgc=bgc: pg.op("act", lambda e: e.activation(out=gc32[:, 0:n], in_=B[:, 0:n], func=AF.Sigmoid, bias=bgc),
                                                          [B, pcol], [gc32]))
                    proj(w_pa, H, f * P, a_t, t0, rng,
                         lambda B, _t, _n: pg.op("dve", lambda e: e.tensor_tensor(out=t1[:, 0:n], in0=B[:, 0:n], in1=ga32[:, 0:n], op=ALU.mult),
                                                 [B, ga32], [t1]))

                    def ev_pc(B, _t, _n, f=f):
                        pg.op("dve", lambda e: e.tensor_tensor(out=t2[:, 0:n], in0=B[:, 0:n], in1=gc32[:, 0:n], op=ALU.mult), [B, gc32], [t2])
                        pg.op("dve", lambda e: e.tensor_tensor(out=merged[:, f, 0:n], in0=t1[:, 0:n], in1=t2[:, 0:n], op=ALU.add), [t1, t2], [merged])
                    proj(w_pc, CC, f * P, cpart, 0, rng, ev_pc)
                for f in range(KC):
                    mergechunk(f)

                def outchunk(f):
                    proj(w_out, KC, f * P, merged, 0, rng,
                         lambda B, _t, _n: pg.op("act", lambda e: e.activation(out=oT[:, f, 0:n], in_=B[:, 0:n], func=AF.Copy), [B], [oT]))
                for f in range(KC):
                    outchunk(f)
                NG = (D + 511) // 512
                for tt in range(n // P):
                    finalize(t0, tt, NG)

            def finalize(t0, tt, NG):
                if True:
                    row0 = t0 + tt * P
                    for g in range(NG):
                        B = PS[2 + g]
                        nq = min(4, KC - 4 * g)
                        for q in range(nq):
                            pg.op("pe", lambda e, B=B, q=q, g=g, tt=tt: e.transpose(
                                out=B[:, q * P:(q + 1) * P], in_=oT[:, 4 * g + q, tt * P:(tt + 1) * P], identity=ident[:, 0:P]), [oT, cst], [B])
                        pg.op("act", lambda e, B=B, g=g, nq=nq: e.activation(out=t2[:, 0:nq * P], in_=B[:, 0:nq * P], func=AF.Square,
                                                                             accum_out=ssq[:, g:g + 1]), [B], [t2, ssq])
                    pg.op("dve", lambda e: e.tensor_reduce(out=rstd[:], in_=ssq[:, 0:NG], axis=mybir.AxisListType.X, op=ALU.add), [ssq], [rstd])
                    pg.op("dve", lambda e: e.tensor_scalar(out=rstd[:], in0=rstd[:], scalar1=1.0 / D, scalar2=EPS, op0=ALU.mult, op1=ALU.add),
                          [rstd], [rstd])
                    pg.op("act", lambda e: e.activation(out=rstd[:], in_=rstd[:], func=AF.Ln), [rstd], [rstd])
                    pg.op("act", lambda e: e.activation(out=rstd[:], in_=rstd[:], func=AF.Exp, scale=-0.5), [rstd], [rstd])
                    for g in range(NG):
                        B = PS[2 + g]
                        w = min(512, D - g * 512)
                        XQ, YQ = xq[xi[0] % 2], yq[xi[0] % 2]
                        xi[0] += 1
                        dma(lambda e, XQ=XQ, g=g, w=w, row0=row0: e.dma_start(out=XQ[:, 0:w], in_=x_own[row0:row0 + P, g * 512:g * 512 + w]), writes=[XQ])
                        pg.op("dve", lambda e, B=B, YQ=YQ, g=g, w=w: e.scalar_tensor_tensor(
                            out=YQ[:, 0:w], in0=B[:, 0:w], scalar=rstd[:, 0:1], in1=prow[:, g * 512:g * 512 + w], op0=ALU.mult, op1=ALU.mult),
                            [B, rstd, prow], [YQ])
                        pg.op("pool", lambda e, XQ=XQ, YQ=YQ, w=w: e.tensor_tensor(out=YQ[:, 0:w], in0=YQ[:, 0:w], in1=XQ[:, 0:w], op=ALU.add),
                              [XQ, YQ], [YQ])
                        dma(lambda e, YQ=YQ, g=g, w=w, row0=row0: e.dma_start(out=y_own[row0:row0 + P, g * 512:g * 512 + w], in_=YQ[:, 0:w]), reads=[YQ])
            for (t0_, n_) in part_ranges:
                part(t0_, n_)
            for j in range(CC):
                B = PS[2 + j % 2]
                pg.op("pe", lambda e, B=B, j=j: e.transpose(out=B[:NCS, 0:P], in_=ucs[:, j, :], identity=ident[:, 0:P]), [ucs, cst], [B])
                pg.op("dve", lambda e, B=B, j=j: e.tensor_copy(out=cstage[:NCS, j, 0:P], in_=B[:NCS, 0:P]), [B], [cstage])
            dma(lambda e: e.dma_start(out=conv_own.rearrange("r (j d) -> r j d", d=P), in_=cstage[:NCS, 0:CC, 0:P]), reads=[cstage])

            with nc.Block() as block:
                pg.emit(nc, block, sems, dsems, [])
    return nc


_CACHE = {}


def make_consts():
    cs = np.zeros((P, 4 * P), np.float32)
    cs[:, 0:P] = np.eye(P, dtype=np.float32)
    j = np.arange(P)[:, None]
    s = np.arange(P)[None, :]
    cs[:, P:2 * P] = -(j >= s).astype(np.float32)
    cs[:, 2 * P:3 * P] = -1.0
    cs[:, 3 * P:4 * P] = (s > j).astype(np.float32)
    return cs


def make_consts_full(GS):
    cs = np.zeros((P, 4 * P + GS * 4 + 1), np.float32)
    cs[:, -1] = np.arange(P, dtype=np.float32)
    cs[:, :4 * P] = make_consts()
    for bi in range(GS):
        for t in range(4):
            cs[:t, 4 * P + bi * 4 + t] = 1.0
    return cs


def run(cfg, x_prompt, x_sample, cache_k, cache_v, state_conv, page_table, meta_tokens,
        g_pre, w_in, b_sb, b_gate, w_conv, b_conv, w_pa, w_pc, w_out, g_post):
    c = cfg
    f32 = lambda a: np.ascontiguousarray(np.asarray(a, dtype=np.float32))
    x_prompt, x_sample, state_conv, meta_tokens = map(f32, (x_prompt, x_sample, state_conv, meta_tokens))
    g_pre, w_in, b_sb, b_gate, w_conv, b_conv, w_pa, w_pc, w_out, g_post = map(
        f32, (g_pre, w_in, b_sb, b_gate, w_conv, b_conv, w_pa, w_pc, w_out, g_post))
    cache_k = np.asarray(cache_k, dtype=np.float32)
    cache_v = np.asarray(cache_v, dtype=np.float32)
    page_table = np.ascontiguousarray(np.asarray(page_table, dtype=np.int32))
    key = (1, c.D, c.SEQ, c.NB, c.NCORES, c.NPG, c.NPOOL)
    if key not in _CACHE:
        _CACHE[key] = build(c)
    nc = _CACHE[key]
    consts = make_consts_full(c.GS)
    B = c.NCORES
    ckv = np.concatenate([cache_k[0].transpose(0, 2, 3, 1), cache_v[0].transpose(0, 2, 1, 3)], axis=3)
    ckv = np.ascontiguousarray(ckv.reshape(c.NPOOL, c.H // 2, 2, P, 2 * P).transpose(0, 1, 3, 2, 4)).reshape(c.NPOOL * (c.H // 2) * P, 4 * P)
    del cache_k, cache_v

    def tile_w(w):
        k, n = w.shape
        return np.ascontiguousarray(w.reshape(k // P, P, n // P, P).transpose(2, 1, 0, 3)).reshape(n, k)
    w_in_t, w_pa_t, w_pc_t, w_out_t = tile_w(w_in[0]), tile_w(w_pa[0]), tile_w(w_pc[0]), tile_w(w_out[0])

    in_maps = []
    for core in range(B):
        xo = np.zeros((c.TTP, c.D), np.float32)
        xo[0:NMETA] = meta_tokens
        xo[NMETA:c.TP] = x_prompt[core]
        xo[c.TP:c.TT] = x_sample[core * c.NSEQ:(core + 1) * c.NSEQ].reshape(c.NSO, c.D)
        pcol = np.zeros((P, c.NPC), np.float32)
        pcol[:, c.o_gpre:c.o_gpre + c.KC] = g_pre[0].reshape(c.KC, P).T
        pcol[:, c.o_bg:c.o_bg + 2 * c.KC] = b_gate[0].reshape(2 * c.KC, P).T
        pcol[:, c.o_wc:c.o_wc + 3 * c.CC] = w_conv[0].reshape(3, c.CC, P).transpose(2, 1, 0).reshape(P, 3 * c.CC)
        pcol[:, c.o_bc:c.o_bc + c.CC] = b_conv[0].reshape(c.CC, P).T
        st = state_conv[0, core * c.NSEQ:(core + 1) * c.NSEQ]
        pcol[:, c.o_st:c.o_st + c.CC * c.NSEQ * 2] = st.reshape(c.NSEQ, 2, c.CC, P).transpose(3, 2, 0, 1).reshape(P, -1)
        prow = np.zeros((P, c.D + c.H + 1), np.float32)
        prow[:, :c.D] = g_post[0][None, :]
        prow[:, c.D:c.D + c.H] = b_sb[0][None, :]
        pt = np.ascontiguousarray(np.broadcast_to(
            page_table[core * c.NSEQ:(core + 1) * c.NSEQ].reshape(1, c.NSEQ * c.NPG), (P, c.NSEQ * c.NPG)))
        in_maps.append({"x_own": xo, "w_in": w_in_t, "w_pa": w_pa_t, "w_pc": w_pc_t, "w_out": w_out_t,
                        "pcol": pcol, "prow": prow, "consts": consts, "ckv": ckv, "pt": pt})
    res = run_bass_kernel_spmd(nc, in_maps, core_ids=list(range(B)))
    R = res.results
    del in_maps, ckv
    y_prompt = np.stack([R[i]["y_own"][NMETA:c.TP] for i in range(B)])
    y_sample = np.concatenate([R[i]["y_own"][c.TP:c.TT].reshape(c.NSEQ, c.DEC_SEQ, c.D) for i in range(B)])
    kp = np.stack([R[i]["k_own"][:c.TP].reshape(c.TP, c.H, P) for i in range(B)])[None]
    vp = np.stack([R[i]["v_own"][:c.TP].reshape(c.TP, c.H, P) for i in range(B)])[None]
    cp = np.stack([R[i]["conv_own"][0:2] for i in range(B)])[None]
    ks = np.concatenate([R[i]["k_own"][c.TP:c.TT].reshape(c.NSEQ, c.DEC_SEQ, c.H, P) for i in range(B)])[None]
    vs = np.concatenate([R[i]["v_own"][c.TP:c.TT].reshape(c.NSEQ, c.DEC_SEQ, c.H, P) for i in range(B)])[None]
    cs = np.concatenate([R[i]["conv_own"][2:].reshape(c.NSEQ, 2, c.DC) for i in range(B)])[None]
    return (y_prompt, y_sample, kp, vp, cp, ks, vs, cs)


def kernel(**inputs):
    return run(Cfg(), **inputs)
```

```python
import numpy as np
import concourse.bass as bass
import concourse.mybir as mybir
from concourse.bass_utils import run_bass_kernel_spmd

F32, BF16, I32 = mybir.dt.float32, mybir.dt.bfloat16, mybir.dt.int32
AF = mybir.ActivationFunctionType
import os as _os
SILU = AF.Copy if _os.environ.get('KSILU', '') == '0' else AF.Silu
ALU = mybir.AluOpType
ET = mybir.EngineType

P = 128
NMETA = 16
EPS = 1e-6


class Cfg:
    def __init__(self, D=2048, SEQ=2048, DEC_BATCH=128, DEC_SEQ=4, NCORES=8, PAST=2048, NPOOL=2560, mode=1):
        self.mode = mode
        self.NB = DEC_BATCH
        self.NPG = PAST // P
        self.NPOOL = NPOOL
        self.GS = 1
        self.NSA = DEC_BATCH * DEC_SEQ
        self.D = D
        self.KC = D // P
        self.DA = D // 2
        self.H = self.DA // P
        self.DC = D // 2
        self.CC = self.DC // P
        self.SEQ = SEQ
        self.TP = (NMETA + SEQ) if mode == 1 else 0
        self.NCORES = NCORES
        self.NSEQ = DEC_BATCH // NCORES
        self.DEC_SEQ = DEC_SEQ
        self.NSO = self.NSEQ * DEC_SEQ
        self.TT = self.TP + self.NSO
        self.NT = (self.TT + P - 1) // P
        self.TTP = self.NT * P
        self.INC = 4 * self.DA + 4 * self.DC + 2 * D
        self.NCS = 2 + 2 * self.NSEQ
        o = 0
        self.o_gpre = o; o += self.KC
        self.o_bg = o; o += 2 * self.KC
        self.o_wc = o; o += 3 * self.CC
        self.o_bc = o; o += self.CC
        self.o_st = o; o += self.CC * self.NSEQ * 2
        self.NPC = o


class Tile:
    def __init__(self, ap, name):
        self.ap = ap
        self.name = name
        self.w = None
        self.r = {}

    def __getitem__(self, k):
        return self.ap[k]


class Prog:
    ENG = ["sp", "act", "dve", "pool", "pe"]
    NDMASEM = 16

    def __init__(self):
        self.ops = {e: [] for e in self.ENG}
        self.needed = set()

    cut = False

    def barrier(self):
        tk = set()
        for e in self.ENG:
            ops = self.ops[e]
            last_c = [i for i in range(len(ops)) if not ops[i][2]][-1:]
            last_d = [i for i in range(len(ops)) if ops[i][2]][-self.NDMASEM:]
            for i in last_c + last_d:
                tk.add((e, i))
        self.pending = {e: set(tk) for e in self.ENG}

    pending = {}

    def op(self, eng, fn, reads=(), writes=(), dma=False):
        if self.cut:
            return None
        deps = set(self.pending.pop(eng, ()))
        for t in reads:
            if t.w is not None:
                deps.add(t.w)
        for t in writes:
            if t.w is not None:
                deps.add(t.w)
            for tk in t.r.values():
                deps.add(tk)
        tk = (eng, len(self.ops[eng]))
        if eng == "pe":
            deps = {d for d in deps if d[0] != "pe"}
        deps.discard(tk)
        self.ops[eng].append((fn, deps, dma))
        self.needed |= deps
        for t in reads:
            t.r[eng] = tk
        for t in writes:
            t.w = tk
            t.r = {}
        return tk

    def emit(self, nc, block, sems, dsems, final_waits):
        val = {}
        cnt = {e: 0 for e in self.ENG}
        dcnt = {}
        dma_prev = {}
        for e in self.ENG:
            ndma = 0
            for i, (fn, deps, dma) in enumerate(self.ops[e]):
                if dma:
                    s = (e, ndma % self.NDMASEM)
                    dcnt[s] = dcnt.get(s, 0) + 16
                    val[(e, i)] = ("d", s, dcnt[s])
                    dma_prev[(e, i)] = ("d", s, dcnt[s] - 16) if dcnt[s] > 16 else None
                    ndma += 1
                elif (e, i) in self.needed:
                    cnt[e] += 1
                    val[(e, i)] = ("e", e, cnt[e])
        all_dma = [(e, i) for e in self.ENG for i, o in enumerate(self.ops[e]) if o[2]]

        def run(e, engobj):
            waited = {}

            def wait(v):
                if v is None:
                    return
                kind, s, n = v
                key = (kind, s)
                if waited.get(key, 0) >= n:
                    return
                waited[key] = n
                engobj.wait_ge(dsems[s] if kind == "d" else sems[s], n)

            for i, (fn, deps, dma) in enumerate(self.ops[e]):
                for d in sorted(deps):
                    wait(val[d])
                if dma:
                    wait(dma_prev[(e, i)])
                ins = fn(engobj)
                v = val.get((e, i))
                if v is not None:
                    kind, s, n = v
                    ins.then_inc(dsems[s] if kind == "d" else sems[s], 16 if kind == "d" else 1)
            if e == "sp":
                for d in all_dma:
                    wait(val[d])
                for d in final_waits:
                    wait(val[d])

        block.sync(lambda eng: run("sp", eng))
        block.scalar(lambda eng: run("act", eng))
        block.vector(lambda eng: run("dve", eng))
        block.gpsimd(lambda eng: run("pool", eng))
        block.tensor(lambda eng: run("pe", eng))


def build(cfg):
    c = cfg
    D, KC, DA, H, DC, CC, TP, TT, NT, TTP = c.D, c.KC, c.DA, c.H, c.DC, c.CC, c.TP, c.TT, c.NT, c.TTP
    NSEQ, NCS = c.NSEQ, c.NCS
    nc = bass.Bass("TRN2", target_bir_lowering=False)
    dt = lambda n, s, k="ExternalInput": nc.dram_tensor(n, s, F32, kind=k).ap()
    x_own = dt("x_own", [TTP, D])
    w_in = dt("w_in", [c.INC, D])
    w_pa = dt("w_pa", [D, DA])
    w_pc = dt("w_pc", [D, DC])
    w_out = dt("w_out", [D, D])
    pcol_d = dt("pcol", [P, c.NPC])
    prow_d = dt("prow", [P, D + H + 1])
    NCST = 4 * P + c.GS * 4 + 1
    consts_d = dt("consts", [P, NCST])
    if c.mode == 1:
        ckv = dt("ckv", [c.NPOOL * (H // 2) * P, 4 * P])
        pt_d = nc.dram_tensor("pt", [P, c.NSEQ * c.NPG], I32, kind="ExternalInput").ap()
    else:
        a_in = dt("a_in", [P, H * P])
    y_own = dt("y_own", [TTP, D], "ExternalOutput")
    k_own = dt("k_own", [TTP, DA], "ExternalOutput")
    v_own = dt("v_own", [TTP, DA], "ExternalOutput")
    conv_own = dt("conv_own", [NCS, DC], "ExternalOutput")

    pg = Prog()
    import contextlib
    es = contextlib.ExitStack()

    def sb(name, shape, dtype=F32):
        return Tile(es.enter_context(nc.sbuf_tensor("s_" + name, shape, dtype)), name)

    def ps(name):
        return Tile(es.enter_context(nc.psum_tensor(name, [P, 512], F32)), name)

    with es:
        sems = {e: es.enter_context(nc.semaphore("sem_" + e)) for e in Prog.ENG}
        dsems = {(e_, i): es.enter_context(nc.semaphore("dsem_%s%d" % (e_, i))) for e_ in ("sp", "pool") for i in range(Prog.NDMASEM)}
        pcol = sb("pcol", [P, c.NPC])
        prow = sb("prow", [P, D + H + 1])
        cst = sb("cst", [P, NCST])
        ident = cst
        cbf = sb("cbf", [P, NCST - P], BF16)
        ones1 = sb("ones1", [P, 1])
        NWB = 4
        wbf = [sb("wbf%d" % i, [P, KC, P], BF16) for i in range(NWB)]
        wst = [sb("wst0", [P, 1, P])]
        PS = [ps("ps%d" % i) for i in range(8)]
        ucs = sb("ucs", [P, CC, NCS])
        uh = sb("uh", [P, CC, 2])

        dma = lambda fn, reads=(), writes=(): pg.op("sp", fn, reads, writes, dma=True)

        dma(lambda e: e.dma_start(out=pcol[:], in_=pcol_d), writes=[pcol])
        dma(lambda e: e.dma_start(out=prow[:], in_=prow_d), writes=[prow])
        dma(lambda e: e.dma_start(out=cst[:], in_=consts_d), writes=[cst])
        pg.op("pool", lambda e: e.tensor_copy(out=cbf[:], in_=cst[:, P:NCST]), [cst], [cbf])
        pg.op("pool", lambda e: e.memset(ones1[:], 1.0), [], [ones1])
        pg.op("pool", lambda e: e.memset(uh[:], 0.0), [], [uh])
        negtri = lambda k, m: cbf[:k, 0:m]
        negones = lambda k, m: cbf[:k, P:P + m]
        masku = lambda k, m: cbf[:k, 2 * P:2 * P + m]
        maskp = lambda: cbf[:, 3 * P:3 * P + c.GS * 4]
        import os

        def stage_a(x_src, ntiles, hT, tag):
          with contextlib.ExitStack() as sa:
            sbl = lambda name, shape, dtype=F32: Tile(sa.enter_context(nc.sbuf_tensor("s_" + tag + name, shape, dtype)), name)
            xt = [sbl("xt%d" % i, [P, D]) for i in range(2)]
            xn = [sbl("xn%d" % i, [P, D]) for i in range(2)]
            junk = sbl("junk", [P, D], BF16)
            ss = [sbl("ss%d" % i, [P, 1]) for i in range(2)]
            rs = [sbl("rs%d" % i, [P, 1]) for i in range(2)]
            for i in range(ntiles):
                X, XN, SS, RS = xt[i % 2], xn[i % 2], ss[i % 2], rs[i % 2]
                dma(lambda e, X=X, i=i: e.dma_start(out=X[:], in_=x_src[i * P:(i + 1) * P, :]), writes=[X])
                pg.op("act", lambda e, X=X, SS=SS: e.activation(out=junk[:], in_=X[:], func=AF.Square, accum_out=SS[:]),
                      [X], [junk, SS])
                pg.op("dve", lambda e, SS=SS, RS=RS: e.tensor_scalar(out=RS[:], in0=SS[:], scalar1=1.0 / D, scalar2=EPS,
                                                                      op0=ALU.mult, op1=ALU.add), [SS], [RS])
                pg.op("act", lambda e, RS=RS: e.activation(out=RS[:], in_=RS[:], func=AF.Ln), [RS], [RS])
                pg.op("act", lambda e, RS=RS: e.activation(out=RS[:], in_=RS[:], func=AF.Exp, scale=-0.5), [RS], [RS])
                pg.op("act", lambda e, X=X, XN=XN, RS=RS: e.activation(out=XN[:], in_=X[:], func=AF.Copy, scale=RS[:, 0:1]),
                      [X, RS], [XN])
                for g in range((KC + 3) // 4):
                    B = PS[g % 2]
                    nq = min(4, KC - 4 * g)
                    for q in range(nq):
                        kc = 4 * g + q
                        pg.op("pe", lambda e, B=B, XN=XN, q=q, kc=kc: e.transpose(
                            out=B[:, q * P:(q + 1) * P], in_=XN[:, kc * P:(kc + 1) * P], identity=ident[:, 0:P]),
                            [XN, cst], [B])
                    for q in range(nq):
                        kc = 4 * g + q
                        pg.op("dve", lambda e, B=B, q=q, kc=kc, i=i: e.tensor_scalar(
                            out=hT[:, kc, i * P:(i + 1) * P], in0=B[:, q * P:(q + 1) * P],
                            scalar1=pcol[:, c.o_gpre + kc:c.o_gpre + kc + 1], scalar2=None, op0=ALU.mult),
                            [B, pcol], [hT])
          pg.barrier()

        slab_i = [0]
        pp_i = [0]

        def proj(wd, kcx, col0, act, act_c0, ranges, evac, pbanks=(0, 1)):
            k = slab_i[0]
            slab_i[0] += 1
            WB = wbf[k % NWB]
            pg.op("pool", lambda e: e.dma_start(out=WB[:, 0:kcx, :], in_=wd[col0:col0 + P, :].rearrange("p (kc c) -> p kc c", c=P)),
                  [], [WB], dma=True)
            for (t0, n) in ranges:
                B = PS[pbanks[pp_i[0] % len(pbanks)]]
                pp_i[0] += 1
                for kc in range(kcx):
                    pg.op("pe", lambda e, B=B, kc=kc, t0=t0, n=n: e.matmul(
                        B[:, 0:n], lhsT=WB[:, kc, :], rhs=act[:, kc, act_c0 + t0:act_c0 + t0 + n],
                        start=(kc == 0), stop=(kc == kcx - 1)), [WB, act], [B])
                evac(B, t0, n)

        def phase_s():
          NPG, GS = c.NPG, 1
          NSO, NSQ = c.NSO, c.NSEQ
          NPP = NPG + 1
          G4 = GS * 4
          W = NPP * G4
          NPT = NSQ * NPG
          HP = H // 2
          with contextlib.ExitStack() as sa:
            sbl = lambda name, shape, dtype=F32: Tile(sa.enter_context(nc.sbuf_tensor("s_ps_" + name, shape, dtype)), name)
            pts = Tile(sa.enter_context(nc.sbuf_tensor("s_ps_pts", [P, NPT], I32)), "pts")
            ptf = sbl("ptf", [P, NPT])
            idxf = sbl("idxf", [P, NPT])
            idxh = [Tile(sa.enter_context(nc.sbuf_tensor("s_ps_idx%d" % i, [P, NPT], I32)), "idx") for i in range(2)]
            kvb = [sbl("kvb%d" % i, [P, NPG, 4 * P], BF16) for i in range(2)]
            kvslot = [[Tile(kvb[i].ap[:, sl, :], "kvs") for sl in range(NPG)] for i in range(2)]
            kpad = [sbl("kpad%d" % i, [P, GS, P], BF16) for i in range(2)]
            vpad = [sbl("vpad%d" % i, [P, GS, P], BF16) for i in range(2)]
            zsb = sbl("zsb", [P, W])
            E = sbl("E", [P, W])
            lg = sbl("lg", [P, W])
            L = sbl("L", [P, W], BF16)
            AT = sbl("AT", [P, W], BF16)
            LKb = sbl("LKb", [P, W], BF16)
            LK32 = sbl("LK32", [P, W])
            dma(lambda e: e.dma_start(out=pts[:], in_=pt_d), writes=[pts])
            pg.op("dve", lambda e: e.tensor_copy(out=ptf[:], in_=pts[:]), [pts], [ptf])
            pg.op("dve", lambda e: e.tensor_scalar(out=ptf[:], in0=ptf[:], scalar1=float(HP * P), scalar2=cst[:, NCST - 1:NCST],
                                                   op0=ALU.mult, op1=ALU.add), [ptf, cst], [ptf])
            for i in range(2):
                pg.op("pool", lambda e, i=i: e.memset(kpad[i][:], 0.0), [], [kpad[i]])
                pg.op("pool", lambda e, i=i: e.memset(vpad[i][:], 0.0), [], [vpad[i]])
            pg.op("pool", lambda e: e.memset(LK32[:], 0.0), [], [LK32])
            units = [(hp, b) for hp in range(HP) for b in range(NSQ)]

            def head_idx(hp):
                IX = idxh[hp % 2]
                pg.op("dve", lambda e: e.tensor_scalar(out=idxf[:], in0=ptf[:], scalar1=float(hp * P), scalar2=None, op0=ALU.add),
                      [ptf], [idxf])
                pg.op("dve", lambda e: e.tensor_copy(out=IX[:], in_=idxf[:]), [idxf], [IX])

            def loads(u):
                hp, b = units[u]
                if b == 0:
                    head_idx(hp)
                KV, KVL, IX = kvb[u % 2], kvslot[u % 2], idxh[hp % 2]
                for p in range(NPG):
                    page_load(KV, KVL, IX, p, b * NPG + p)

            def page_load(KV, KVL, IX, slot, j):
                pg.op("pool", lambda e: e.indirect_dma_start(
                    out=KV[:, slot, :], out_offset=None, in_=ckv,
                    in_offset=bass.IndirectOffsetOnAxis(ap=IX[:, j:j + 1], axis=0)), [IX], [KVL[slot]], dma=True)

            cnt = [0]

            def unit(u):
                hp, b = units[u]
                if u == 0:
                    loads(0)
                if u + 1 < len(units):
                    loads(u + 1)
                for hh in range(2):
                    group(u, 2 * hp + hh, hh, b)

            def group(u, h, hh, b):
                i2 = cnt[0] % 2
                cnt[0] += 1
                KP, VP = kpad[i2], vpad[i2]
                KV, KVL = kvb[u % 2], kvslot[u % 2]
                ko, vo = hh * 2 * P, hh * 2 * P + P
                AB, CB, OB, VT = PS[2 + i2], PS[4 + i2], PS[6 + i2], PS[0]
                bias_c = lambda: prow[:, D + h:D + h + 1]
                pg.op("dve", lambda e: e.tensor_copy(out=KP[:, 0, 0:4], in_=kS[:, h, 4 * b:4 * b + 4]), [kS], [KP])
                pg.op("pe", lambda e: e.transpose(out=VT[0:4, 0:P], in_=vS32[:, h, 4 * b:4 * b + 4], identity=ident[:, 0:P]), [vS32, cst], [VT])
                pg.op("dve", lambda e: e.tensor_copy(out=VP[0:4, 0, :], in_=VT[0:4, 0:P]), [VT], [VP])
                for p in range(NPP):
                    zmm(AB, KP, KV, KVL, ko, p, p * 4, h, b)
                pg.op("dve", lambda e: e.tensor_copy(out=zsb[:], in_=AB[:, 0:W]), [AB], [zsb])
                pg.op("act", lambda e: e.activation(out=E[:], in_=zsb[:], func=AF.Exp, bias=bias_c()), [zsb, prow], [E])
                pg.op("act", lambda e: e.activation(out=L[:], in_=E[:], func=AF.Ln, bias=ones1[:, 0:1]), [E, ones1], [L])
                pg.op("dve", lambda e: e.tensor_tensor(out=L[:, NPG * G4:W], in0=L[:, NPG * G4:W], in1=maskp(), op=ALU.mult), [L, cbf], [L])
                for p in range(NPG - 1, -1, -1):
                    pg.op("dve", lambda e, p=p: e.tensor_tensor(out=LK32[:, p * G4:(p + 1) * G4], in0=LK32[:, (p + 1) * G4:(p + 2) * G4],
                                                                in1=L[:, (p + 1) * G4:(p + 2) * G4], op=ALU.add), [LK32, L], [LK32])
                pg.op("dve", lambda e: e.tensor_copy(out=LKb[:], in_=LK32[:]), [LK32], [LKb])
                pg.op("pe", lambda e: e.matmul(CB[:, 0:W], lhsT=negtri(P, P), rhs=L[:], start=True, stop=False), [L, cbf], [CB])
                pg.op("pe", lambda e: e.matmul(CB[:, 0:W], lhsT=negones(P, P), rhs=LKb[:], start=False, stop=True), [LKb, cbf], [CB])
                pg.op("dve", lambda e: e.tensor_tensor(out=lg[:], in0=CB[:, 0:W], in1=zsb[:], op=ALU.add), [CB, zsb], [lg])
                pg.op("act", lambda e: e.activation(out=AT[:], in_=lg[:], func=AF.Exp, bias=bias_c()), [lg, prow], [AT])
                pg.op("dve", lambda e: e.tensor_tensor(out=AT[:, NPG * G4:W], in0=AT[:, NPG * G4:W], in1=maskp(), op=ALU.mult), [AT, cbf], [AT])
                for p in range(NPP):
                    avmm(OB, VP, KV, KVL, vo, p, p * 4)
                pg.op("dve", lambda e: e.tensor_tensor(out=a_t[:, h, TP + 4 * b:TP + 4 * b + 4], in0=OB[:, 0:4], in1=sgS[:, h, 4 * b:4 * b + 4],
                                                       op=ALU.mult), [OB, sgS], [a_t])

            def zmm(AB, KP, KV, KVL, ko, p, col, h, b):
                if p < NPG:
                    pg.op("pe", lambda e: e.matmul(AB[:, col:col + 4], lhsT=KV[:, p, ko:ko + P], rhs=qS[:, h, 4 * b:4 * b + 4],
                                                   start=True, stop=True), [KVL[p], qS], [AB])
                else:
                    pg.op("pe", lambda e: e.matmul(AB[:, col:col + 4], lhsT=KP[:, 0, :], rhs=qS[:, h, 4 * b:4 * b + 4],
                                                   start=True, stop=True), [KP, qS], [AB])

            def avmm(OB, VP, KV, KVL, vo, p, col):
                if p < NPG:
                    pg.op("pe", lambda e: e.matmul(OB[:, 0:4], lhsT=KV[:, p, vo:vo + P], rhs=AT[:, col:col + 4],
                                                   start=(p == 0), stop=False, skip_group_check=True), [KVL[p], AT], [OB])
                else:
                    pg.op("pe", lambda e: e.matmul(OB[:, 0:4], lhsT=VP[:, 0, :], rhs=AT[:, col:col + 4],
                                                   start=False, stop=True, skip_group_check=True), [VP, AT], [OB])

            for u in range(len(units)):
                unit(u)
          pg.barrier()

        full_ranges = [(t, min(512, TTP - t)) for t in range(0, TTP, 512)]

        hT = sb("hT", [P, KC, TTP], BF16)
        a_t = sb("a_t", [P, H, TTP], BF16)
        qS = sb("qS", [P, H, c.NSO], BF16)
        kS = sb("kS", [P, H, c.NSO], BF16)
        vS32 = sb("vS32", [P, H, c.NSO])
        sgS = sb("sgS", [P, H, c.NSO])
        if c.mode == 1:
            pg.op("pool", lambda e: e.memset(a_t[:], 0.0), [], [a_t])
        else:
            for hh in range(H):
                ST = wst[0]
                dma(lambda e, ST=ST, hh=hh: e.dma_start(out=ST[:, 0, :], in_=a_in[:, hh * P:(hh + 1) * P]), writes=[ST])
                pg.op("pool", lambda e, ST=ST, hh=hh: e.tensor_copy(out=a_t[:, hh, 0:P], in_=ST[:, 0, :]), [ST], [a_t])
        stage_a(x_own, NT, hT, "a")
        if os.environ.get("KSTOP", "") == "A":
            pg.cut = True

        with contextlib.ExitStack() as sa:
            sbl = lambda name, shape, dtype=F32: Tile(sa.enter_context(nc.sbuf_tensor("s_" + name, shape, dtype)), name)
            qT = sbl("qT", [P, TTP], BF16)
            kT = sbl("kT", [P, TTP], BF16)
            vtok = sbl("vtok", [P, NT, P], BF16)
            sg = sbl("sg", [P, TTP], BF16)
            kv32 = [sbl("kv32_%d" % i, [P, 512]) for i in range(2)]
            tokst = [sbl("tokst%d" % i, [P, 4, P]) for i in range(2)]
            e32 = [sbl("e32_%d" % i, [P, 512]) for i in range(2)]
            lkp = [sbl("lkp%d" % i, [P, 512], BF16) for i in range(2)]
            lks = [sbl("lks%d" % i, [P, 512], BF16) for i in range(3)]
            att = [sbl("att%d" % i, [P, 512], BF16) for i in range(2)]
            kvi = [0]
            scale = float(P) ** -0.5

            def kv_out(B, t0, n, out_d, h, isv):
                i = kvi[0] % 2
                kvi[0] += 1
                K32, TS, PT = kv32[i], tokst[i], PS[2 + i]
                pg.op("act", lambda e: e.activation(out=K32[:, 0:n], in_=B[:, 0:n], func=AF.Copy), [B], [K32])
                nb = n // P
                if os.environ.get("KKV", "") == "min":
                    return K32
                for j in range(nb):
                    pg.op("pe", lambda e, j=j: e.transpose(out=PT[:, j * P:(j + 1) * P], in_=K32[:, j * P:(j + 1) * P],
                                                           identity=ident[:, 0:P]), [K32, cst], [PT])
                pg.op("dve", lambda e: e.tensor_copy(out=TS[:, 0:nb, :], in_=PT[:, 0:nb * P].rearrange("p (j d) -> p j d", d=P)),
                      [PT], [TS])
                if isv:
                    pg.op("pool", lambda e: e.tensor_copy(out=vtok[:, t0 // P:t0 // P + nb, :], in_=TS[:, 0:nb, :]), [TS], [vtok])
                if os.environ.get("KKV", "") != "noout":
                    dma(lambda e: e.dma_start(out=out_d[t0:t0 + n, h * P:(h + 1) * P].rearrange("(j p) d -> p j d", p=P),
                                              in_=TS[:, 0:nb, :]), reads=[TS])
                return K32

            def head(h):
                def ev_q(B, t0, n):
                    pg.op("act", lambda e: e.activation(out=qT[:, t0:t0 + n], in_=B[:, 0:n], func=AF.Copy, scale=scale), [B], [qT])
                proj(w_in, KC, h * P, hT, 0, full_ranges, ev_q)
                if os.environ.get("KSTOP", "") == "P0":
                    pg.cut = True

                def ev_k(B, t0, n, h=h):
                    K32 = kv_out(B, t0, n, k_own, h, False)
                    pg.op("dve", lambda e: e.tensor_copy(out=kT[:, t0:t0 + n], in_=K32[:, 0:n]), [K32], [kT])
                proj(w_in, KC, DA + h * P, hT, 0, full_ranges, ev_k)

                def ev_v(B, t0, n, h=h):
                    K32 = kv_out(B, t0, n, v_own, h, True)
                    if t0 <= TP and TT <= t0 + n:
                        pg.op("dve", lambda e: e.tensor_copy(out=vS32[:, h, :], in_=K32[:, TP - t0:TT - t0]), [K32], [vS32])
                proj(w_in, KC, 2 * DA + h * P, hT, 0, full_ranges, ev_v)

                def ev_g(B, t0, n):
                    pg.op("act", lambda e: e.activation(out=sg[:, t0:t0 + n], in_=B[:, 0:n], func=SILU), [B], [sg])
                proj(w_in, KC, 3 * DA + h * P, hT, 0, full_ranges, ev_g)

                pg.op("dve", lambda e: e.tensor_copy(out=qS[:, h, :], in_=qT[:, TP:TT]), [qT], [qS])
                pg.op("dve", lambda e: e.tensor_copy(out=kS[:, h, :], in_=kT[:, TP:TT]), [kT], [kS])
                pg.op("dve", lambda e: e.tensor_copy(out=sgS[:, h, :], in_=sg[:, TP:TT]), [sg], [sgS])
                bias_h = lambda k, h=h: prow[:k, D + h:D + h + 1]
                if os.environ.get("KSTOP", "") == "P1":
                    pg.cut = True
                for qi, q0 in enumerate(range(0, TP, 512)):
                    qtile(h, bias_h, qi, q0)

            def qtile(h, bias_h, qi, q0):
                if True:
                    qn = min(512, TP - q0)
                    OB = PS[6 + qi % 2]
                    for i3 in range(3):
                        pg.op("pool", lambda e, i3=i3: e.memset(lks[i3][:], 0.0), [], [lks[i3]])
                    S_hi = (q0 + qn - 1) // P
                    steps = list(enumerate(range(S_hi, -1, -1)))
                    step(h, bias_h, q0, qn, OB, steps[0][0], steps[0][1], 1)
                    for i_, (st, S) in enumerate(steps):
                        if i_ + 1 < len(steps):
                            step(h, bias_h, q0, qn, OB, steps[i_ + 1][0], steps[i_ + 1][1], 1)
                        step(h, bias_h, q0, qn, OB, st, S, 2)
                    pg.op("dve", lambda e: e.tensor_tensor(
                        out=a_t[:, h, q0:q0 + qn], in0=OB[:, 0:qn], in1=sg[:, q0:q0 + qn], op=ALU.mult), [OB, sg], [a_t])

            def step(h, bias_h, q0, qn, OB, st, S, half):
                if True:
                    if True:
                        li = st
                        nk = min(P, TP - S * P)
                        c0 = max(q0, S * P) - q0
                        diag = S * P >= q0
                        nd = min(P, qn - c0)
                        AB = PS[4 + st % 2]
                        E, L, AT = e32[st % 2], lkp[st % 2], att[st % 2]
                        LO, LN = lks[li % 3], lks[(li + 1) % 3]
                        first, last = (st == 0), (S == 0)
                        if half == 1:
                            pg.op("pe", lambda e, AB=AB, S=S, nk=nk, c0=c0, qn=qn, q0=q0: e.matmul(
                                AB[:nk, c0:qn], lhsT=kT[:, S * P:S * P + nk], rhs=qT[:, q0 + c0:q0 + qn], start=True, stop=False),
                                [kT, qT], [AB])
                            pg.op("act", lambda e, AB=AB, E=E, nk=nk, c0=c0, qn=qn: e.activation(
                                out=E[:nk, c0:qn], in_=AB[:nk, c0:qn], func=AF.Exp, bias=bias_h(nk)), [AB, prow], [E])
                            pg.op("act", lambda e, L=L, E=E, nk=nk, c0=c0, qn=qn: e.activation(
                                out=L[:nk, c0:qn], in_=E[:nk, c0:qn], func=AF.Ln, bias=ones1[:nk, 0:1]), [E, ones1], [L])
                            if diag:
                                pg.op("pool", lambda e, L=L, nk=nk, c0=c0, nd=nd: e.tensor_tensor(
                                    out=L[:nk, c0:c0 + nd], in0=L[:nk, c0:c0 + nd], in1=masku(nk, nd), op=ALU.mult), [L, cbf], [L])
                            if not last:
                                pg.op("dve", lambda e, LO=LO, LN=LN, L=L, nk=nk, c0=c0, qn=qn: e.tensor_tensor(
                                    out=LN[:nk, c0:qn], in0=LO[:nk, c0:qn], in1=L[:nk, c0:qn], op=ALU.add), [LO, L], [LN])
                        if half == 2:
                            pg.op("pe", lambda e, AB=AB, L=L, nk=nk, c0=c0, qn=qn, first=first: e.matmul(
                                AB[:nk, c0:qn], lhsT=negtri(nk, nk), rhs=L[:nk, c0:qn], start=False, stop=first,
                                skip_group_check=True), [L, cbf], [AB])
                            if not first:
                                pg.op("pe", lambda e, AB=AB, LO=LO, nk=nk, c0=c0, qn=qn: e.matmul(
                                    AB[:nk, c0:qn], lhsT=negones(P, nk), rhs=LO[:, c0:qn], start=False, stop=True,
                                    skip_group_check=True), [LO, cbf], [AB])
                            pg.op("act", lambda e, AB=AB, AT=AT, nk=nk, c0=c0, qn=qn: e.activation(
                                out=AT[:nk, c0:qn], in_=AB[:nk, c0:qn], func=AF.Exp, bias=bias_h(nk)), [AB, prow], [AT])
                            if diag:
                                pg.op("pool", lambda e, AT=AT, nk=nk, c0=c0, nd=nd: e.tensor_tensor(
                                    out=AT[:nk, c0:c0 + nd], in0=AT[:nk, c0:c0 + nd], in1=masku(nk, nd), op=ALU.mult), [AT, cbf], [AT])
                            pg.op("pe", lambda e, OB=OB, AT=AT, S=S, nk=nk, c0=c0, qn=qn, first=first, last=last: e.matmul(
                                OB[:, c0:qn], lhsT=vtok[:nk, S, :], rhs=AT[:nk, c0:qn], start=first, stop=last,
                                skip_group_check=True), [vtok, AT], [OB])
            if c.mode == 1:
                for h in range(H):
                    head(h)
        pg.barrier()
        if c.mode == 1 and os.environ.get("KSTOP", "") != "P":
            phase_s()
        if os.environ.get("KSTOP", "") == "S":
            pg.cut = True

        if os.environ.get("KSTOP", "") == "P":
            pg.cut = True
        with contextlib.ExitStack() as sa:
            sbl = lambda name, shape, dtype=F32: Tile(sa.enter_context(nc.sbuf_tensor("s_" + name, shape, dtype)), name)
            cpart = sbl("cpart", [P, CC, 384], BF16)
            merged = sbl("merged", [P, KC, 384], BF16)
            oT = sbl("oT", [P, KC, 384])
            cb32 = sbl("cb32", [P, 384])
            cc32 = sbl("cc32", [P, 384])
            uext = sbl("uext", [P, 386])
            sgc = sbl("sgc", [P, 384])
            yc = sbl("yc", [P, 384])
            ues = sbl("ues", [P, NSEQ, 6])
            ga32 = cb32
            gc32 = cc32
            t1 = sbl("t1", [P, 384])
            t2 = sbl("t2", [P, 512])
            ssq = sbl("ssq", [P, 8])
            rstd = sbl("rstd", [P, 1])
            xq = [sbl("xq%d" % i, [P, 512]) for i in range(2)]
            yq = [sbl("yq%d" % i, [P, 512]) for i in range(2)]
            cstage = oT
            wc = lambda j, i: pcol[:, c.o_wc + 3 * j + i:c.o_wc + 3 * j + i + 1]
            bc = lambda j: pcol[:, c.o_bc + j:c.o_bc + j + 1]
            xi = [0]
            PART = 384
            part_ranges = [(t, min(PART, TTP - t)) for t in range(0, TTP, PART)]

            def part(t0, n):
                rng = [(0, n)]
                np_ = max(0, min(t0 + n, TP) - t0)
                s0, s1 = max(t0, TP) - t0, min(t0 + n, TT) - t0
                has_s = s1 > s0
                if has_s:
                    assert s1 - s0 == c.NSO, "sample tokens must sit inside one part"
                def convchunk(j):
                    base = 4 * DA
                    proj(w_in, KC, base + j * P, hT, t0, rng,
                         lambda B, _t, _n: pg.op("act", lambda e: e.activation(out=cb32[:, 0:n], in_=B[:, 0:n], func=AF.Copy), [B], [cb32]))
                    proj(w_in, KC, base + DC + j * P, hT, t0, rng,
                         lambda B, _t, _n: pg.op("act", lambda e: e.activation(out=cc32[:, 0:n], in_=B[:, 0:n], func=AF.Copy), [B], [cc32]))
                    proj(w_in, KC, base + 2 * DC + j * P, hT, t0, rng,
                         lambda B, _t, _n: pg.op("dve", lambda e: e.tensor_tensor(out=uext[:, 2:2 + n], in0=B[:, 0:n], in1=cc32[:, 0:n], op=ALU.mult),
                                                 [B, cc32], [uext]))
                    proj(w_in, KC, base + 3 * DC + j * P, hT, t0, rng,
                         lambda B, _t, _n: pg.op("act", lambda e: e.activation(out=sgc[:, 0:n], in_=B[:, 0:n], func=SILU), [B], [sgc]))
                    pg.op("dve", lambda e, j=j: e.tensor_copy(out=uext[:, 0:2], in_=uh[:, j, :]), [uh], [uext])
                    pg.op("dve", lambda e, j=j: e.tensor_scalar(out=yc[:, 0:n], in0=uext[:, 2:2 + n], scalar1=wc(j, 2), scalar2=bc(j),
                                                                 op0=ALU.mult, op1=ALU.add), [uext, pcol], [yc])
                    pg.op("dve", lambda e, j=j: e.scalar_tensor_tensor(out=yc[:, 0:n], in0=uext[:, 1:1 + n], scalar=wc(j, 1), in1=yc[:, 0:n],
                                                                        op0=ALU.mult, op1=ALU.add), [uext, pcol, yc], [yc])
                    pg.op("dve", lambda e, j=j: e.scalar_tensor_tensor(out=yc[:, 0:n], in0=uext[:, 0:n], scalar=wc(j, 0), in1=yc[:, 0:n],
                                                                        op0=ALU.mult, op1=ALU.add), [uext, pcol, yc], [yc])
                    if has_s:
                        so = c.o_st + j * NSEQ * 2
                        pg.op("dve", lambda e, so=so: e.tensor_copy(out=ues[:, :, 0:2], in_=pcol[:, so:so + 2 * NSEQ].rearrange("p (b r) -> p b r", r=2)),
                              [pcol], [ues])
                        pg.op("dve", lambda e: e.tensor_copy(out=ues[:, :, 2:6], in_=uext[:, 2 + s0:2 + s1].rearrange("p (b t) -> p b t", t=4)),
                              [uext], [ues])
                        ysv = lambda: yc[:, s0:s1].rearrange("p (b t) -> p b t", t=4)
                        pg.op("dve", lambda e, j=j: e.tensor_scalar(out=ysv(), in0=ues[:, :, 2:6], scalar1=wc(j, 2), scalar2=bc(j),
                                                                     op0=ALU.mult, op1=ALU.add), [ues, pcol], [yc])
                        pg.op("dve", lambda e, j=j: e.scalar_tensor_tensor(out=ysv(), in0=ues[:, :, 1:5], scalar=wc(j, 1), in1=ysv(),
                                                                            op0=ALU.mult, op1=ALU.add), [ues, pcol, yc], [yc])
                        pg.op("dve", lambda e, j=j: e.scalar_tensor_tensor(out=ysv(), in0=ues[:, :, 0:4], scalar=wc(j, 0), in1=ysv(),
                                                                            op0=ALU.mult, op1=ALU.add), [ues, pcol, yc], [yc])
                        pg.op("dve", lambda e, j=j: e.tensor_copy(out=ucs[:, j, 2:NCS].rearrange("p (b r) -> p b r", r=2), in_=ues[:, :, 4:6]),
                              [ues], [ucs])
                    if t0 <= TP - 2 and TP <= t0 + n:
                        pg.op("dve", lambda e, j=j: e.tensor_copy(out=ucs[:, j, 0:2], in_=uext[:, 2 + TP - 2 - t0:2 + TP - t0]), [uext], [ucs])
                    if np_ >= 2:
                        pg.op("dve", lambda e, j=j: e.tensor_copy(out=uh[:, j, :], in_=uext[:, np_:np_ + 2]), [uext], [uh])
                    pg.op("dve", lambda e: e.tensor_tensor(out=t1[:, 0:n], in0=cb32[:, 0:n], in1=yc[:, 0:n], op=ALU.mult), [cb32, yc], [t1])
                    pg.op("dve", lambda e, j=j: e.tensor_tensor(out=cpart[:, j, 0:n], in0=t1[:, 0:n], in1=sgc[:, 0:n], op=ALU.mult),
                          [t1, sgc], [cpart])
                for j in range(CC):
                    convchunk(j)

                def mergechunk(f):
                    base = 4 * DA + 4 * DC
                    bga = pcol[:, c.o_bg + f:c.o_bg + f + 1]
                    bgc = pcol[:, c.o_bg + KC + f:c.o_bg + KC + f + 1]
                    proj(w_in, KC, base + f * P, hT, t0, rng,
                         lambda B, _t, _n, bga=bga: pg.op("act", lambda e: e.activation(out=ga32[:, 0:n], in_=B[:, 0:n], func=AF.Sigmoid, bias=bga),
                                                          [B, pcol], [ga32]))
                    proj(w_in, KC, base + D + f * P, hT, t0, rng,
                         lambda B, _t, _n, bgc=bgc: pg.op("act", lambda e: e.activation(out=gc32[:, 0:n], in_=B[:, 0:n], func=AF.Sigmoid, bias=bgc),
                                                          [B, pcol], [gc32]))
                    proj(w_pa, H, f * P, a_t, t0, rng,
                         lambda B, _t, _n: pg.op("dve", lambda e: e.tensor_tensor(out=t1[:, 0:n], in0=B[:, 0:n], in1=ga32[:, 0:n], op=ALU.mult),
                                                 [B, ga32], [t1]))

                    def ev_pc(B, _t, _n, f=f):
                        pg.op("dve", lambda e: e.tensor_tensor(out=t2[:, 0:n], in0=B[:, 0:n], in1=gc32[:, 0:n], op=ALU.mult), [B, gc32], [t2])
                        pg.op("dve", lambda e: e.tensor_tensor(out=merged[:, f, 0:n], in0=t1[:, 0:n], in1=t2[:, 0:n], op=ALU.add), [t1, t2], [merged])
                    proj(w_pc, CC, f * P, cpart, 0, rng, ev_pc)
                for f in range(KC):
                    mergechunk(f)

                def outchunk(f):
                    proj(w_out, KC, f * P, merged, 0, rng,
                         lambda B, _t, _n: pg.op("act", lambda e: e.activation(out=oT[:, f, 0:n], in_=B[:, 0:n], func=AF.Copy), [B], [oT]))
                for f in range(KC):
                    outchunk(f)
                NG = (D + 511) // 512
                for tt in range(n // P):
                    finalize(t0, tt, NG)

            def finalize(t0, tt, NG):
                if True:
                    row0 = t0 + tt * P
                    for g in range(NG):
                        B = PS[2 + g]
                        nq = min(4, KC - 4 * g)
                        for q in range(nq):
                            pg.op("pe", lambda e, B=B, q=q, g=g, tt=tt: e.transpose(
                                out=B[:, q * P:(q + 1) * P], in_=oT[:, 4 * g + q, tt * P:(tt + 1) * P], identity=ident[:, 0:P]), [oT, cst], [B])
                        pg.op("act", lambda e, B=B, g=g, nq=nq: e.activation(out=t2[:, 0:nq * P], in_=B[:, 0:nq * P], func=AF.Square,
                                                                             accum_out=ssq[:, g:g + 1]), [B], [t2, ssq])
                    pg.op("dve", lambda e: e.tensor_reduce(out=rstd[:], in_=ssq[:, 0:NG], axis=mybir.AxisListType.X, op=ALU.add), [ssq], [rstd])
                    pg.op("dve", lambda e: e.tensor_scalar(out=rstd[:], in0=rstd[:], scalar1=1.0 / D, scalar2=EPS, op0=ALU.mult, op1=ALU.add),
                          [rstd], [rstd])
                    pg.op("act", lambda e: e.activation(out=rstd[:], in_=rstd[:], func=AF.Ln), [rstd], [rstd])
                    pg.op("act", lambda e: e.activation(out=rstd[:], in_=rstd[:], func=AF.Exp, scale=-0.5), [rstd], [rstd])
                    for g in range(NG):
                        B = PS[2 + g]
                        w = min(512, D - g * 512)
                        XQ, YQ = xq[xi[0] % 2], yq[xi[0] % 2]
                        xi[0] += 1
                        dma(lambda e, XQ=XQ, g=g, w=w, row0=row0: e.dma_start(out=XQ[:, 0:w], in_=x_own[row0:row0 + P, g * 512:g * 512 + w]), writes=[XQ])
                        pg.op("dve", lambda e, B=B, YQ=YQ, g=g, w=w: e.scalar_tensor_tensor(
                            out=YQ[:, 0:w], in0=B[:, 0:w], scalar=rstd[:, 0:1], in1=prow[:, g * 512:g * 512 + w], op0=ALU.mult, op1=ALU.mult),
                            [B, rstd, prow], [YQ])
                        pg.op("pool", lambda e, XQ=XQ, YQ=YQ, w=w: e.tensor_tensor(out=YQ[:, 0:w], in0=YQ[:, 0:w], in1=XQ[:, 0:w], op=ALU.add),
                              [XQ, YQ], [YQ])
                        dma(lambda e, YQ=YQ, g=g, w=w, row0=row0: e.dma_start(out=y_own[row0:row0 + P, g * 512:g * 512 + w], in_=YQ[:, 0:w]), reads=[YQ])
            for (t0_, n_) in part_ranges:
                part(t0_, n_)
            for j in range(CC):
                B = PS[2 + j % 2]
                pg.op("pe", lambda e, B=B, j=j: e.transpose(out=B[:NCS, 0:P], in_=ucs[:, j, :], identity=ident[:, 0:P]), [ucs, cst], [B])
                pg.op("dve", lambda e, B=B, j=j: e.tensor_copy(out=cstage[:NCS, j, 0:P], in_=B[:NCS, 0:P]), [B], [cstage])
            dma(lambda e: e.dma_start(out=conv_own.rearrange("r (j d) -> r j d", d=P), in_=cstage[:NCS, 0:CC, 0:P]), reads=[cstage])

            with nc.Block() as block:
                pg.emit(nc, block, sems, dsems, [])
    return nc


_CACHE = {}


def make_consts():
    cs = np.zeros((P, 4 * P), np.float32)
    cs[:, 0:P] = np.eye(P, dtype=np.float32)
    j = np.arange(P)[:, None]
    s = np.arange(P)[None, :]
    cs[:, P:2 * P] = -(j >= s).astype(np.float32)
    cs[:, 2 * P:3 * P] = -1.0
    cs[:, 3 * P:4 * P] = (s > j).astype(np.float32)
    return cs


def make_consts_full(GS):
    cs = np.zeros((P, 4 * P + GS * 4 + 1), np.float32)
    cs[:, -1] = np.arange(P, dtype=np.float32)
    cs[:, :4 * P] = make_consts()
    for bi in range(GS):
        for t in range(4):
            cs[:t, 4 * P + bi * 4 + t] = 1.0
    return cs


def run(cfg, x_prompt, x_sample, cache_k, cache_v, state_conv, page_table, meta_tokens,
        g_pre, w_in, b_sb, b_gate, w_conv, b_conv, w_pa, w_pc, w_out, g_post):
    c = cfg
    f32 = lambda a: np.ascontiguousarray(np.asarray(a, dtype=np.float32))
    x_prompt, x_sample, state_conv, meta_tokens = map(f32, (x_prompt, x_sample, state_conv, meta_tokens))
    g_pre, w_in, b_sb, b_gate, w_conv, b_conv, w_pa, w_pc, w_out, g_post = map(
        f32, (g_pre, w_in, b_sb, b_gate, w_conv, b_conv, w_pa, w_pc, w_out, g_post))
    cache_k = np.asarray(cache_k, dtype=np.float32)
    cache_v = np.asarray(cache_v, dtype=np.float32)
    page_table = np.ascontiguousarray(np.asarray(page_table, dtype=np.int32))
    key = (1, c.D, c.SEQ, c.NB, c.NCORES, c.NPG, c.NPOOL)
    if key not in _CACHE:
        _CACHE[key] = build(c)
    nc = _CACHE[key]
    consts = make_consts_full(c.GS)
    B = c.NCORES
    ckv = np.concatenate([cache_k[0].transpose(0, 2, 3, 1), cache_v[0].transpose(0, 2, 1, 3)], axis=3)
    ckv = np.ascontiguousarray(ckv.reshape(c.NPOOL, c.H // 2, 2, P, 2 * P).transpose(0, 1, 3, 2, 4)).reshape(c.NPOOL * (c.H // 2) * P, 4 * P)
    del cache_k, cache_v

    def tile_w(w):
        k, n = w.shape
        return np.ascontiguousarray(w.reshape(k // P, P, n // P, P).transpose(2, 1, 0, 3)).reshape(n, k)
    w_in_t, w_pa_t, w_pc_t, w_out_t = tile_w(w_in[0]), tile_w(w_pa[0]), tile_w(w_pc[0]), tile_w(w_out[0])

    in_maps = []
    for core in range(B):
        xo = np.zeros((c.TTP, c.D), np.float32)
        xo[0:NMETA] = meta_tokens
        xo[NMETA:c.TP] = x_prompt[core]
        xo[c.TP:c.TT] = x_sample[core * c.NSEQ:(core + 1) * c.NSEQ].reshape(c.NSO, c.D)
        pcol = np.zeros((P, c.NPC), np.float32)
        pcol[:, c.o_gpre:c.o_gpre + c.KC] = g_pre[0].reshape(c.KC, P).T
        pcol[:, c.o_bg:c.o_bg + 2 * c.KC] = b_gate[0].reshape(2 * c.KC, P).T
        pcol[:, c.o_wc:c.o_wc + 3 * c.CC] = w_conv[0].reshape(3, c.CC, P).transpose(2, 1, 0).reshape(P, 3 * c.CC)
        pcol[:, c.o_bc:c.o_bc + c.CC] = b_conv[0].reshape(c.CC, P).T
        st = state_conv[0, core * c.NSEQ:(core + 1) * c.NSEQ]
        pcol[:, c.o_st:c.o_st + c.CC * c.NSEQ * 2] = st.reshape(c.NSEQ, 2, c.CC, P).transpose(3, 2, 0, 1).reshape(P, -1)
        prow = np.zeros((P, c.D + c.H + 1), np.float32)
        prow[:, :c.D] = g_post[0][None, :]
        prow[:, c.D:c.D + c.H] = b_sb[0][None, :]
        pt = np.ascontiguousarray(np.broadcast_to(
            page_table[core * c.NSEQ:(core + 1) * c.NSEQ].reshape(1, c.NSEQ * c.NPG), (P, c.NSEQ * c.NPG)))
        in_maps.append({"x_own": xo, "w_in": w_in_t, "w_pa": w_pa_t, "w_pc": w_pc_t, "w_out": w_out_t,
                        "pcol": pcol, "prow": prow, "consts": consts, "ckv": ckv, "pt": pt})
    res = run_bass_kernel_spmd(nc, in_maps, core_ids=list(range(B)))
    R = res.results
    del in_maps, ckv
    y_prompt = np.stack([R[i]["y_own"][NMETA:c.TP] for i in range(B)])
    y_sample = np.concatenate([R[i]["y_own"][c.TP:c.TT].reshape(c.NSEQ, c.DEC_SEQ, c.D) for i in range(B)])
    kp = np.stack([R[i]["k_own"][:c.TP].reshape(c.TP, c.H, P) for i in range(B)])[None]
    vp = np.stack([R[i]["v_own"][:c.TP].reshape(c.TP, c.H, P) for i in range(B)])[None]
    cp = np.stack([R[i]["conv_own"][0:2] for i in range(B)])[None]
    ks = np.concatenate([R[i]["k_own"][c.TP:c.TT].reshape(c.NSEQ, c.DEC_SEQ, c.H, P) for i in range(B)])[None]
    vs = np.concatenate([R[i]["v_own"][c.TP:c.TT].reshape(c.NSEQ, c.DEC_SEQ, c.H, P) for i in range(B)])[None]
    cs = np.concatenate([R[i]["conv_own"][2:].reshape(c.NSEQ, 2, c.DC) for i in range(B)])[None]
    return (y_prompt, y_sample, kp, vp, cp, ks, vs, cs)


def kernel(**inputs):
    return run(Cfg(), **inputs)
```
